# Optimizing a Trainium2 kernel written in Bass

```python
import jax
import jax.numpy as jnp
from jax import lax
import numpy as np

D_MODEL = 2048
BATCH = 4
SEQ = 4096
DEPTH = 1
DEC_BATCH = 8
DEC_SEQ = 2048
PAST_LEN = 128

D_MIX = D_MODEL
D_MLSTM = D_MIX // 2
D_RET = D_MIX - D_MLSTM
H_MLSTM = 4
H_RET = 4
DH_MLSTM = D_MLSTM // H_MLSTM
DH_RET = D_RET // H_RET
CHUNK = 128
D_FF = 5632
CONV_W = 3
N_IN = 4 * D_MLSTM + 4 * H_MLSTM + 4 * D_RET
DEEPNORM_ALPHA = (2.0 * DEPTH) ** 0.25
DEEPNORM_BETA = (8.0 * DEPTH) ** -0.25
LN_EPS = 1e-5
NORM_EPS = 1e-6
ROPE_BASE = 10000.0
RET_DECAY_EXP_FWD = 5.0
RET_DECAY_EXP_BWD = 5.5
NEG_INIT = -1e30

kernel_name = 'hymba_mlstm_retention_convffn_encoder'


def _layernorm(x, g, b):
    xf = x.astype(jnp.float32)
    mu = jnp.mean(xf, -1, keepdims=True)
    var = jnp.mean(jnp.square(xf - mu), -1, keepdims=True)
    y = (xf - mu) * lax.rsqrt(var + LN_EPS) * g.astype(jnp.float32) + b.astype(jnp.float32)
    return y.astype(x.dtype)


def _split_proj(z):
    sizes = [D_MLSTM] * 4 + [2 * H_MLSTM] * 2 + [D_RET] * 4
    return jnp.split(z, [int(s) for s in np.cumsum(sizes)[:-1]], axis=-1)


def _heads(t, n_heads):
    b, s, _ = t.shape
    return t.reshape(b, s, n_heads, -1).transpose(0, 2, 1, 3)


def _unheads(t):
    b, h, s, d = t.shape
    return t.transpose(0, 2, 1, 3).reshape(b, s, h * d)


def _two_dir(t_fwd, t_bwd):
    return jnp.concatenate([t_fwd, jnp.flip(t_bwd, axis=2)], axis=1)


def _merge_dir(y, n_heads):
    return y[:, :n_heads] + jnp.flip(y[:, n_heads:], axis=2)


def _to_chunks(t):
    b, h, s = t.shape[:3]
    t = t.reshape(b, h, s // CHUNK, CHUNK, *t.shape[3:])
    return jnp.moveaxis(t, 2, 0)


def _from_chunks(t):
    t = jnp.moveaxis(t, 0, 2)
    b, h, nc, l = t.shape[:4]
    return t.reshape(b, h, nc * l, *t.shape[4:])


def _rotary(t):
    s, d = t.shape[2], t.shape[3]
    inv = 1.0 / (ROPE_BASE ** jnp.linspace(0.0, 1.0, d // 2, dtype=jnp.float32))
    ang = jnp.arange(s, dtype=jnp.float32)[:, None] * inv[None, :]
    cos, sin = jnp.cos(ang), jnp.sin(ang)
    t2 = t.reshape(*t.shape[:-1], d // 2, 2)
    a, b = t2[..., 0], t2[..., 1]
    return jnp.stack([a * cos - b * sin, a * sin + b * cos], axis=-1).reshape(t.shape)


def _mlstm_chunkwise(q, k, v, ig, lf):
    b, h, s, dk = q.shape
    dv = v.shape[-1]
    causal = jnp.tril(jnp.ones((CHUNK, CHUNK), dtype=bool))

    def step(carry, inp):
        c_st, n_st, m_st = carry
        qj, kj, vj, ij, fj = inp
        g = jnp.cumsum(fj, axis=-1)
        log_d = g[..., :, None] - g[..., None, :] + ij[..., None, :]
        log_d = jnp.where(causal, log_d, -jnp.inf)
        inter = g + m_st[..., None]
        m_out = jnp.maximum(jnp.max(log_d, axis=-1), inter)
        d_mat = jnp.exp(log_d - m_out[..., None])
        inter_w = jnp.exp(inter - m_out)
        sc = jnp.einsum('bhld,bhsd->bhls', qj, kj) * d_mat
        num = jnp.einsum('bhls,bhse->bhle', sc, vj) + inter_w[..., None] * jnp.einsum('bhld,bhde->bhle', qj, c_st)
        den = jnp.sum(sc, axis=-1) + inter_w * jnp.einsum('bhld,bhd->bhl', qj, n_st)
        h_out = num / jnp.maximum(jnp.abs(den), jnp.exp(-m_out))[..., None]
        g_last = g[..., -1]
        w_log = g_last[..., None] - g + ij
        m_new = jnp.maximum(g_last + m_st, jnp.max(w_log, axis=-1))
        decay = jnp.exp(g_last + m_st - m_new)
        ws = jnp.exp(w_log - m_new[..., None])[..., None]
        c_new = decay[..., None, None] * c_st + jnp.einsum('bhsd,bhse->bhde', kj * ws, vj)
        n_new = decay[..., None] * n_st + jnp.sum(kj * ws, axis=2)
        return (c_new, n_new, m_new), h_out

    init = (jnp.zeros((b, h, dk, dv), jnp.float32), jnp.zeros((b, h, dk), jnp.float32),
            jnp.full((b, h), NEG_INIT, jnp.float32))
    _, hs = lax.scan(step, init, tuple(_to_chunks(t) for t in (q, k, v, ig, lf)))
    return _from_chunks(hs)


def _retention_chunkwise(q, k, v, log_gamma):
    b, h, s, dk = q.shape
    dv = v.shape[-1]
    pos = jnp.arange(CHUNK, dtype=jnp.float32)
    diff = pos[:, None] - pos[None, :]
    d_mat = jnp.where(diff >= 0, jnp.exp(log_gamma[:, None, None] * jnp.maximum(diff, 0.0)), 0.0)
    xi = jnp.exp(log_gamma[:, None] * (pos + 1.0))
    zeta = jnp.exp(log_gamma[:, None] * (CHUNK - 1.0 - pos))
    chunk_decay = jnp.exp(log_gamma * CHUNK)

    def step(r_st, inp):
        qj, kj, vj = inp
        sc = jnp.einsum('bhld,bhsd->bhls', qj, kj) * d_mat
        y = jnp.einsum('bhls,bhse->bhle', sc, vj) + xi[..., None] * jnp.einsum('bhld,bhde->bhle', qj, r_st)
        r_new = chunk_decay[:, None, None] * r_st + jnp.einsum('bhsd,bhse->bhde', kj * zeta[..., None], vj)
        return r_new, y

    _, ys = lax.scan(step, jnp.zeros((b, h, dk, dv), jnp.float32), tuple(_to_chunks(t) for t in (q, k, v)))
    return _from_chunks(ys)


def _ret_log_gamma():
    hd = jnp.arange(H_RET, dtype=jnp.float32)
    fwd = jnp.log1p(-jnp.exp2(-RET_DECAY_EXP_FWD - hd))
    bwd = jnp.log1p(-jnp.exp2(-RET_DECAY_EXP_BWD - hd))
    return jnp.concatenate([fwd, bwd])


def _mixer(h, w_in, b_igate, b_fgate, mlstm_norm_w, w_out):
    z = jnp.einsum('bsd,dn->bsn', h, w_in).astype(jnp.float32)
    mq, mk, mv, mo, mi, mf, rq, rk, rv, rg = _split_proj(z)
    q = _heads(mq, H_MLSTM) * DH_MLSTM ** -0.5
    k = _heads(mk, H_MLSTM)
    v = _heads(mv, H_MLSTM)
    ig = jnp.swapaxes(mi + b_igate.astype(jnp.float32), 1, 2)
    lf = jnp.swapaxes(jax.nn.log_sigmoid(mf + b_fgate.astype(jnp.float32)), 1, 2)
    hm = _mlstm_chunkwise(_two_dir(q, q), _two_dir(k, k), _two_dir(v, v),
                          _two_dir(ig[:, :H_MLSTM], ig[:, H_MLSTM:]),
                          _two_dir(lf[:, :H_MLSTM], lf[:, H_MLSTM:]))
    hm = _merge_dir(hm, H_MLSTM)
    mu = jnp.mean(hm, -1, keepdims=True)
    var = jnp.mean(jnp.square(hm - mu), -1, keepdims=True)
    hm = (hm - mu) * lax.rsqrt(var + NORM_EPS) * mlstm_norm_w.astype(jnp.float32).reshape(H_MLSTM, 1, DH_MLSTM)
    y_m = _unheads(hm) * jax.nn.sigmoid(mo)
    rqh = _rotary(_heads(rq, H_RET))
    rkh = _rotary(_heads(rk, H_RET)) * DH_RET ** -0.5
    rvh = _heads(rv, H_RET)
    yr = _retention_chunkwise(_two_dir(rqh, rqh), _two_dir(rkh, rkh), _two_dir(rvh, rvh), _ret_log_gamma())
    yr = _merge_dir(yr, H_RET)
    yr = yr * lax.rsqrt(jnp.mean(jnp.square(yr), -1, keepdims=True) + NORM_EPS)
    y_r = _unheads(yr) * jax.nn.silu(rg)
    y = jnp.concatenate([y_m, y_r], axis=-1).astype(h.dtype)
    return jnp.einsum('bsm,md->bsd', y, w_out)


def _conv_ffn(h, w_up, conv_w, conv_b, w_down):
    u = jnp.einsum('bsd,df->bsf', h, w_up)
    up = jnp.pad(u, ((0, 0), (1, 1), (0, 0)))
    u = up[:, :-2] * conv_w[0] + up[:, 1:-1] * conv_w[1] + up[:, 2:] * conv_w[2] + conv_b
    a, g = jnp.split(u, 2, axis=-1)
    return jnp.einsum('bsf,fd->bsd', jax.nn.silu(a) * g, w_down)


def _trunk(x, c, w_ada, b_ada, w_in, b_igate, b_fgate, mlstm_norm_w, w_out,
           ln1_g, ln1_b, w_up, conv_w, conv_b, w_down, ln2_g, ln2_b):
    for l in range(DEPTH):
        ada = jnp.einsum('bd,de->be', jax.nn.silu(c), w_ada[l]) + b_ada[l]
        sh1, sc1, g1, sh2, sc2, g2 = jnp.split(ada[:, None, :], 6, axis=-1)
        h = x * (1.0 + sc1) + sh1
        x = _layernorm(DEEPNORM_ALPHA * x + g1 * _mixer(h, w_in[l], b_igate[l], b_fgate[l], mlstm_norm_w[l], w_out[l]),
                       ln1_g[l], ln1_b[l])
        h = x * (1.0 + sc2) + sh2
        x = _layernorm(DEEPNORM_ALPHA * x + g2 * _conv_ffn(h, w_up[l], conv_w[l], conv_b[l], w_down[l]),
                       ln2_g[l], ln2_b[l])
    return x


def setup_inputs(seed: int = 0) -> dict:
    key = jax.random.key(seed)
    ks = jax.random.split(key, 20)

    def nrm(k, shape, s):
        return jax.random.normal(k, shape, jnp.float32) * s

    L = DEPTH
    f_base = jnp.tile(jnp.linspace(3.0, 6.0, H_MLSTM, dtype=jnp.float32), 2)
    return {
        'x_prompt': nrm(ks[0], (BATCH, SEQ, D_MODEL), 1.0),
        'x_sample': nrm(ks[1], (DEC_BATCH, DEC_SEQ, D_MODEL), 1.0),
        'c_prompt': nrm(ks[2], (BATCH, D_MODEL), 1.0),
        'c_sample': nrm(ks[3], (DEC_BATCH, D_MODEL), 1.0),
        'w_ada': nrm(ks[4], (L, D_MODEL, 6 * D_MODEL), 0.5 * D_MODEL ** -0.5),
        'b_ada': nrm(ks[5], (L, 6 * D_MODEL), 0.02),
        'w_in': nrm(ks[6], (L, D_MODEL, N_IN), D_MODEL ** -0.5),
        'b_igate': nrm(ks[7], (L, 2 * H_MLSTM), 0.1),
        'b_fgate': f_base + nrm(ks[8], (L, 2 * H_MLSTM), 0.1),
        'mlstm_norm_w': 1.0 + nrm(ks[9], (L, D_MLSTM), 0.02),
        'w_out': nrm(ks[10], (L, D_MIX, D_MODEL), DEEPNORM_BETA * D_MIX ** -0.5),
        'ln1_g': 1.0 + nrm(ks[11], (L, D_MODEL), 0.02),
        'ln1_b': nrm(ks[12], (L, D_MODEL), 0.02),
        'w_up': nrm(ks[13], (L, D_MODEL, 2 * D_FF), D_MODEL ** -0.5),
        'conv_w': nrm(ks[14], (L, CONV_W, 2 * D_FF), CONV_W ** -0.5),
        'conv_b': nrm(ks[15], (L, 2 * D_FF), 0.02),
        'w_down': nrm(ks[16], (L, D_FF, D_MODEL), DEEPNORM_BETA * D_FF ** -0.5),
        'ln2_g': 1.0 + nrm(ks[17], (L, D_MODEL), 0.02),
        'ln2_b': nrm(ks[18], (L, D_MODEL), 0.02),
    }


def reference(x_prompt, x_sample, c_prompt, c_sample, w_ada, b_ada, w_in, b_igate, b_fgate,
              mlstm_norm_w, w_out, ln1_g, ln1_b, w_up, conv_w, conv_b, w_down, ln2_g, ln2_b):
    y_prompt = _trunk(x_prompt, c_prompt, w_ada, b_ada, w_in, b_igate, b_fgate, mlstm_norm_w, w_out,
                      ln1_g, ln1_b, w_up, conv_w, conv_b, w_down, ln2_g, ln2_b)
    y_sample = _trunk(x_sample, c_sample, w_ada, b_ada, w_in, b_igate, b_fgate, mlstm_norm_w, w_out,
                      ln1_g, ln1_b, w_up, conv_w, conv_b, w_down, ln2_g, ln2_b)
    return (y_prompt, y_sample)
```

```python
import numpy as np
from contextlib import ExitStack
import concourse.bass as bass
import concourse.mybir as mybir
from concourse.bass_utils import run_bass_kernel_spmd

F32 = mybir.dt.float32
BF16 = mybir.dt.bfloat16
ALU = mybir.AluOpType
AF = mybir.ActivationFunctionType

D = 2048
KC = 16
DFF = 5632
NFC = 44
N_IN = 8208
CH = 128
ALPHA = 2.0 ** 0.25
LN_EPS = 1e-5
NORM_EPS = 1e-6

ENGS = ("pe", "act", "dve", "pool", "sp")
SAME_ENGINE_SYNC = True
import os
NOCONV = os.environ.get("MK_NOCONV", "0") == "1"
NOCONV_IN = NOCONV or os.environ.get("MK_NOCONV_IN", "0") == "1"
NOCONV_UP = NOCONV or os.environ.get("MK_NOCONV_UP", "0") == "1"
PROD_ACT = os.environ.get("MK_PROD_ACT", "1") == "1"
PROD_POOL = os.environ.get("MK_PROD_POOL", "1") == "1"


class T:
    __slots__ = ("name", "lastw", "reads", "dsem")

    def __init__(self, name):
        self.name = name
        self.lastw = None
        self.reads = {}
        self.dsem = None


class Emitter:
    def __init__(self, nc, es):
        self.nc = nc
        self.es = es
        self.sems = {}
        for e in ("pe", "act", "dve", "pool"):
            self.sems[e] = es.enter_context(nc.semaphore("sem_" + e))
        self.cnt = {e: 0 for e in ENGS}
        self.seen = {e: {} for e in ENGS}
        self.q = {e: [] for e in ENGS}
        self.dma_sems = {}
        self.free_dsems = []
        self.bound = []
        self.bg_tiles = []
        self.ntile = 0
        self.n_wait = 0
        self.n_ops = 0

    def tile(self, name=None):
        self.ntile += 1
        return T((name or "t") + "_%d" % self.ntile)

    def tiles(self, name, n):
        return [self.tile(name) for _ in range(n)]

    def dsem_for(self, t):
        if t.dsem is None:
            if self.free_dsems:
                key = self.free_dsems.pop()
            else:
                key = "d%d" % len(self.dma_sems)
                self.sems[key] = self.es.enter_context(self.nc.semaphore(key))
                self.dma_sems[key] = 0
            t.dsem = key
            self.bound.append(t)
        return t.dsem

    def release_dsems(self, keep=()):
        nb = []
        for t in self.bound:
            if t in keep or t in self.bg_tiles:
                nb.append(t)
            else:
                self.free_dsems.append(t.dsem)
                t.dsem = None
        self.bound = nb

    def _need(self, eng, ev, waits):
        if ev is None:
            return
        k, v = ev
        if k == eng and (eng == "pe" or not SAME_ENGINE_SYNC):
            return
        if self.seen[eng].get(k, 0) >= v:
            return
        self.seen[eng][k] = v
        waits.append((k, v))

    def _deps(self, eng, reads, writes):
        waits = []
        for t in reads:
            self._need(eng, t.lastw, waits)
        for t in writes:
            self._need(eng, t.lastw, waits)
            for k, v in t.reads.items():
                if k == eng:
                    continue
                self._need(eng, (k, v), waits)
        return waits

    def _mark(self, ev, reads, writes):
        k, v = ev
        for t in reads:
            if t.reads.get(k, 0) < v:
                t.reads[k] = v
        for t in writes:
            t.lastw = ev
            t.reads = {}

    def op(self, eng, fn, R=(), W=(), inc=True):
        waits = self._deps(eng, R, W)
        if inc:
            self.cnt[eng] += 1
            ev = (eng, self.cnt[eng])
        else:
            ev = (eng, self.cnt[eng] + 1)
        self._mark(ev, R, W)
        self.q[eng].append((waits, fn, (eng, 1) if inc else None))
        self.n_wait += len(waits)
        self.n_ops += 1

    def dma(self, queue, out, in_, sem_tile, R=(), W=(), cast=False):
        if queue == "act" and not PROD_ACT:
            queue = "sp"
        if queue == "pool" and not PROD_POOL and out.dtype == in_.dtype:
            queue = "sp"
        waits = self._deps(queue, R, W)
        key = self.dsem_for(sem_tile)
        self.dma_sems[key] += 16
        ev = (key, self.dma_sems[key])
        self._mark(ev, R, W)
        self.q[queue].append((waits, lambda e: e.dma_start(out=out, in_=in_), (key, 16)))
        self.n_wait += len(waits)
        self.n_ops += 1

    def fence(self):
        bgk = set(t.dsem for t in self.bg_tiles if t.dsem is not None)
        evs = [(e, self.cnt[e]) for e in ("pe", "act", "dve", "pool") if self.cnt[e] > 0]
        evs += [(k, v) for k, v in self.dma_sems.items() if v > 0 and k not in bgk]
        for eng in ENGS:
            waits = []
            for k, v in evs:
                if k == eng or self.seen[eng].get(k, 0) >= v:
                    continue
                self.seen[eng][k] = v
                waits.append((k, v))
            if waits:
                self.q[eng].append((waits, None, None))
                self.n_wait += len(waits)

    def flush(self):
        nc, sems, q = self.nc, self.sems, self.q
        if os.environ.get("MK_VERBOSE"):
            print("flush: sbuf_remaining", nc.sbuf_bytes_remaining, "ops", self.n_ops, "sems", len(self.sems), flush=True)

        def run(name):
            def body(eng):
                for waits, fn, inc in q[name]:
                    for k, v in waits:
                        eng.wait_ge(sems[k], v)
                    if fn is not None:
                        ins = fn(eng)
                        if inc is not None:
                            ins.then_inc(sems[inc[0]], inc[1])
            return body

        with nc.Block(no_gpsimd_drain=True) as block:
            block.tensor(run("pe"))
            block.scalar(run("act"))
            block.vector(run("dve"))
            block.gpsimd(run("pool"))
            block.sync(run("sp"))
        self.q = {e: [] for e in ENGS}

    def mm(self, out, lhsT, rhs, start, stop, R, W, inc=None):
        self.op("pe", lambda e: e.matmul(out, lhsT=lhsT, rhs=rhs, start=start, stop=stop), R, W,
                inc=stop if inc is None else inc)

    def tr(self, out, in_, ident, R, W, inc=True):
        self.op("pe", lambda e: e.transpose(out, in_, ident), R, W, inc=inc)

    def act(self, out, in_, func, R, W, bias=None, scale=None, accum=None):
        kw = {}
        if bias is not None:
            kw["bias"] = bias
        if scale is not None:
            kw["scale"] = scale
        if accum is not None:
            kw["accum_out"] = accum
        self.op("act", lambda e: e.activation(out=out, in_=in_, func=func, **kw), R, W)

    def tt(self, eng, out, in0, in1, op, R, W):
        self.op(eng, lambda e: e.tensor_tensor(out=out, in0=in0, in1=in1, op=op), R, W)

    def ts(self, eng, out, in0, s1, s2, op0, op1, R, W):
        if s2 is None and eng == "pool":
            s2, op1 = 1.0, ALU.mult
        if s2 is None:
            self.op(eng, lambda e: e.tensor_scalar(out=out, in0=in0, scalar1=s1, scalar2=None, op0=op0), R, W)
        else:
            self.op(eng, lambda e: e.tensor_scalar(out=out, in0=in0, scalar1=s1, scalar2=s2, op0=op0, op1=op1), R, W)

    def stt(self, out, in0, scalar, in1, op0, op1, R, W):
        self.op("dve", lambda e: e.scalar_tensor_tensor(out=out, in0=in0, scalar=scalar, in1=in1, op0=op0, op1=op1),
                R, W)

    def copy(self, eng, out, in_, R, W):
        if eng == "act":
            self.op("act", lambda e: e.copy(out=out, in_=in_), R, W)
        else:
            self.op(eng, lambda e: e.tensor_copy(out=out, in_=in_), R, W)

    def memset(self, eng, ap, val, W):
        self.op(eng, lambda e: e.memset(ap, val), (), W)


class RR:
    def __init__(self, items):
        self.items = items
        self.i = 0

    def next(self):
        it = self.items[self.i % len(self.items)]
        self.i += 1
        return it


def build_program(SEG, debug=False):
    NT = 2 * SEG
    NTILE = NT // 512
    NCH = NT // CH
    SEGCH = SEG // CH
    nc = bass.Bass("TRN2", target_bir_lowering=False)

    def din(name, shape, dt=F32):
        return nc.dram_tensor(name, list(shape), dt, kind="ExternalInput").ap()

    def dscr(name, shape, dt):
        return nc.dram_tensor(name, list(shape), dt, kind="ExternalOutput" if debug else "Internal").ap()

    x = din("x", [NT, D])
    c_lay = din("c_lay", [128, KC, 2])
    carry_in = din("carry", [128, 1])
    rot = din("rot", [NT, 2, 128])
    w_ada = din("w_ada", [D, 6 * D])
    b_ada_fm = din("b_ada_fm", [128, 96])
    b_ada = din("b_ada", [6 * D])
    w_in = din("w_in", [D, N_IN])
    b_gate = din("b_gate", [16])
    norm_w = din("norm_w", [1024])
    w_out = din("w_out", [D, D])
    ln1g = din("ln1_g", [D])
    ln1b = din("ln1_b", [D])
    w_up = din("w_up", [D, 2 * DFF])
    conv_fm = din("conv_fm", [128, 2 * NFC, 4])
    w_down = din("w_down", [DFF, D])
    ln2g = din("ln2_g", [D])
    ln2b = din("ln2_b", [D])
    tri_in = din("tri", [3, 128, 128])
    rmask_in = din("rmask", [8, 128, 128])
    rxi_in = din("rxi", [8 * 128])
    rzd_in = din("rzd", [128, 16])
    y = nc.dram_tensor("y", [NT, D], F32, kind="ExternalOutput").ap()

    zq = dscr("zq", [NT, D], BF16)
    zk = dscr("zk", [NT, D], BF16)
    zv = dscr("zv", [NT, D], BF16)
    zg = dscr("zg", [NT, D], F32)
    zgate = dscr("zgate", [NT, 16], F32)
    gbc = dscr("gbc", [2, 2, 128, D], F32)
    hb = dscr("hb", [NT, D], F32)
    yT = dscr("yT", [D, NT], BF16)
    x1 = dscr("x1", [NT, D], F32)
    h2T = dscr("h2T", [D, NT], BF16)
    wbin = nc.dram_tensor("wbin", [16, 128, KC, 512], BF16, kind="Internal").ap()
    wbup = nc.dram_tensor("wbup", [NFC // 2, 2, 128, KC, 256], BF16, kind="Internal").ap()
    wbdn = nc.dram_tensor("wbdn", [4, 4, 128, 11, 512], BF16, kind="Internal").ap()

    with ExitStack() as es:
        em = Emitter(nc, es)

        sbn = [0]

        def sb(name, shape, dt, st=None):
            sbn[0] += 1
            return (st or es).enter_context(nc.sbuf_tensor("s%d_%s" % (sbn[0], name), list(shape), dt))

        PB = [es.enter_context(nc.psum_tensor("pb%d" % i, [128, 512], F32)) for i in range(8)]
        PBT = [em.tile("pb%d" % i) for i in range(8)]

        cv = sb("cv", [128, 4, KC, 2], F32)
        Tcv = em.tile("cv")
        identf = sb("identf", [128, 128], F32)
        identb = sb("identb", [128, 128], BF16)
        Tid = em.tile("ident")
        carry = sb("carry_sb", [128, 1], F32)
        Tcarry = em.tile("carry")
        em.memset("pool", identf[:], 0.0, [Tid])
        em.op("pool", lambda e: e.affine_select(out=identf[:], in_=identf[:], compare_op=ALU.not_equal, fill=1.0,
                                                base=0, pattern=[[-1, 128]], channel_multiplier=1), [Tid], [Tid])
        em.copy("pool", identb[:], identf[:], [Tid], [Tid])
        em.dma("sp", carry[:], carry_in, Tcarry, W=[Tcarry])

        Twin = em.tiles("wbin", 16)
        Twup = em.tile("wbup")
        Twdn = em.tile("wbdn")
        wiv = w_in.rearrange("(k p) n -> p k n", p=128)
        wuv = w_up.rearrange("(k p) n -> p k n", p=128)
        wdv = w_down.rearrange("(c p) n -> p c n", p=128)
        blocks = []
        for cb in range(16):
            c0 = cb * 512 if cb < 8 else 4112 + (cb - 8) * 512
            kind = ["qs", "cast", "cast", "sig", "rotq", "rotk", "cast", "silu"][cb // 2]
            dst = [zq, zk, zv, zg, zq, zk, zv, zg][cb // 2]
            d0 = (cb % 2) * 512 + (1024 if cb >= 8 else 0)
            blocks.append((c0, kind, dst, d0))

        def convert_in():
            for cb in range(16):
                em.dma("pool", wbin[cb], wiv[:, :, blocks[cb][0]:blocks[cb][0] + 512], Twin[cb], W=[Twin[cb]])

        def convert_up():
            for cp in range(NFC // 2):
                em.dma("pool", wbup[cp, 0], wuv[:, :, cp * 256:(cp + 1) * 256], Twup, W=[Twup])
                em.dma("pool", wbup[cp, 1], wuv[:, :, DFF + cp * 256:DFF + (cp + 1) * 256], Twup, W=[Twup])

        def convert_down():
            for n4 in range(4):
                for ks in range(4):
                    em.dma("pool", wbdn[n4, ks], wdv[:, ks * 11:(ks + 1) * 11, n4 * 512:(n4 + 1) * 512], Twdn,
                           W=[Twdn])

        wG = sb("wG", [128, KC, 16], BF16)
        TwG = em.tile("wG")
        wGf = sb("wGf", [128, KC, 16], F32)
        TwGf = em.tile("wGf")
        em.dma("sp", wGf[:], wiv[:, :, 4096:4112], TwGf, W=[TwGf])
        em.copy("dve", wG[:], wGf[:], [TwGf], [TwG])
        CONV_AT = os.environ.get("MK_CONV_AT", "a_end")
        if not NOCONV_IN and CONV_AT == "a_start":
            convert_in()
        if not NOCONV_UP and CONV_AT == "a_start":
            convert_up()
            convert_down()
        CONV_UP_LATE = (not NOCONV_UP) and CONV_AT == "a_end"
        if not NOCONV_IN and CONV_AT == "a_end":
            convert_in()

        with ExitStack() as st:
            cl = sb("cl", [128, KC, 2], F32, st)
            sT = sb("sT", [128, KC, 2], BF16, st)
            sB = [sb("sB%d" % s, [128, KC, 128], BF16, st) for s in range(2)]
            bfm = sb("bfm", [128, 96], F32, st)
            brow = sb("brow", [128, 2, D], F32, st)
            wA = [sb("wA%d" % i, [128, KC, 512], BF16, st) for i in range(3)]
            gst = [sb("gst%d" % i, [128, 512], F32, st) for i in range(2)]
            Tcl, TsT, TsB, Tbfm, Tbrow = em.tile("cl"), em.tile("sT"), em.tile("sB"), em.tile("bfm"), em.tile("brow")
            wArr = RR(list(zip(wA, em.tiles("wA", 3))))
            gstrr = RR(list(zip(gst, em.tiles("gst", 2))))
            em.dma("sp", cl[:], c_lay, Tcl, W=[Tcl])
            em.dma("sp", bfm[:], b_ada_fm, Tbfm, W=[Tbfm])
            em.dma("sp", brow[:, 0, :], b_ada[2 * D:3 * D].partition_broadcast(128), Tbrow, W=[Tbrow])
            em.dma("sp", brow[:, 1, :], b_ada[5 * D:6 * D].partition_broadcast(128), Tbrow, W=[Tbrow])
            em.act(sT[:], cl[:], AF.Silu, [Tcl], [TsT])
            for s in range(2):
                em.copy("dve", sB[s][:], sT[:, :, s:s + 1].broadcast_to([128, KC, 128]), [TsT], [TsB])
            pcv = PB[0][:, 0:128].rearrange("p (a b) -> p a b", b=2)
            Tpcv = PBT[0]
            pgrr = RR([(PB[1], PBT[1]), (PB[2], PBT[2])])
            wav = w_ada.rearrange("(k p) n -> p k n", p=128)
            order = [0, 1, 3, 4, 2, 5]
            fm_index = {0: 0, 1: 1, 3: 2, 4: 3}
            for v6 in order:
                for sub in range(4):
                    jb = v6 * 4 + sub
                    wt, Tw = wArr.next()
                    em.dma("pool", wt[:], wav[:, :, jb * 512:(jb + 1) * 512], Tw, W=[Tw])
                    if v6 in fm_index:
                        vi = fm_index[v6]
                        for qq in range(4):
                            fc = sub * 4 + qq
                            for kc in range(KC):
                                em.mm(pcv[:, vi * 16 + fc, :], wt[:, kc, qq * 128:(qq + 1) * 128], sT[:, kc, :],
                                      kc == 0, kc == KC - 1, [Tw, TsT], [Tpcv])
                    else:
                        vi2 = 0 if v6 == 2 else 1
                        for s in range(2):
                            pg, Tpg = pgrr.next()
                            for kc in range(KC):
                                em.mm(pg[:], sB[s][:, kc, :], wt[:, kc, :], kc == 0, kc == KC - 1, [Tw, TsB], [Tpg])
                            g, Tg = gstrr.next()
                            em.tt("dve", g[:], pg[:], brow[:, vi2, sub * 512:(sub + 1) * 512], ALU.add,
                                  [Tpg, Tbrow], [Tg])
                            em.dma("sp", gbc[vi2, s, :, sub * 512:(sub + 1) * 512], g[:], Tg, R=[Tg])
                if v6 == 1 or v6 == 4:
                    for v6b in ((0, 1) if v6 == 1 else (3, 4)):
                        vi = fm_index[v6b]
                        em.tt("dve", cv[:, vi, :, :], pcv[:, vi * 16:(vi + 1) * 16, :],
                              bfm[:, v6b * 16:(v6b + 1) * 16].unsqueeze(2).broadcast_to([128, KC, 2]), ALU.add,
                              [Tpcv, Tbfm], [Tcv])
                        if v6b in (1, 4):
                            em.ts("dve", cv[:, vi, :, :], cv[:, vi, :, :], 1.0, None, ALU.add, None, [Tcv], [Tcv])
            em.fence()
            em.flush()
            em.release_dsems()

        def modulate_T(src, Tsrc, dst, Tdst, col0, vi_sh, vi_sc, seg, pbanks):
            for g4 in range(4):
                pt, Tpt = pbanks.next()
                ptv = pt[:].rearrange("p (a b) -> p a b", b=128)
                for j in range(4):
                    kc = g4 * 4 + j
                    em.tr(ptv[:, j, :], src[:, kc * 128:(kc + 1) * 128], identf[:], [Tsrc, Tid], [Tpt], inc=(j == 3))
                for j in range(4):
                    kc = g4 * 4 + j
                    em.act(dst[:, kc, col0:col0 + 128], ptv[:, j, :], AF.Identity, [Tpt, Tcv], [Tdst],
                           bias=cv[:, vi_sh, kc, seg:seg + 1], scale=cv[:, vi_sc, kc, seg:seg + 1])

        with ExitStack() as st:
            xt = [sb("xt%d" % i, [128, 4, D], F32, st) for i in range(2)]
            rt = [sb("rt%d" % i, [128, 4, 2, 128], F32, st) for i in range(2)]
            hT = sb("hT", [128, KC, 512], BF16, st)
            wB = [sb("wB%d" % i, [128, KC, 512], BF16, st) for i in range(3)]
            szb = [sb("szb%d" % i, [128, 512], BF16, st) for i in range(4)]
            szf = [sb("szf%d" % i, [128, 512], F32, st) for i in range(3)]
            rtmp = [sb("rtmp%d" % i, [128, 256], F32, st) for i in range(4)]
            xrr = RR(list(zip(xt, em.tiles("xt", 2), rt, em.tiles("rt", 2))))
            ThT = em.tile("hT")
            wrr = RR(list(zip(wB, em.tiles("wB", 3))))
            szbrr = RR(list(zip(szb, em.tiles("szb", 4))))
            szfrr = RR(list(zip(szf, em.tiles("szf", 3))))
            Trtmp = em.tiles("rtmp", 4)
            ptrr = RR([(PB[0], PBT[0]), (PB[1], PBT[1])])
            pzrr = RR([(PB[i], PBT[i]) for i in range(2, 8)])

            def load_x(i):
                xb, Tx, rb, Tr = xrr.next()
                em.dma("sp", xb[:], x[i * 512:(i + 1) * 512, :].rearrange("(m p) d -> p m d", p=128), Tx, W=[Tx])
                em.dma("sp", rb[:], rot[i * 512:(i + 1) * 512].rearrange("(m p) a b -> p m a b", p=128), Tr, W=[Tr])
                return xb, Tx, rb, Tr

            if CONV_UP_LATE:
                em.bg_tiles = [Twup, Twdn]
                convert_up()
                convert_down()
            nxt = load_x(0)
            for i in range(NTILE):
                xb, Tx, rb, Tr = nxt
                seg = (i * 512) // SEG
                if i + 1 < NTILE:
                    nxt = load_x(i + 1)
                for m in range(4):
                    modulate_T(xb[:, m, :], Tx, hT, ThT, m * 128, 0, 1, seg, ptrr)
                for m in range(4):
                    pz, Tpz = pzrr.next()
                    for kc in range(KC):
                        em.mm(pz[:, 0:16], hT[:, kc, m * 128:(m + 1) * 128], wG[:, kc, :], kc == 0, kc == KC - 1,
                              [ThT, TwG], [Tpz])
                    sf, Tsf = szfrr.next()
                    em.copy("act", sf[:, 0:16], pz[:, 0:16], [Tpz], [Tsf])
                    r0 = i * 512 + m * 128
                    em.dma("act", zgate[r0:r0 + 128, :], sf[:, 0:16], Tsf, R=[Tsf])
                for cb in range(16):
                    c0, kind, dst, d0 = blocks[cb]
                    wt, Tw = wrr.next()
                    if NOCONV_IN:
                        em.dma("pool", wt[:], wiv[:, :, c0:c0 + 512], Tw, W=[Tw])
                    else:
                        em.dma("sp", wt[:], wbin[cb], Tw, R=[Twin[cb]], W=[Tw])
                    for m in range(4):
                        pz, Tpz = pzrr.next()
                        for kc in range(KC):
                            em.mm(pz[:], hT[:, kc, m * 128:(m + 1) * 128], wt[:, kc, :], kc == 0, kc == KC - 1,
                                  [ThT, Tw], [Tpz])
                        r0 = i * 512 + m * 128
                        sq = "act"
                        if kind in ("sig", "silu"):
                            so, Tso = szfrr.next()
                            em.act(so[:], pz[:], AF.Sigmoid if kind == "sig" else AF.Silu, [Tpz], [Tso])
                        elif kind == "qs":
                            so, Tso = szbrr.next()
                            em.act(so[:], pz[:], AF.Copy, [Tpz], [Tso], scale=1.0 / 16.0)
                        elif kind == "cast":
                            so, Tso = szbrr.next()
                            em.copy("act", so[:], pz[:], [Tpz], [Tso])
                        else:
                            so, Tso = szbrr.next()
                            sc = 1.0 if kind == "rotq" else 1.0 / 16.0
                            pv = pz[:].rearrange("p (h two i) -> p h two i", two=2, i=128)
                            sov = so[:].rearrange("p (h two i) -> p h two i", two=2, i=128)
                            a_, b_ = pv[:, :, 0, :], pv[:, :, 1, :]
                            cosb = rb[:, m, 0:1, :].broadcast_to([128, 2, 128])
                            sinb = rb[:, m, 1:2, :].broadcast_to([128, 2, 128])
                            t = [rtmp[k][:].rearrange("p (h i) -> p h i", i=128) for k in range(4)]
                            em.stt(t[0], a_, sc, cosb, ALU.mult, ALU.mult, [Tpz, Tr], [Trtmp[0]])
                            em.stt(t[1], b_, sc, sinb, ALU.mult, ALU.mult, [Tpz, Tr], [Trtmp[1]])
                            em.stt(t[2], a_, sc, sinb, ALU.mult, ALU.mult, [Tpz, Tr], [Trtmp[2]])
                            em.stt(t[3], b_, sc, cosb, ALU.mult, ALU.mult, [Tpz, Tr], [Trtmp[3]])
                            em.tt("dve", sov[:, :, 0, :], t[0], t[1], ALU.subtract, [Trtmp[0], Trtmp[1]], [Tso])
                            em.tt("dve", sov[:, :, 1, :], t[2], t[3], ALU.add, [Trtmp[2], Trtmp[3]], [Tso])
                        em.dma(sq, dst[r0:r0 + 128, d0:d0 + 512], so[:], Tso, R=[Tso])
            em.fence()
            em.flush()
            em.release_dsems()

        with ExitStack() as st:
            tri = sb("tri", [128, 3, 128], F32, st)
            rmask = sb("rmask", [128, 8, 128], F32, st)
            rxi = sb("rxi", [128, 8, 128], F32, st)
            rzd = sb("rzd", [128, 16], F32, st)
            bg = sb("bg", [128, 16], F32, st)
            nw = sb("nw", [128, 1024], F32, st)
            gall = sb("gall", [128, NCH, 16], F32, st)
            gt_ = sb("gt_", [128, NCH, 4], F32, st)
            gsp = sb("gsp", [128, NCH, 4], F32, st)
            gW = sb("gW", [128, NCH, 4], F32, st)
            gTHR = sb("gTHR", [128, NCH, 4], F32, st)
            gDEC = sb("gDEC", [128, NCH, 4], F32, st)
            Cm = [sb("Cm%d" % h, [128, 2, 257], F32, st) for h in range(8)]
            Cb = [sb("Cb%d" % h, [128, 2, 257], BF16, st) for h in range(8)]
            TCm = em.tiles("Cm", 8)
            TCb = em.tiles("Cb", 8)
            NB = 2
            qin = [sb("qin%d" % i, [128, D], BF16, st) for i in range(NB)]
            kin = [sb("kin%d" % i, [128, D], BF16, st) for i in range(NB)]
            vex = [sb("vex%d" % i, [128, 8, 257], BF16, st) for i in range(NB)]
            inrr = RR(list(zip(qin, kin, vex, em.tiles("qin", NB), em.tiles("kin", NB), em.tiles("vex", NB))))
            qT_ = [sb("qT%d" % i, [128, 16, 128], BF16, st) for i in range(2)]
            kT_ = [sb("kT%d" % i, [128, 16, 128], BF16, st) for i in range(2)]
            qkTrr = RR(list(zip(qT_, kT_, em.tiles("qT", 2), em.tiles("kT", 2))))
            scT = [sb("scT%d" % i, [128, 128], BF16, st) for i in range(2)]
            scrr = RR(list(zip(scT, em.tiles("scT", 2))))
            kw_ = [sb("kw%d" % i, [128, 256], BF16, st) for i in range(2)]
            kwrr = RR(list(zip(kw_, em.tiles("kw", 2))))
            qx_ = [sb("qx%d" % i, [128, 2, 128], BF16, st) for i in range(2)]
            qxrr = RR(list(zip(qx_, em.tiles("qx", 2))))
            hout = [sb("hout%d" % i, [128, D], F32, st) for i in range(2)]
            horr = RR(list(zip(hout, em.tiles("hout", 2))))
            hbl = [sb("hbl%d" % i, [128, D], F32, st) for i in range(2)]
            ogl = [sb("ogl%d" % i, [128, D], F32, st) for i in range(2)]
            hbrr = RR(list(zip(hbl, ogl, em.tiles("hbl", 2), em.tiles("ogl", 2))))
            dsm = [sb("dsm%d" % i, [128, 4], F32, st) for i in range(2)]
            dsmrr = RR(list(zip(dsm, em.tiles("dsm", 2))))
            stats = sb("stats", [128, 4, 6], F32, st)
            mv = sb("mv", [128, 4, 2], F32, st)
            rs = sb("rs", [128, 8], F32, st)
            ss = sb("ss", [128, 4], F32, st)
            junk = sb("junk", [128, 256], F32, st)
            tmpn = sb("tmpn", [128, 1024], F32, st)
            ybf = [sb("ybf%d" % i, [128, D], BF16, st) for i in range(2)]
            ybrr = RR(list(zip(ybf, em.tiles("ybf", 2))))
            yTs = [sb("yTs%d" % i, [128, 16, 128], BF16, st) for i in range(2)]
            yTrr = RR(list(zip(yTs, em.tiles("yTs", 2))))
            Tc = em.tile("consts")
            Tgall, Tgt, Tgsp, TgW = em.tile("gall"), em.tile("gt"), em.tile("gsp"), em.tile("gW")
            Tstats, Tmv, Trs, Tss, Ttmpn = (em.tiles("stats", 4), em.tiles("mv", 4), em.tiles("rs", 8),
                                            em.tiles("ss", 4), em.tiles("tmpn", 4))
            Tjunk = em.tile("junk")

            em.dma("sp", tri[:], tri_in.rearrange("a s l -> s a l"), Tc, W=[Tc])
            em.dma("sp", rmask[:], rmask_in.rearrange("a s l -> s a l"), Tc, W=[Tc])
            em.dma("sp", rxi[:].rearrange("p a l -> p (a l)"), rxi_in.partition_broadcast(128), Tc, W=[Tc])
            em.dma("sp", rzd[:], rzd_in, Tc, W=[Tc])
            em.dma("sp", bg[:], b_gate.partition_broadcast(128), Tc, W=[Tc])
            em.dma("sp", nw[:], norm_w.partition_broadcast(128), Tc, W=[Tc])
            em.dma("sp", gall[:], zgate.rearrange("(j p) g -> p j g", p=128), Tgall, W=[Tgall])
            for b in range(NB):
                em.memset("pool", vex[b][:, :, 256:257], 1.0, [inrr.items[b][5]])

            ptq = RR([(PB[0], PBT[0]), (PB[1], PBT[1])])
            TpS = [PBT[2], PBT[7]]
            pS = [PB[2][:, 0:128], PB[7][:, 0:128]]
            pn2 = [PB[2][:, 128:130], PB[7][:, 128:130]]
            pNrr = RR([(PB[3], PBT[3]), (PB[4], PBT[4])])
            pUrr = RR([(PB[5], PBT[5]), (PB[6], PBT[6])])
            pGate = PB[7]
            TpG = PBT[7]

            def load_chunk(j):
                qb, kb, vb, Tq, Tk, Tv = inrr.next()
                r0 = j * CH
                em.dma("sp", qb[:], zq[r0:r0 + CH, :], Tq, W=[Tq])
                em.dma("sp", kb[:], zk[r0:r0 + CH, :], Tk, W=[Tk])
                em.dma("sp", vb[:, :, 0:256], zv[r0:r0 + CH, :].rearrange("p (h e) -> p h e", e=256), Tv, W=[Tv])
                return qb, kb, vb, Tq, Tk, Tv

            def load_hb(j):
                hb_, og_, Thb_, Tog_ = hbrr.next()
                r0 = j * CH
                em.dma("sp", hb_[:], hb[r0:r0 + CH, :], Thb_, W=[Thb_])
                em.dma("sp", og_[:], zg[r0:r0 + CH, :], Tog_, W=[Tog_])
                return hb_, og_, Thb_, Tog_

            for sweep in range(2):
                dr = 1 - sweep
                gmf = gall[:, :, 8 + 4 * dr:12 + 4 * dr]
                gmi = gall[:, :, 4 * dr:4 * dr + 4]
                bfb = bg[:, 8 + 4 * dr:12 + 4 * dr].unsqueeze(1).broadcast_to([128, NCH, 4])
                bib = bg[:, 4 * dr:4 * dr + 4].unsqueeze(1).broadcast_to([128, NCH, 4])
                em.tt("dve", gt_[:], gmf, bfb, ALU.add, [Tgall, Tc], [Tgt])
                em.act(gsp[:], gt_[:], AF.Exp, [Tgt], [Tgsp], scale=-1.0)
                em.act(gsp[:], gsp[:], AF.Ln, [Tgsp], [Tgsp], bias=1.0)
                spf = gsp[:].rearrange("p j h -> p (j h)")
                NG = NCH * 4
                em.mm(pGate[:, 0:NG], tri[:, dr, :], spf, True, True, [Tc, Tgsp], [TpG])
                em.mm(pGate[:, 256:256 + NG], tri[:, 2, :], spf, True, True, [Tc, Tgsp], [TpG])
                pgc = pGate[:, 0:NG].rearrange("p (j h) -> p j h", h=4)
                pgt = pGate[:, 256:256 + NG].rearrange("p (j h) -> p j h", h=4)
                em.tt("dve", gt_[:], gmi, bib, ALU.add, [Tgall, Tc], [Tgt])
                em.tt("dve", gt_[:], gt_[:], pgc, ALU.add, [Tgt, TpG], [Tgt])
                em.act(gW[:], gt_[:], AF.Exp, [Tgt], [TgW])
                em.act(gTHR[:], pgc, AF.Exp, [TpG], [TgW])
                em.act(gDEC[:], pgt, AF.Exp, [TpG], [TgW], scale=-1.0)
                for h in range(8):
                    em.memset("pool", Cm[h][:], 0.0, [TCm[h]])
                    em.memset("pool", Cb[h][:], 0.0, [TCb[h]])
                order = list(range(NCH)) if dr == 0 else list(range(NCH - 1, -1, -1))

                def tr_begin():
                    return qkTrr.next()

                def tr_group(bufs, chunk, g8):
                    qTb, kTb, TqT, TkT = bufs
                    qb, kb, vb, Tq, Tk, Tv = chunk
                    if g8 < 4:
                        src, Tsrc, dstT, TdT, g4 = qb, Tq, qTb, TqT, g8
                    else:
                        src, Tsrc, dstT, TdT, g4 = kb, Tk, kTb, TkT, g8 - 4
                    pt, Tpt = ptq.next()
                    ptv = pt[:].bitcast(BF16)[:, 0:512].rearrange("p (a b) -> p a b", b=128)
                    for jj in range(4):
                        blk = g4 * 4 + jj
                        em.tr(ptv[:, jj, :], src[:, blk * 128:(blk + 1) * 128], identb[:], [Tsrc, Tid], [Tpt],
                              inc=(jj == 3))
                    em.copy("act", dstT[:, g4 * 4:(g4 + 1) * 4, :], ptv, [Tpt], [TdT])

                def make_epilogue(j, yb, Tyb):
                    def epi():
                        r0 = j * CH
                        yt_, Tyt = yTrr.next()
                        for g4 in range(4):
                            pt, Tpt = ptq.next()
                            ptv = pt[:].bitcast(BF16)[:, 0:512].rearrange("p (a b) -> p a b", b=128)
                            for jj in range(4):
                                blk = g4 * 4 + jj
                                em.tr(ptv[:, jj, :], yb[:, blk * 128:(blk + 1) * 128], identb[:], [Tyb, Tid], [Tpt],
                                      inc=(jj == 3))
                            em.copy("act", yt_[:, g4 * 4:(g4 + 1) * 4, :], ptv, [Tpt], [Tyt])
                        em.dma("act", yT.rearrange("(k p) t -> p k t", p=128)[:, :, r0:r0 + CH], yt_[:], Tyt, R=[Tyt])
                    return epi

                cur = load_chunk(order[0])
                curhb = load_hb(order[0]) if dr == 0 else None
                curT = tr_begin()
                for g8 in range(8):
                    tr_group(curT, cur, g8)
                pending_epi = None
                for oi, j in enumerate(order):
                    qb, kb, vb, Tq, Tk, Tv = cur
                    qTb, kTb, TqT, TkT = curT
                    nxt = nxthb = nxtT = None
                    if oi + 1 < NCH:
                        nxt = load_chunk(order[oi + 1])
                        nxtT = tr_begin()
                    if (dr == 0 and j == SEGCH) or (dr == 1 and j == SEGCH - 1):
                        for h in range(8):
                            em.ts("pool", Cm[h][:], Cm[h][:], carry[:, 0:1], None, ALU.mult, None,
                                  [TCm[h], Tcarry], [TCm[h]])
                            em.ts("pool", Cb[h][:], Cb[h][:], carry[:, 0:1], None, ALU.mult, None,
                                  [TCb[h], Tcarry], [TCb[h]])
                    ho, Tho = horr.next()
                    ctx = {}
                    if dr == 0:
                        hb_, og_, Thb_, Tog_ = curhb
                        yb, Tyb = ybrr.next()

                    def stA(head):
                        si = head % 2
                        for kc in range(2):
                            em.mm(pS[si], kTb[:, 2 * head + kc, :], qTb[:, 2 * head + kc, :], kc == 0, kc == 1,
                                  [TkT, TqT], [TpS[si]])

                    def stB(head):
                        is_m = head < 4
                        h = head % 4
                        ci = dr * 4 + h
                        si = head % 2
                        sc_, Tsc = scrr.next()
                        if is_m:
                            em.stt(sc_[:], pS[si], gW[:, j, h:h + 1], tri[:, dr, :], ALU.mult, ALU.mult,
                                   [TpS[si], TgW, Tc], [Tsc])
                            qsrc, Tqsrc = qTb[:, 2 * head:2 * head + 2, :], TqT
                        else:
                            em.tt("dve", sc_[:], pS[si], rmask[:, ci, :], ALU.mult, [TpS[si], Tc], [Tsc])
                            qx, Tqx = qxrr.next()
                            em.tt("pool", qx[:], qTb[:, 2 * head:2 * head + 2, :],
                                  rxi[:, ci:ci + 1, :].broadcast_to([128, 2, 128]), ALU.mult, [TqT, Tc], [Tqx])
                            qsrc, Tqsrc = qx[:], Tqx
                        kw, Tkw = kwrr.next()
                        wcol = gW[:, j, h:h + 1] if is_m else rzd[:, ci:ci + 1]
                        em.ts("pool", kw[:], kb[:, head * 256:(head + 1) * 256], wcol, None, ALU.mult, None,
                              [Tk, TgW, Tc], [Tkw])
                        if is_m:
                            em.ts("pool", Cm[head][:], Cm[head][:], gDEC[:, j, h:h + 1], None, ALU.mult, None,
                                  [TCm[head], TgW], [TCm[head]])
                        ctx[head] = (sc_, Tsc, qsrc, Tqsrc, kw, Tkw)

                    def stC(head):
                        is_m = head < 4
                        si = head % 2
                        NV = 257 if is_m else 256
                        sc_, Tsc, qsrc, Tqsrc, kw, Tkw = ctx[head]
                        pN, TpN = pNrr.next()
                        em.mm(pN[:, 0:NV], sc_[:], vb[:, head, 0:NV], True, False, [Tsc, Tv], [TpN], inc=False)
                        for kc in range(2):
                            em.mm(pN[:, 0:NV], qsrc[:, kc, :], Cb[head][:, kc, 0:NV], False, kc == 1,
                                  [Tqsrc, TCb[head]], [TpN])
                        pU, TpU = pUrr.next()
                        pUv = pU[:].rearrange("p (a b) -> p a b", b=256)
                        for kc in range(2):
                            em.mm(pUv[:, kc, :], kw[:, kc * 128:(kc + 1) * 128], vb[:, head, 0:256], True, True,
                                  [Tkw, Tv], [TpU], inc=(kc == 1))
                        pn2h = pN[:, 320:322]
                        if is_m:
                            for kc in range(2):
                                em.mm(pn2h[:, kc:kc + 1], kw[:, kc * 128:(kc + 1) * 128], vb[:, head, 256:257],
                                      False, True, [Tkw, Tv], [TpN], inc=(kc == 1))
                        ctx[head] = (pN, TpN, pUv, TpU, pn2h)

                    def stD(head):
                        is_m = head < 4
                        h = head % 4
                        ci = dr * 4 + h
                        si = head % 2
                        pN, TpN, pUv, TpU, pn2h = ctx[head]
                        if is_m:
                            ds_, Tds = dsmrr.next()
                            em.ts("dve", ds_[:, 0:1], pN[:, 256:257], -1.0, gTHR[:, j, h:h + 1], ALU.mult, ALU.max,
                                  [TpN, TgW], [Tds])
                            em.tt("dve", ds_[:, 0:1], ds_[:, 0:1], pN[:, 256:257], ALU.max, [TpN, Tds], [Tds])
                            em.op("dve", lambda e, o=ds_[:, 1:2], i_=ds_[:, 0:1]: e.reciprocal(out=o, in_=i_),
                                  [Tds], [Tds])
                            hsl = slice(head * 256, (head + 1) * 256)
                            if dr == 1:
                                em.ts("dve", ho[:, hsl], pN[:, 0:256], ds_[:, 1:2], None, ALU.mult, None,
                                      [TpN, Tds], [Tho])
                            else:
                                em.stt(ho[:, hsl], pN[:, 0:256], ds_[:, 1:2], hb_[:, hsl], ALU.mult, ALU.add,
                                       [TpN, Tds, Thb_], [Tho])
                                em.op("dve", lambda e, o=stats[:, h, :], i_=ho[:, hsl]: e.bn_stats(out=o, in_=i_),
                                      [Tho], [Tstats[h]])
                            dcol = gDEC[:, j, h:h + 1]
                            em.stt(Cm[head][:, :, 0:256], pUv, dcol, Cm[head][:, :, 0:256], ALU.mult, ALU.add,
                                   [TpU, TgW, TCm[head]], [TCm[head]])
                            em.stt(Cm[head][:, :, 256], pn2h, dcol, Cm[head][:, :, 256], ALU.mult, ALU.add,
                                   [TpN, TgW, TCm[head]], [TCm[head]])
                            em.copy("act", Cb[head][:], Cm[head][:], [TCm[head]], [TCb[head]])
                        else:
                            hsl = slice(head * 256, (head + 1) * 256)
                            if dr == 1:
                                em.copy("act", ho[:, hsl], pN[:, 0:256], [TpN], [Tho])
                            else:
                                em.tt("dve", ho[:, hsl], pN[:, 0:256], hb_[:, hsl], ALU.add, [TpN, Thb_], [Tho])
                                em.act(junk[:], ho[:, hsl], AF.Square, [Tho], [Tjunk, Tss[h]], accum=ss[:, h:h + 1])
                            dcol = rzd[:, 8 + ci:9 + ci]
                            em.stt(Cm[head][:, :, 0:256], Cm[head][:, :, 0:256], dcol, pUv, ALU.mult, ALU.add,
                                   [TpU, Tc, TCm[head]], [TCm[head]])
                            em.copy("act", Cb[head][:, :, 0:256], Cm[head][:, :, 0:256], [TCm[head]], [TCb[head]])

                    def stE1(head):
                        is_m = head < 4
                        h = head % 4
                        if is_m:
                            em.op("dve", lambda e, o=mv[:, h, :], i_=stats[:, h, :]: e.bn_aggr(out=o, in_=i_),
                                  [Tstats[h]], [Tmv[h]])
                            em.act(rs[:, h:h + 1], mv[:, h, 1:2], AF.Sqrt, [Tmv[h]], [Trs[h]], bias=NORM_EPS)
                        else:
                            em.act(rs[:, 4 + h:5 + h], ss[:, h:h + 1], AF.Sqrt, [Tss[h]], [Trs[4 + h]], bias=NORM_EPS,
                                   scale=1.0 / 256.0)

                    def stE2(head):
                        is_m = head < 4
                        h = head % 4
                        hsl = slice(head * 256, (head + 1) * 256)
                        if is_m:
                            em.op("dve", lambda e, o=rs[:, h:h + 1], i_=rs[:, h:h + 1]: e.reciprocal(out=o, in_=i_),
                                  [Trs[h]], [Trs[h]])
                            em.ts("dve", tmpn[:, hsl], ho[:, hsl], mv[:, h, 0:1], rs[:, h:h + 1], ALU.subtract, ALU.mult,
                                  [Tho, Tmv[h], Trs[h]], [Ttmpn[h]])
                        else:
                            em.op("dve", lambda e, o=rs[:, 4 + h:5 + h], i_=rs[:, 4 + h:5 + h]:
                                  e.reciprocal(out=o, in_=i_), [Trs[4 + h]], [Trs[4 + h]])
                            em.stt(yb[:, hsl], ho[:, hsl], rs[:, 4 + h:5 + h], og_[:, hsl], ALU.mult, ALU.mult,
                                   [Tho, Trs[4 + h], Tog_], [Tyb])

                    def stE3(head):
                        if head < 4:
                            hsl = slice(head * 256, (head + 1) * 256)
                            h = head
                            em.tt("pool", tmpn[:, hsl], tmpn[:, hsl], nw[:, hsl], ALU.mult, [Ttmpn[h], Tc], [Ttmpn[h]])
                            em.tt("pool", yb[:, hsl], tmpn[:, hsl], og_[:, hsl], ALU.mult, [Ttmpn[h], Tog_], [Tyb])

                    stA(0)
                    stB(0)
                    for head in range(8):
                        if head + 1 < 8:
                            stA(head + 1)
                            stB(head + 1)
                        stC(head)
                        stD(head)
                        if dr == 0:
                            stE1(head)
                            if head >= 1:
                                stE2(head - 1)
                            if head >= 2:
                                stE3(head - 2)
                        if head == 1:
                            if pending_epi is not None:
                                pending_epi()
                                pending_epi = None
                            if dr == 0 and oi + 1 < NCH:
                                nxthb = load_hb(order[oi + 1])
                        if nxt is not None:
                            tr_group(nxtT, nxt, head)
                    r0 = j * CH
                    if dr == 1:
                        em.dma("act", hb[r0:r0 + CH, :], ho[:], Tho, R=[Tho])
                    else:
                        stE2(7)
                        stE3(6)
                        stE3(7)
                        pending_epi = make_epilogue(j, yb, Tyb)
                    cur, curhb, curT = nxt, nxthb, nxtT
                if pending_epi is not None:
                    pending_epi()
                    pending_epi = None
                em.fence()
            em.flush()
            em.release_dsems()

        def layernorm_rows(r, Tr, out_ap, Tout, gam, bet, Tgb, stats6, mv2, rs1, Tst, eps):
            for q4 in range(4):
                em.op("dve", lambda e, o=stats6[:, q4, :], i_=r[:, q4 * 512:(q4 + 1) * 512]: e.bn_stats(out=o, in_=i_),
                      [Tr], [Tst])
            em.op("dve", lambda e, o=mv2[:], i_=stats6[:].rearrange("p a b -> p (a b)"): e.bn_aggr(out=o, in_=i_),
                  [Tst], [Tst])
            em.act(rs1[:], mv2[:, 1:2], AF.Sqrt, [Tst], [Tst], bias=eps)
            em.op("dve", lambda e, o=rs1[:], i_=rs1[:]: e.reciprocal(out=o, in_=i_), [Tst], [Tst])
            em.ts("dve", mv2[:, 1:2], mv2[:, 0:1], -1.0, rs1[:, 0:1], ALU.mult, ALU.mult, [Tst], [Tst])
            em.act(r, r, AF.Identity, [Tr, Tst], [Tr], bias=mv2[:, 1:2], scale=rs1[:, 0:1])
            em.tt("dve", r, r, gam, ALU.mult, [Tr, Tgb], [Tr])
            em.tt("pool", out_ap, r, bet, ALU.add, [Tr, Tgb], [Tout])

        with ExitStack() as st:
            wo = sb("wo", [128, KC, D], BF16, st)
            Two = em.tile("wo")
            yti = [sb("yti%d" % i, [128, KC, 512], BF16, st) for i in range(2)]
            ytrr = RR(list(zip(yti, em.tiles("yti", 2))))
            xin = [sb("xin%d" % i, [128, D], F32, st) for i in range(2)]
            xinrr = RR(list(zip(xin, em.tiles("xin", 2))))
            g1t = sb("g1t", [128, D], F32, st)
            Tg1 = em.tile("g1t")
            lng = sb("lng", [128, D], F32, st)
            lnb = sb("lnb", [128, D], F32, st)
            Tln = em.tile("ln")
            rr_ = [sb("r%d" % i, [128, D], F32, st) for i in range(2)]
            rrr = RR(list(zip(rr_, em.tiles("r", 2))))
            x1o = [sb("x1o%d" % i, [128, D], F32, st) for i in range(2)]
            x1rr = RR(list(zip(x1o, em.tiles("x1o", 2))))
            h2s = [sb("h2s%d" % i, [128, KC, 128], BF16, st) for i in range(2)]
            h2rr = RR(list(zip(h2s, em.tiles("h2s", 2))))
            stats6 = sb("stats6", [128, 4, 6], F32, st)
            mv2 = sb("mv2", [128, 2], F32, st)
            rs1 = sb("rs1", [128, 1], F32, st)
            Tst = em.tile("st")
            wov = w_out.rearrange("(k p) n -> p k n", p=128)
            for q4 in range(4):
                em.dma("pool", wo[:, :, q4 * 512:(q4 + 1) * 512], wov[:, :, q4 * 512:(q4 + 1) * 512], Two, W=[Two])
            em.dma("sp", lng[:], ln1g.partition_broadcast(128), Tln, W=[Tln])
            em.dma("sp", lnb[:], ln1b.partition_broadcast(128), Tln, W=[Tln])
            pmix = [(PB[i], PBT[i]) for i in range(4)]
            ptrr = RR([(PB[4], PBT[4]), (PB[5], PBT[5]), (PB[6], PBT[6]), (PB[7], PBT[7])])
            yTv = yT.rearrange("(k p) t -> p k t", p=128)
            h2Tv = h2T.rearrange("(k p) t -> p k t", p=128)
            pendX = None
            for i in range(NTILE):
                seg = (i * 512) // SEG
                if i == 0 or (i * 512) % SEG == 0:
                    em.dma("sp", g1t[:], gbc[0, seg], Tg1, W=[Tg1])
                yt_, Tyt = ytrr.next()
                em.dma("sp", yt_[:], yTv[:, :, i * 512:(i + 1) * 512], Tyt, W=[Tyt])
                for m in range(4):
                    r0 = i * 512 + m * 128
                    xi_, Txi = xinrr.next()
                    em.dma("sp", xi_[:], x[r0:r0 + 128, :], Txi, W=[Txi])
                    r, Tr = rrr.next()
                    for n4 in range(4):
                        pm, Tpm = pmix[n4]
                        for kc in range(KC):
                            em.mm(pm[:], yt_[:, kc, m * 128:(m + 1) * 128], wo[:, kc, n4 * 512:(n4 + 1) * 512],
                                  kc == 0, kc == KC - 1, [Tyt, Two], [Tpm])
                        sl = slice(n4 * 512, (n4 + 1) * 512)
                        em.tt("dve", r[:, sl], pm[:], g1t[:, sl], ALU.mult, [Tpm, Tg1], [Tr])
                    if pendX is not None:
                        pendX()
                        pendX = None
                    em.stt(r[:], xi_[:], ALPHA, r[:], ALU.mult, ALU.add, [Txi, Tr], [Tr])
                    xo, Txo = x1rr.next()
                    layernorm_rows(r[:], Tr, xo[:], Txo, lng[:], lnb[:], Tln, stats6, mv2, rs1, Tst, LN_EPS)

                    def postX(xo=xo, Txo=Txo, seg=seg, r0=r0):
                        em.dma("act", x1[r0:r0 + 128, :], xo[:], Txo, R=[Txo])
                        h2, Th2 = h2rr.next()
                        modulate_T(xo[:], Txo, h2, Th2, 0, 2, 3, seg, ptrr)
                        em.dma("act", h2Tv[:, :, r0:r0 + 128], h2[:], Th2, R=[Th2])
                    pendX = postX
            if pendX is not None:
                pendX()
            em.fence()
            em.flush()
            em.release_dsems()

        em.bg_tiles = []
        with ExitStack() as st:
            AW = 514
            h2i = sb("h2i", [128, KC, 512], BF16, st)
            Th2i = em.tile("h2i")
            act_ = sb("act_", [128, NFC, AW], BF16, st)
            Tact = em.tile("act")
            wua = [sb("wua%d" % i, [128, KC, 256], BF16, st) for i in range(2)]
            wug = [sb("wug%d" % i, [128, KC, 256], BF16, st) for i in range(2)]
            wurr = RR(list(zip(wua, wug, em.tiles("wua", 2), em.tiles("wug", 2))))
            wd = [sb("wd%d" % i, [128, 11, 512], BF16, st) for i in range(2)]
            wdrr = RR(list(zip(wd, em.tiles("wd", 2))))
            cf = sb("cf", [128, 2 * NFC, 4], F32, st)
            ncf = sb("ncf", [128, 2 * NFC, 2], F32, st)
            cm1 = sb("cm1", [128, 1], F32, st)
            Tcf = em.tile("cf")
            cwa = [sb("cwa%d" % i, [128, 516], F32, st) for i in range(3)]
            cwg = [sb("cwg%d" % i, [128, 516], F32, st) for i in range(3)]
            cwrr = RR(list(zip(cwa, cwg, em.tiles("cwa", 3), em.tiles("cwg", 3))))
            oa = [sb("oa%d" % i, [128, AW], F32, st) for i in range(3)]
            og2 = [sb("og2%d" % i, [128, AW], F32, st) for i in range(3)]
            oarr = RR(list(zip(oa, og2, em.tiles("oa", 3), em.tiles("og2", 3))))
            sav = sb("sav", [128, 2 * NFC, 2], F32, st)
            Tsav = em.tile("sav")
            g2t = sb("g2t", [128, D], F32, st)
            Tg2 = em.tile("g2t")
            g2r = sb("g2r", [128, 512], F32, st)
            Tg2r = em.tile("g2r")
            etmp = [sb("etmp%d" % i, [128, 512], F32, st) for i in range(2)]
            etrr = RR(list(zip(etmp, em.tiles("etmp", 2))))
            lng = sb("lng2", [128, D], F32, st)
            lnb = sb("lnb2", [128, D], F32, st)
            Tln = em.tile("ln2")
            racc = [sb("racc%d" % i, [128, D], F32, st) for i in range(4)]
            Tracc = em.tiles("racc", 4)
            stats6 = sb("stats6b", [128, 4, 6], F32, st)
            mv2 = sb("mv2b", [128, 2], F32, st)
            rs1 = sb("rs1b", [128, 1], F32, st)
            Tst = em.tile("stb")
            em.dma("sp", cf[:], conv_fm, Tcf, W=[Tcf])
            em.dma("sp", lng[:], ln2g.partition_broadcast(128), Tln, W=[Tln])
            em.dma("sp", lnb[:], ln2b.partition_broadcast(128), Tln, W=[Tln])
            em.ts("dve", cm1[:], carry[:], -1.0, None, ALU.add, None, [Tcarry], [Tcf])
            em.ts("dve", ncf[:, :, 0], cf[:, :, 0], cm1[:, 0:1], None, ALU.mult, None, [Tcf], [Tcf])
            em.ts("dve", ncf[:, :, 1], cf[:, :, 2], cm1[:, 0:1], None, ALU.mult, None, [Tcf], [Tcf])
            em.memset("pool", sav[:], 0.0, [Tsav])
            for wi in range(4):
                em.memset("pool", racc[wi][:], 0.0, [Tracc[wi]])
            for b in range(3):
                em.memset("pool", cwa[b][:], 0.0, [cwrr.items[b][2]])
                em.memset("pool", cwg[b][:], 0.0, [cwrr.items[b][3]])
            purr = RR([((PB[0], PBT[0]), (PB[1], PBT[1])), ((PB[2], PBT[2]), (PB[3], PBT[3]))])
            pdn = [(PB[4 + i], PBT[4 + i]) for i in range(4)]
            h2Tv = h2T.rearrange("(k p) t -> p k t", p=128)
            wuv2 = w_up.rearrange("(k p) n -> p k n", p=128)
            wdv2 = w_down.rearrange("(c p) n -> p c n", p=128)
            em.dma("sp", h2i[:], h2Tv[:, :, 0:512], Th2i, W=[Th2i])
            pending = []
            for i in range(NTILE):
                t0 = i * 512
                seg = t0 // SEG
                last = (i == NTILE - 1)
                boundary = (i > 0 and t0 % SEG == 0)
                Wd = 513 if last else 512
                if i == 0 or t0 % SEG == 0:
                    em.dma("sp", g2t[:], gbc[1, seg], Tg2, W=[Tg2])
                for cp in range(NFC // 2):
                    if pending and cp >= 2 and cp % 2 == 0:
                        pending.pop(0)()
                    wa, wg, Twa, Twg = wurr.next()
                    if NOCONV_UP:
                        em.dma("pool", wa[:], wuv2[:, :, cp * 256:(cp + 1) * 256], Twa, W=[Twa])
                        em.dma("pool", wg[:], wuv2[:, :, DFF + cp * 256:DFF + (cp + 1) * 256], Twg, W=[Twg])
                    else:
                        em.dma("sp", wa[:], wbup[cp, 0], Twa, R=[Twup], W=[Twa])
                        em.dma("sp", wg[:], wbup[cp, 1], Twg, R=[Twup], W=[Twg])
                    for c2 in range(2):
                        c = cp * 2 + c2
                        (pa, Tpa), (pg, Tpg) = purr.next()
                        for kc in range(KC):
                            em.mm(pa[:], wa[:, kc, c2 * 128:(c2 + 1) * 128], h2i[:, kc, :], kc == 0, kc == KC - 1,
                                  [Twa, Th2i], [Tpa])
                        for kc in range(KC):
                            em.mm(pg[:], wg[:, kc, c2 * 128:(c2 + 1) * 128], h2i[:, kc, :], kc == 0, kc == KC - 1,
                                  [Twg, Th2i], [Tpg])
                        ca, cg, Tca, Tcg = cwrr.next()
                        o_a, o_g, Toa, Tog = oarr.next()
                        for half, (ps_, Tps, cw, Tcw, oo, Too) in enumerate(
                                ((pa, Tpa, ca, Tca, o_a, Toa), (pg, Tpg, cg, Tcg, o_g, Tog))):
                            ch = c + half * NFC
                            em.copy("act", cw[:, 2:514], ps_[:], [Tps], [Tcw])
                            em.copy("act", cw[:, 0:2], sav[:, ch, :], [Tsav], [Tcw])
                            em.copy("act", sav[:, ch, :], cw[:, 512:514], [Tcw], [Tsav])
                            em.ts("pool", oo[:, 0:Wd], cw[:, 1:1 + Wd], cf[:, ch, 1:2], cf[:, ch, 3:4], ALU.mult, ALU.add,
                                  [Tcw, Tcf], [Too])
                            em.stt(oo[:, 0:Wd], cw[:, 0:Wd], cf[:, ch, 0:1], oo[:, 0:Wd], ALU.mult, ALU.add,
                                   [Tcw, Tcf, Too], [Too])
                            em.stt(oo[:, 0:Wd], cw[:, 2:2 + Wd], cf[:, ch, 2:3], oo[:, 0:Wd], ALU.mult, ALU.add,
                                   [Tcw, Tcf, Too], [Too])
                            if boundary:
                                em.stt(oo[:, 0:1], cw[:, 2:3], ncf[:, ch, 1:2], oo[:, 0:1], ALU.mult, ALU.add,
                                       [Tcw, Tcf, Too], [Too])
                                em.stt(oo[:, 1:2], cw[:, 1:2], ncf[:, ch, 0:1], oo[:, 1:2], ALU.mult, ALU.add,
                                       [Tcw, Tcf, Too], [Too])
                        em.act(o_a[:, 0:Wd], o_a[:, 0:Wd], AF.Silu, [Toa], [Toa])
                        em.tt("pool", act_[:, c, 0:Wd], o_a[:, 0:Wd], o_g[:, 0:Wd], ALU.mult, [Toa, Tog], [Tact])
                if not last:
                    em.dma("act", h2i[:], h2Tv[:, :, t0 + 512:t0 + 1024], Th2i, W=[Th2i])
                wins = []
                for m in range(4):
                    wins.append((m * 128, t0 - 1 + m * 128, 1 if (i == 0 and m == 0) else 0))
                groups = [wins]
                if last:
                    groups.append([(385, t0 + 384, 0)])
                for gi, grp in enumerate(groups):
                    while pending:
                        pending.pop(0)()
                    for wi, (col0, tok0, p0) in enumerate(grp):
                        em.dma("sp", racc[wi][p0:128, :], x1[tok0 + p0:tok0 + 128, :], Tracc[wi], W=[Tracc[wi]])
                    for n4 in range(4):
                        sl = slice(n4 * 512, (n4 + 1) * 512)
                        if boundary:
                            em.dma("sp", g2r[0:1, :], gbc[1, seg - 1, 0:1, sl], Tg2r, W=[Tg2r])
                        for ks in range(4):
                            wdt, Twd = wdrr.next()
                            if NOCONV_UP:
                                em.dma("pool", wdt[:], wdv2[:, ks * 11:(ks + 1) * 11, n4 * 512:(n4 + 1) * 512], Twd, W=[Twd])
                            else:
                                em.dma("sp", wdt[:], wbdn[n4, ks], Twd, R=[Twdn], W=[Twd])
                            for wi, (col0, tok0, p0) in enumerate(grp):
                                pd, Tpd = pdn[wi]
                                for cc in range(11):
                                    em.mm(pd[:], act_[:, ks * 11 + cc, col0:col0 + 128], wdt[:, cc, :],
                                          ks == 0 and cc == 0, ks == 3 and cc == 10, [Tact, Twd], [Tpd],
                                          inc=(cc == 10))
                        for wi in range(len(grp)):
                            pd, Tpd = pdn[wi]
                            tm, Ttm = etrr.next()
                            em.tt("dve", tm[:], pd[:], g2t[:, sl], ALU.mult, [Tpd, Tg2], [Ttm])
                            if boundary and grp[wi][0] == 0:
                                em.tt("dve", tm[0:1, :], pd[0:1, :], g2r[0:1, :], ALU.mult, [Tpd, Tg2r], [Ttm])
                            em.stt(racc[wi][:, sl], racc[wi][:, sl], ALPHA, tm[:], ALU.mult, ALU.add,
                                   [Tracc[wi], Ttm], [Tracc[wi]])

                    def make_ep(wi, tok0, p0):
                        def ep():
                            r, Tr = racc[wi], Tracc[wi]
                            layernorm_rows(r[:], Tr, r[:], Tr, lng[:], lnb[:], Tln, stats6, mv2, rs1, Tst, LN_EPS)
                        return ep

                    def make_st(wi, tok0, p0):
                        def stf():
                            em.dma("act", y[tok0 + p0:tok0 + 128, :], racc[wi][p0:128, :], Tracc[wi], R=[Tracc[wi]])
                        return stf
                    pending = []
                    prev_st = None
                    for wi, (col0, tok0, p0) in enumerate(grp):
                        ep_, st_ = make_ep(wi, tok0, p0), make_st(wi, tok0, p0)
                        if prev_st is None:
                            pending.append(ep_)
                        else:
                            pending.append(lambda a=ep_, b=prev_st: (a(), b()))
                        prev_st = st_
                    pending.append(prev_st)
            while pending:
                pending.pop(0)()
            em.fence()
            em.flush()
    nc._n_ops = (em.n_ops, em.n_wait, dict(em.cnt))
    return nc


def _consts():
    pos = np.arange(128, dtype=np.float32)
    s = pos[:, None]
    l = pos[None, :]
    tri = np.stack([(s <= l), (s >= l), np.ones((128, 128), bool)]).astype(np.float32)
    hd = np.arange(4, dtype=np.float32)
    lg_f = np.log1p(-np.exp2(-5.0 - hd)).astype(np.float32)
    lg_b = np.log1p(-np.exp2(-5.5 - hd)).astype(np.float32)
    rmask = np.zeros((8, 128, 128), np.float32)
    rxi = np.zeros((8, 128), np.float32)
    rzd = np.zeros((128, 16), np.float32)
    for h in range(4):
        rmask[h] = np.where(s <= l, np.exp(lg_f[h] * np.maximum(l - s, 0.0)), 0.0)
        rxi[h] = np.exp(lg_f[h] * (pos + 1.0))
        rzd[:, h] = np.exp(lg_f[h] * (127.0 - pos))
        rzd[:, 8 + h] = np.exp(lg_f[h] * 128.0)
        rmask[4 + h] = np.where(s >= l, np.exp(lg_b[h] * np.maximum(s - l, 0.0)), 0.0)
        rxi[4 + h] = np.exp(lg_b[h] * (128.0 - pos))
        rzd[:, 4 + h] = np.exp(lg_b[h] * pos)
        rzd[:, 12 + h] = np.exp(lg_b[h] * 128.0)
    return tri, rmask.astype(np.float32), rxi.reshape(-1).astype(np.float32), rzd.astype(np.float32)


def _rot_table(npos):
    inv = (1.0 / (np.float32(10000.0) ** np.linspace(0.0, 1.0, 128, dtype=np.float32))).astype(np.float32)
    ang = (np.arange(npos, dtype=np.float32)[:, None] * inv[None, :]).astype(np.float32)
    return np.stack([np.cos(ang), np.sin(ang)], axis=1).astype(np.float32)


def prepare_shared(w_ada, b_ada, w_in, b_igate, b_fgate, mlstm_norm_w, w_out, ln1_g, ln1_b, w_up, conv_w, conv_b,
                   w_down, ln2_g, ln2_b):
    f = lambda a: np.ascontiguousarray(np.asarray(a, dtype=np.float32))
    w_in0 = f(w_in)[0]
    perm = np.arange(N_IN)
    for base in (4112, 5136):
        for h in range(4):
            b0 = base + h * 256
            perm[b0:b0 + 128] = b0 + 2 * np.arange(128)
            perm[b0 + 128:b0 + 256] = b0 + 2 * np.arange(128) + 1
    w_in_p = np.ascontiguousarray(w_in0[:, perm])
    cw = f(conv_w)[0]
    cb = f(conv_b)[0]
    conv4 = np.concatenate([cw, cb[None, :]], axis=0)
    conv_fm = np.ascontiguousarray(conv4.T.reshape(2 * NFC, 128, 4).transpose(1, 0, 2))
    tri, rmask, rxi, rzd = _consts()
    b_ada0 = f(b_ada)[0]
    return {
        "w_ada": f(w_ada)[0], "b_ada": b_ada0,
        "b_ada_fm": np.ascontiguousarray(b_ada0.reshape(96, 128).T),
        "w_in": w_in_p, "b_gate": np.concatenate([f(b_igate)[0], f(b_fgate)[0]]),
        "norm_w": f(mlstm_norm_w)[0], "w_out": f(w_out)[0], "ln1_g": f(ln1_g)[0], "ln1_b": f(ln1_b)[0],
        "w_up": f(w_up)[0], "conv_fm": conv_fm, "w_down": f(w_down)[0], "ln2_g": f(ln2_g)[0], "ln2_b": f(ln2_b)[0],
        "tri": tri, "rmask": rmask, "rxi": rxi, "rzd": rzd,
    }


def core_inputs(xc, c2, carry, rot):
    return {
        "x": np.ascontiguousarray(xc, dtype=np.float32),
        "c_lay": np.ascontiguousarray(np.asarray(c2, np.float32).reshape(2, KC, 128).transpose(2, 1, 0)),
        "carry": np.full((128, 1), carry, np.float32),
        "rot": np.ascontiguousarray(rot),
    }


_NC_CACHE = {}


def kernel(x_prompt, x_sample, c_prompt, c_sample, w_ada, b_ada, w_in, b_igate, b_fgate, mlstm_norm_w, w_out,
           ln1_g, ln1_b, w_up, conv_w, conv_b, w_down, ln2_g, ln2_b):
    SEG = 2048
    x_prompt = np.asarray(x_prompt, np.float32)
    x_sample = np.asarray(x_sample, np.float32)
    c_prompt = np.asarray(c_prompt, np.float32)
    c_sample = np.asarray(c_sample, np.float32)
    shared = prepare_shared(w_ada, b_ada, w_in, b_igate, b_fgate, mlstm_norm_w, w_out, ln1_g, ln1_b, w_up, conv_w,
                            conv_b, w_down, ln2_g, ln2_b)
    rot_full = _rot_table(2 * SEG)
    rot_p = rot_full
    rot_s = np.concatenate([rot_full[:SEG], rot_full[:SEG]], axis=0)
    in_maps = []
    for i in range(4):
        m = core_inputs(x_prompt[i], np.stack([c_prompt[i], c_prompt[i]]), 1.0, rot_p)
        m.update(shared)
        in_maps.append(m)
    for i in range(4):
        m = core_inputs(x_sample[2 * i:2 * i + 2].reshape(2 * SEG, D), c_sample[2 * i:2 * i + 2], 0.0, rot_s)
        m.update(shared)
        in_maps.append(m)
    if SEG not in _NC_CACHE:
        _NC_CACHE[SEG] = build_program(SEG)
    nc = _NC_CACHE[SEG]
    res = run_bass_kernel_spmd(nc, in_maps, core_ids=list(range(8)))
    outs = [np.asarray(r["y"], np.float32) for r in res.results]
    y_prompt = np.stack(outs[:4], axis=0)
    y_sample = np.concatenate([o.reshape(2, SEG, D) for o in outs[4:]], axis=0)
    return (y_prompt, y_sample)
```

```python
import numpy as np
from contextlib import ExitStack
import concourse.bass as bass
import concourse.mybir as mybir
from concourse.bass_utils import run_bass_kernel_spmd

F32 = mybir.dt.float32
BF16 = mybir.dt.bfloat16
ALU = mybir.AluOpType
AF = mybir.ActivationFunctionType

D = 2048
KC = 16
DFF = 5632
NFC = 44
N_IN = 8208
CH = 128
ALPHA = 2.0 ** 0.25
LN_EPS = 1e-5
NORM_EPS = 1e-6

ENGS = ("pe", "act", "dve", "pool", "sp")
SAME_ENGINE_SYNC = True
import os
NOCONV = os.environ.get("MK_NOCONV", "0") == "1"
NOCONV_IN = NOCONV or os.environ.get("MK_NOCONV_IN", "0") == "1"
NOCONV_UP = NOCONV or os.environ.get("MK_NOCONV_UP", "0") == "1"
PROD_ACT = os.environ.get("MK_PROD_ACT", "1") == "1"
PROD_POOL = os.environ.get("MK_PROD_POOL", "1") == "1"


class T:
    __slots__ = ("name", "lastw", "reads", "dsem")

    def __init__(self, name):
        self.name = name
        self.lastw = None
        self.reads = {}
        self.dsem = None


class Emitter:
    def __init__(self, nc, es):
        self.nc = nc
        self.es = es
        self.sems = {}
        for e in ("pe", "act", "dve", "pool"):
            self.sems[e] = es.enter_context(nc.semaphore("sem_" + e))
        self.cnt = {e: 0 for e in ENGS}
        self.seen = {e: {} for e in ENGS}
        self.q = {e: [] for e in ENGS}
        self.dma_sems = {}
        self.free_dsems = []
        self.bound = []
        self.bg_tiles = []
        self.ntile = 0
        self.n_wait = 0
        self.n_ops = 0

    def tile(self, name=None):
        self.ntile += 1
        return T((name or "t") + "_%d" % self.ntile)

    def tiles(self, name, n):
        return [self.tile(name) for _ in range(n)]

    def dsem_for(self, t):
        if t.dsem is None:
            if self.free_dsems:
                key = self.free_dsems.pop()
            else:
                key = "d%d" % len(self.dma_sems)
                self.sems[key] = self.es.enter_context(self.nc.semaphore(key))
                self.dma_sems[key] = 0
            t.dsem = key
            self.bound.append(t)
        return t.dsem

    def release_dsems(self, keep=()):
        nb = []
        for t in self.bound:
            if t in keep or t in self.bg_tiles:
                nb.append(t)
            else:
                self.free_dsems.append(t.dsem)
                t.dsem = None
        self.bound = nb

    def _need(self, eng, ev, waits):
        if ev is None:
            return
        k, v = ev
        if k == eng and (eng == "pe" or not SAME_ENGINE_SYNC):
            return
        if self.seen[eng].get(k, 0) >= v:
            return
        self.seen[eng][k] = v
        waits.append((k, v))

    def _deps(self, eng, reads, writes):
        waits = []
        for t in reads:
            self._need(eng, t.lastw, waits)
        for t in writes:
            self._need(eng, t.lastw, waits)
            for k, v in t.reads.items():
                if k == eng:
                    continue
                self._need(eng, (k, v), waits)
        return waits

    def _mark(self, ev, reads, writes):
        k, v = ev
        for t in reads:
            if t.reads.get(k, 0) < v:
                t.reads[k] = v
        for t in writes:
            t.lastw = ev
            t.reads = {}

    def op(self, eng, fn, R=(), W=(), inc=True):
        waits = self._deps(eng, R, W)
        if inc:
            self.cnt[eng] += 1
            ev = (eng, self.cnt[eng])
        else:
            ev = (eng, self.cnt[eng] + 1)
        self._mark(ev, R, W)
        self.q[eng].append((waits, fn, (eng, 1) if inc else None))
        self.n_wait += len(waits)
        self.n_ops += 1

    def dma(self, queue, out, in_, sem_tile, R=(), W=(), cast=False):
        if queue == "act" and not PROD_ACT:
            queue = "sp"
        if queue == "pool" and not PROD_POOL and out.dtype == in_.dtype:
            queue = "sp"
        waits = self._deps(queue, R, W)
        key = self.dsem_for(sem_tile)
        self.dma_sems[key] += 16
        ev = (key, self.dma_sems[key])
        self._mark(ev, R, W)
        self.q[queue].append((waits, lambda e: e.dma_start(out=out, in_=in_), (key, 16)))
        self.n_wait += len(waits)
        self.n_ops += 1

    def fence(self):
        bgk = set(t.dsem for t in self.bg_tiles if t.dsem is not None)
        evs = [(e, self.cnt[e]) for e in ("pe", "act", "dve", "pool") if self.cnt[e] > 0]
        evs += [(k, v) for k, v in self.dma_sems.items() if v > 0 and k not in bgk]
        for eng in ENGS:
            waits = []
            for k, v in evs:
                if k == eng or self.seen[eng].get(k, 0) >= v:
                    continue
                self.seen[eng][k] = v
                waits.append((k, v))
            if waits:
                self.q[eng].append((waits, None, None))
                self.n_wait += len(waits)

    def flush(self):
        nc, sems, q = self.nc, self.sems, self.q
        if os.environ.get("MK_VERBOSE"):
            print("flush: sbuf_remaining", nc.sbuf_bytes_remaining, "ops", self.n_ops, "sems", len(self.sems), flush=True)

        def run(name):
            def body(eng):
                for waits, fn, inc in q[name]:
                    for k, v in waits:
                        eng.wait_ge(sems[k], v)
                    if fn is not None:
                        ins = fn(eng)
                        if inc is not None:
                            ins.then_inc(sems[inc[0]], inc[1])
            return body

        with nc.Block(no_gpsimd_drain=True) as block:
            block.tensor(run("pe"))
            block.scalar(run("act"))
            block.vector(run("dve"))
            block.gpsimd(run("pool"))
            block.sync(run("sp"))
        self.q = {e: [] for e in ENGS}

    def mm(self, out, lhsT, rhs, start, stop, R, W, inc=None):
        self.op("pe", lambda e: e.matmul(out, lhsT=lhsT, rhs=rhs, start=start, stop=stop), R, W,
                inc=stop if inc is None else inc)

    def tr(self, out, in_, ident, R, W, inc=True):
        self.op("pe", lambda e: e.transpose(out, in_, ident), R, W, inc=inc)

    def act(self, out, in_, func, R, W, bias=None, scale=None, accum=None):
        kw = {}
        if bias is not None:
            kw["bias"] = bias
        if scale is not None:
            kw["scale"] = scale
        if accum is not None:
            kw["accum_out"] = accum
        self.op("act", lambda e: e.activation(out=out, in_=in_, func=func, **kw), R, W)

    def tt(self, eng, out, in0, in1, op, R, W):
        self.op(eng, lambda e: e.tensor_tensor(out=out, in0=in0, in1=in1, op=op), R, W)

    def ts(self, eng, out, in0, s1, s2, op0, op1, R, W):
        if s2 is None and eng == "pool":
            s2, op1 = 1.0, ALU.mult
        if s2 is None:
            self.op(eng, lambda e: e.tensor_scalar(out=out, in0=in0, scalar1=s1, scalar2=None, op0=op0), R, W)
        else:
            self.op(eng, lambda e: e.tensor_scalar(out=out, in0=in0, scalar1=s1, scalar2=s2, op0=op0, op1=op1), R, W)

    def stt(self, out, in0, scalar, in1, op0, op1, R, W):
        self.op("dve", lambda e: e.scalar_tensor_tensor(out=out, in0=in0, scalar=scalar, in1=in1, op0=op0, op1=op1),
                R, W)

    def copy(self, eng, out, in_, R, W):
        if eng == "act":
            self.op("act", lambda e: e.copy(out=out, in_=in_), R, W)
        else:
            self.op(eng, lambda e: e.tensor_copy(out=out, in_=in_), R, W)

    def memset(self, eng, ap, val, W):
        self.op(eng, lambda e: e.memset(ap, val), (), W)


class RR:
    def __init__(self, items):
        self.items = items
        self.i = 0

    def next(self):
        it = self.items[self.i % len(self.items)]
        self.i += 1
        return it


def build_program(SEG, debug=False):
    NT = 2 * SEG
    NTILE = NT // 512
    NCH = NT // CH
    SEGCH = SEG // CH
    nc = bass.Bass("TRN2", target_bir_lowering=False)

    def din(name, shape, dt=F32):
        return nc.dram_tensor(name, list(shape), dt, kind="ExternalInput").ap()

    def dscr(name, shape, dt):
        return nc.dram_tensor(name, list(shape), dt, kind="ExternalOutput" if debug else "Internal").ap()

    x = din("x", [NT, D])
    c_lay = din("c_lay", [128, KC, 2])
    carry_in = din("carry", [128, 1])
    rot = din("rot", [NT, 2, 128])
    w_ada = din("w_ada", [D, 6 * D])
    b_ada_fm = din("b_ada_fm", [128, 96])
    b_ada = din("b_ada", [6 * D])
    w_in = din("w_in", [D, N_IN])
    b_gate = din("b_gate", [16])
    norm_w = din("norm_w", [1024])
    w_out = din("w_out", [D, D])
    ln1g = din("ln1_g", [D])
    ln1b = din("ln1_b", [D])
    w_up = din("w_up", [D, 2 * DFF])
    conv_fm = din("conv_fm", [128, 2 * NFC, 4])
    w_down = din("w_down", [DFF, D])
    ln2g = din("ln2_g", [D])
    ln2b = din("ln2_b", [D])
    tri_in = din("tri", [3, 128, 128])
    rmask_in = din("rmask", [8, 128, 128])
    rxi_in = din("rxi", [8 * 128])
    rzd_in = din("rzd", [128, 16])
    y = nc.dram_tensor("y", [NT, D], F32, kind="ExternalOutput").ap()

    zq = dscr("zq", [NT, D], BF16)
    zk = dscr("zk", [NT, D], BF16)
    zv = dscr("zv", [NT, D], BF16)
    zg = dscr("zg", [NT, D], F32)
    zgate = dscr("zgate", [NT, 16], F32)
    gbc = dscr("gbc", [2, 2, 128, D], F32)
    hb = dscr("hb", [NT, D], F32)
    yT = dscr("yT", [D, NT], BF16)
    x1 = dscr("x1", [NT, D], F32)
    h2T = dscr("h2T", [D, NT], BF16)
    wbin = nc.dram_tensor("wbin", [16, 128, KC, 512], BF16, kind="Internal").ap()
    wbup = nc.dram_tensor("wbup", [NFC // 2, 2, 128, KC, 256], BF16, kind="Internal").ap()
    wbdn = nc.dram_tensor("wbdn", [4, 4, 128, 11, 512], BF16, kind="Internal").ap()

    with ExitStack() as es:
        em = Emitter(nc, es)

        sbn = [0]

        def sb(name, shape, dt, st=None):
            sbn[0] += 1
            return (st or es).enter_context(nc.sbuf_tensor("s%d_%s" % (sbn[0], name), list(shape), dt))

        PB = [es.enter_context(nc.psum_tensor("pb%d" % i, [128, 512], F32)) for i in range(8)]
        PBT = [em.tile("pb%d" % i) for i in range(8)]

        cv = sb("cv", [128, 4, KC, 2], F32)
        Tcv = em.tile("cv")
        identf = sb("identf", [128, 128], F32)
        identb = sb("identb", [128, 128], BF16)
        Tid = em.tile("ident")
        carry = sb("carry_sb", [128, 1], F32)
        Tcarry = em.tile("carry")
        em.memset("pool", identf[:], 0.0, [Tid])
        em.op("pool", lambda e: e.affine_select(out=identf[:], in_=identf[:], compare_op=ALU.not_equal, fill=1.0,
                                                base=0, pattern=[[-1, 128]], channel_multiplier=1), [Tid], [Tid])
        em.copy("pool", identb[:], identf[:], [Tid], [Tid])
        em.dma("sp", carry[:], carry_in, Tcarry, W=[Tcarry])

        Twin = em.tiles("wbin", 16)
        Twup = em.tile("wbup")
        Twdn = em.tile("wbdn")
        wiv = w_in.rearrange("(k p) n -> p k n", p=128)
        wuv = w_up.rearrange("(k p) n -> p k n", p=128)
        wdv = w_down.rearrange("(c p) n -> p c n", p=128)
        blocks = []
        for cb in range(16):
            c0 = cb * 512 if cb < 8 else 4112 + (cb - 8) * 512
            kind = ["qs", "cast", "cast", "sig", "rotq", "rotk", "cast", "silu"][cb // 2]
            dst = [zq, zk, zv, zg, zq, zk, zv, zg][cb // 2]
            d0 = (cb % 2) * 512 + (1024 if cb >= 8 else 0)
            blocks.append((c0, kind, dst, d0))

        def convert_in():
            for cb in range(16):
                em.dma("pool", wbin[cb], wiv[:, :, blocks[cb][0]:blocks[cb][0] + 512], Twin[cb], W=[Twin[cb]])

        def convert_up():
            for cp in range(NFC // 2):
                em.dma("pool", wbup[cp, 0], wuv[:, :, cp * 256:(cp + 1) * 256], Twup, W=[Twup])
                em.dma("pool", wbup[cp, 1], wuv[:, :, DFF + cp * 256:DFF + (cp + 1) * 256], Twup, W=[Twup])

        def convert_down():
            for n4 in range(4):
                for ks in range(4):
                    em.dma("pool", wbdn[n4, ks], wdv[:, ks * 11:(ks + 1) * 11, n4 * 512:(n4 + 1) * 512], Twdn,
                           W=[Twdn])

        wG = sb("wG", [128, KC, 16], BF16)
        TwG = em.tile("wG")
        wGf = sb("wGf", [128, KC, 16], F32)
        TwGf = em.tile("wGf")
        em.dma("sp", wGf[:], wiv[:, :, 4096:4112], TwGf, W=[TwGf])
        em.copy("dve", wG[:], wGf[:], [TwGf], [TwG])
        CONV_AT = os.environ.get("MK_CONV_AT", "a_end")
        if not NOCONV_IN and CONV_AT == "a_start":
            convert_in()
        if not NOCONV_UP and CONV_AT == "a_start":
            convert_up()
            convert_down()
        CONV_UP_LATE = (not NOCONV_UP) and CONV_AT == "a_end"
        if not NOCONV_IN and CONV_AT == "a_end":
            convert_in()

        with ExitStack() as st:
            cl = sb("cl", [128, KC, 2], F32, st)
            sT = sb("sT", [128, KC, 2], BF16, st)
            sB = [sb("sB%d" % s, [128, KC, 128], BF16, st) for s in range(2)]
            bfm = sb("bfm", [128, 96], F32, st)
            brow = sb("brow", [128, 2, D], F32, st)
            wA = [sb("wA%d" % i, [128, KC, 512], BF16, st) for i in range(3)]
            gst = [sb("gst%d" % i, [128, 512], F32, st) for i in range(2)]
            Tcl, TsT, TsB, Tbfm, Tbrow = em.tile("cl"), em.tile("sT"), em.tile("sB"), em.tile("bfm"), em.tile("brow")
            wArr = RR(list(zip(wA, em.tiles("wA", 3))))
            gstrr = RR(list(zip(gst, em.tiles("gst", 2))))
            em.dma("sp", cl[:], c_lay, Tcl, W=[Tcl])
            em.dma("sp", bfm[:], b_ada_fm, Tbfm, W=[Tbfm])
            em.dma("sp", brow[:, 0, :], b_ada[2 * D:3 * D].partition_broadcast(128), Tbrow, W=[Tbrow])
            em.dma("sp", brow[:, 1, :], b_ada[5 * D:6 * D].partition_broadcast(128), Tbrow, W=[Tbrow])
            em.act(sT[:], cl[:], AF.Silu, [Tcl], [TsT])
            for s in range(2):
                em.copy("dve", sB[s][:], sT[:, :, s:s + 1].broadcast_to([128, KC, 128]), [TsT], [TsB])
            pcv = PB[0][:, 0:128].rearrange("p (a b) -> p a b", b=2)
            Tpcv = PBT[0]
            pgrr = RR([(PB[1], PBT[1]), (PB[2], PBT[2])])
            wav = w_ada.rearrange("(k p) n -> p k n", p=128)
            order = [0, 1, 3, 4, 2, 5]
            fm_index = {0: 0, 1: 1, 3: 2, 4: 3}
            for v6 in order:
                for sub in range(4):
                    jb = v6 * 4 + sub
                    wt, Tw = wArr.next()
                    em.dma("pool", wt[:], wav[:, :, jb * 512:(jb + 1) * 512], Tw, W=[Tw])
                    if v6 in fm_index:
                        vi = fm_index[v6]
                        for qq in range(4):
                            fc = sub * 4 + qq
                            for kc in range(KC):
                                em.mm(pcv[:, vi * 16 + fc, :], wt[:, kc, qq * 128:(qq + 1) * 128], sT[:, kc, :],
                                      kc == 0, kc == KC - 1, [Tw, TsT], [Tpcv])
                    else:
                        vi2 = 0 if v6 == 2 else 1
                        for s in range(2):
                            pg, Tpg = pgrr.next()
                            for kc in range(KC):
                                em.mm(pg[:], sB[s][:, kc, :], wt[:, kc, :], kc == 0, kc == KC - 1, [Tw, TsB], [Tpg])
                            g, Tg = gstrr.next()
                            em.tt("dve", g[:], pg[:], brow[:, vi2, sub * 512:(sub + 1) * 512], ALU.add,
                                  [Tpg, Tbrow], [Tg])
                            em.dma("sp", gbc[vi2, s, :, sub * 512:(sub + 1) * 512], g[:], Tg, R=[Tg])
                if v6 == 1 or v6 == 4:
                    for v6b in ((0, 1) if v6 == 1 else (3, 4)):
                        vi = fm_index[v6b]
                        em.tt("dve", cv[:, vi, :, :], pcv[:, vi * 16:(vi + 1) * 16, :],
                              bfm[:, v6b * 16:(v6b + 1) * 16].unsqueeze(2).broadcast_to([128, KC, 2]), ALU.add,
                              [Tpcv, Tbfm], [Tcv])
                        if v6b in (1, 4):
                            em.ts("dve", cv[:, vi, :, :], cv[:, vi, :, :], 1.0, None, ALU.add, None, [Tcv], [Tcv])
            em.fence()
            em.flush()
            em.release_dsems()

        def modulate_T(src, Tsrc, dst, Tdst, col0, vi_sh, vi_sc, seg, pbanks):
            for g4 in range(4):
                pt, Tpt = pbanks.next()
                ptv = pt[:].rearrange("p (a b) -> p a b", b=128)
                for j in range(4):
                    kc = g4 * 4 + j
                    em.tr(ptv[:, j, :], src[:, kc * 128:(kc + 1) * 128], identf[:], [Tsrc, Tid], [Tpt], inc=(j == 3))
                for j in range(4):
                    kc = g4 * 4 + j
                    em.act(dst[:, kc, col0:col0 + 128], ptv[:, j, :], AF.Identity, [Tpt, Tcv], [Tdst],
                           bias=cv[:, vi_sh, kc, seg:seg + 1], scale=cv[:, vi_sc, kc, seg:seg + 1])

        with ExitStack() as st:
            xt = [sb("xt%d" % i, [128, 4, D], F32, st) for i in range(2)]
            rt = [sb("rt%d" % i, [128, 4, 2, 128], F32, st) for i in range(2)]
            hT = sb("hT", [128, KC, 512], BF16, st)
            wB = [sb("wB%d" % i, [128, KC, 512], BF16, st) for i in range(3)]
            szb = [sb("szb%d" % i, [128, 512], BF16, st) for i in range(4)]
            szf = [sb("szf%d" % i, [128, 512], F32, st) for i in range(3)]
            rtmp = [sb("rtmp%d" % i, [128, 256], F32, st) for i in range(4)]
            xrr = RR(list(zip(xt, em.tiles("xt", 2), rt, em.tiles("rt", 2))))
            ThT = em.tile("hT")
            wrr = RR(list(zip(wB, em.tiles("wB", 3))))
            szbrr = RR(list(zip(szb, em.tiles("szb", 4))))
            szfrr = RR(list(zip(szf, em.tiles("szf", 3))))
            Trtmp = em.tiles("rtmp", 4)
            ptrr = RR([(PB[0], PBT[0]), (PB[1], PBT[1])])
            pzrr = RR([(PB[i], PBT[i]) for i in range(2, 8)])

            def load_x(i):
                xb, Tx, rb, Tr = xrr.next()
                em.dma("sp", xb[:], x[i * 512:(i + 1) * 512, :].rearrange("(m p) d -> p m d", p=128), Tx, W=[Tx])
                em.dma("sp", rb[:], rot[i * 512:(i + 1) * 512].rearrange("(m p) a b -> p m a b", p=128), Tr, W=[Tr])
                return xb, Tx, rb, Tr

            if CONV_UP_LATE:
                em.bg_tiles = [Twup, Twdn]
                convert_up()
                convert_down()
            nxt = load_x(0)
            for i in range(NTILE):
                xb, Tx, rb, Tr = nxt
                seg = (i * 512) // SEG
                if i + 1 < NTILE:
                    nxt = load_x(i + 1)
                for m in range(4):
                    modulate_T(xb[:, m, :], Tx, hT, ThT, m * 128, 0, 1, seg, ptrr)
                for m in range(4):
                    pz, Tpz = pzrr.next()
                    for kc in range(KC):
                        em.mm(pz[:, 0:16], hT[:, kc, m * 128:(m + 1) * 128], wG[:, kc, :], kc == 0, kc == KC - 1,
                              [ThT, TwG], [Tpz])
                    sf, Tsf = szfrr.next()
                    em.copy("act", sf[:, 0:16], pz[:, 0:16], [Tpz], [Tsf])
                    r0 = i * 512 + m * 128
                    em.dma("act", zgate[r0:r0 + 128, :], sf[:, 0:16], Tsf, R=[Tsf])
                for cb in range(16):
                    c0, kind, dst, d0 = blocks[cb]
                    wt, Tw = wrr.next()
                    if NOCONV_IN:
                        em.dma("pool", wt[:], wiv[:, :, c0:c0 + 512], Tw, W=[Tw])
                    else:
                        em.dma("sp", wt[:], wbin[cb], Tw, R=[Twin[cb]], W=[Tw])
                    for m in range(4):
                        pz, Tpz = pzrr.next()
                        for kc in range(KC):
                            em.mm(pz[:], hT[:, kc, m * 128:(m + 1) * 128], wt[:, kc, :], kc == 0, kc == KC - 1,
                                  [ThT, Tw], [Tpz])
                        r0 = i * 512 + m * 128
                        sq = "act"
                        if kind in ("sig", "silu"):
                            so, Tso = szfrr.next()
                            em.act(so[:], pz[:], AF.Sigmoid if kind == "sig" else AF.Silu, [Tpz], [Tso])
                        elif kind == "qs":
                            so, Tso = szbrr.next()
                            em.act(so[:], pz[:], AF.Copy, [Tpz], [Tso], scale=1.0 / 16.0)
                        elif kind == "cast":
                            so, Tso = szbrr.next()
                            em.copy("act", so[:], pz[:], [Tpz], [Tso])
                        else:
                            so, Tso = szbrr.next()
                            sc = 1.0 if kind == "rotq" else 1.0 / 16.0
                            pv = pz[:].rearrange("p (h two i) -> p h two i", two=2, i=128)
                            sov = so[:].rearrange("p (h two i) -> p h two i", two=2, i=128)
                            a_, b_ = pv[:, :, 0, :], pv[:, :, 1, :]
                            cosb = rb[:, m, 0:1, :].broadcast_to([128, 2, 128])
                            sinb = rb[:, m, 1:2, :].broadcast_to([128, 2, 128])
                            t = [rtmp[k][:].rearrange("p (h i) -> p h i", i=128) for k in range(4)]
                            em.stt(t[0], a_, sc, cosb, ALU.mult, ALU.mult, [Tpz, Tr], [Trtmp[0]])
                            em.stt(t[1], b_, sc, sinb, ALU.mult, ALU.mult, [Tpz, Tr], [Trtmp[1]])
                            em.stt(t[2], a_, sc, sinb, ALU.mult, ALU.mult, [Tpz, Tr], [Trtmp[2]])
                            em.stt(t[3], b_, sc, cosb, ALU.mult, ALU.mult, [Tpz, Tr], [Trtmp[3]])
                            em.tt("dve", sov[:, :, 0, :], t[0], t[1], ALU.subtract, [Trtmp[0], Trtmp[1]], [Tso])
                            em.tt("dve", sov[:, :, 1, :], t[2], t[3], ALU.add, [Trtmp[2], Trtmp[3]], [Tso])
                        em.dma(sq, dst[r0:r0 + 128, d0:d0 + 512], so[:], Tso, R=[Tso])
            em.fence()
            em.flush()
            em.release_dsems()

        with ExitStack() as st:
            tri = sb("tri", [128, 3, 128], F32, st)
            rmask = sb("rmask", [128, 8, 128], F32, st)
            rxi = sb("rxi", [128, 8, 128], F32, st)
            rzd = sb("rzd", [128, 16], F32, st)
            bg = sb("bg", [128, 16], F32, st)
            nw = sb("nw", [128, 1024], F32, st)
            gall = sb("gall", [128, NCH, 16], F32, st)
            gt_ = sb("gt_", [128, NCH, 4], F32, st)
            gsp = sb("gsp", [128, NCH, 4], F32, st)
            gW = sb("gW", [128, NCH, 4], F32, st)
            gTHR = sb("gTHR", [128, NCH, 4], F32, st)
            gDEC = sb("gDEC", [128, NCH, 4], F32, st)
            Cm = [sb("Cm%d" % h, [128, 2, 257], F32, st) for h in range(8)]
            Cb = [sb("Cb%d" % h, [128, 2, 257], BF16, st) for h in range(8)]
            TCm = em.tiles("Cm", 8)
            TCb = em.tiles("Cb", 8)
            NB = 2
            qin = [sb("qin%d" % i, [128, D], BF16, st) for i in range(NB)]
            kin = [sb("kin%d" % i, [128, D], BF16, st) for i in range(NB)]
            vex = [sb("vex%d" % i, [128, 8, 257], BF16, st) for i in range(NB)]
            inrr = RR(list(zip(qin, kin, vex, em.tiles("qin", NB), em.tiles("kin", NB), em.tiles("vex", NB))))
            qT_ = [sb("qT%d" % i, [128, 16, 128], BF16, st) for i in range(2)]
            kT_ = [sb("kT%d" % i, [128, 16, 128], BF16, st) for i in range(2)]
            qkTrr = RR(list(zip(qT_, kT_, em.tiles("qT", 2), em.tiles("kT", 2))))
            scT = [sb("scT%d" % i, [128, 128], BF16, st) for i in range(2)]
            scrr = RR(list(zip(scT, em.tiles("scT", 2))))
            kw_ = [sb("kw%d" % i, [128, 256], BF16, st) for i in range(2)]
            kwrr = RR(list(zip(kw_, em.tiles("kw", 2))))
            qx_ = [sb("qx%d" % i, [128, 2, 128], BF16, st) for i in range(2)]
            qxrr = RR(list(zip(qx_, em.tiles("qx", 2))))
            hout = [sb("hout%d" % i, [128, D], F32, st) for i in range(2)]
            horr = RR(list(zip(hout, em.tiles("hout", 2))))
            hbl = [sb("hbl%d" % i, [128, D], F32, st) for i in range(2)]
            ogl = [sb("ogl%d" % i, [128, D], F32, st) for i in range(2)]
            hbrr = RR(list(zip(hbl, ogl, em.tiles("hbl", 2), em.tiles("ogl", 2))))
            dsm = [sb("dsm%d" % i, [128, 4], F32, st) for i in range(2)]
            dsmrr = RR(list(zip(dsm, em.tiles("dsm", 2))))
            stats = sb("stats", [128, 4, 6], F32, st)
            mv = sb("mv", [128, 4, 2], F32, st)
            rs = sb("rs", [128, 8], F32, st)
            ss = sb("ss", [128, 4], F32, st)
            junk = sb("junk", [128, 256], F32, st)
            tmpn = sb("tmpn", [128, 1024], F32, st)
            ybf = [sb("ybf%d" % i, [128, D], BF16, st) for i in range(2)]
            ybrr = RR(list(zip(ybf, em.tiles("ybf", 2))))
            yTs = [sb("yTs%d" % i, [128, 16, 128], BF16, st) for i in range(2)]
            yTrr = RR(list(zip(yTs, em.tiles("yTs", 2))))
            Tc = em.tile("consts")
            Tgall, Tgt, Tgsp, TgW = em.tile("gall"), em.tile("gt"), em.tile("gsp"), em.tile("gW")
            Tstats, Tmv, Trs, Tss, Ttmpn = (em.tiles("stats", 4), em.tiles("mv", 4), em.tiles("rs", 8),
                                            em.tiles("ss", 4), em.tiles("tmpn", 4))
            Tjunk = em.tile("junk")

            em.dma("sp", tri[:], tri_in.rearrange("a s l -> s a l"), Tc, W=[Tc])
            em.dma("sp", rmask[:], rmask_in.rearrange("a s l -> s a l"), Tc, W=[Tc])
            em.dma("sp", rxi[:].rearrange("p a l -> p (a l)"), rxi_in.partition_broadcast(128), Tc, W=[Tc])
            em.dma("sp", rzd[:], rzd_in, Tc, W=[Tc])
            em.dma("sp", bg[:], b_gate.partition_broadcast(128), Tc, W=[Tc])
            em.dma("sp", nw[:], norm_w.partition_broadcast(128), Tc, W=[Tc])
            em.dma("sp", gall[:], zgate.rearrange("(j p) g -> p j g", p=128), Tgall, W=[Tgall])
            for b in range(NB):
                em.memset("pool", vex[b][:, :, 256:257], 1.0, [inrr.items[b][5]])

            ptq = RR([(PB[0], PBT[0]), (PB[1], PBT[1])])
            TpS = [PBT[2], PBT[7]]
            pS = [PB[2][:, 0:128], PB[7][:, 0:128]]
            pn2 = [PB[2][:, 128:130], PB[7][:, 128:130]]
            pNrr = RR([(PB[3], PBT[3]), (PB[4], PBT[4])])
            pUrr = RR([(PB[5], PBT[5]), (PB[6], PBT[6])])
            pGate = PB[7]
            TpG = PBT[7]

            def load_chunk(j):
                qb, kb, vb, Tq, Tk, Tv = inrr.next()
                r0 = j * CH
                em.dma("sp", qb[:], zq[r0:r0 + CH, :], Tq, W=[Tq])
                em.dma("sp", kb[:], zk[r0:r0 + CH, :], Tk, W=[Tk])
                em.dma("sp", vb[:, :, 0:256], zv[r0:r0 + CH, :].rearrange("p (h e) -> p h e", e=256), Tv, W=[Tv])
                return qb, kb, vb, Tq, Tk, Tv

            def load_hb(j):
                hb_, og_, Thb_, Tog_ = hbrr.next()
                r0 = j * CH
                em.dma("sp", hb_[:], hb[r0:r0 + CH, :], Thb_, W=[Thb_])
                em.dma("sp", og_[:], zg[r0:r0 + CH, :], Tog_, W=[Tog_])
                return hb_, og_, Thb_, Tog_

            for sweep in range(2):
                dr = 1 - sweep
                gmf = gall[:, :, 8 + 4 * dr:12 + 4 * dr]
                gmi = gall[:, :, 4 * dr:4 * dr + 4]
                bfb = bg[:, 8 + 4 * dr:12 + 4 * dr].unsqueeze(1).broadcast_to([128, NCH, 4])
                bib = bg[:, 4 * dr:4 * dr + 4].unsqueeze(1).broadcast_to([128, NCH, 4])
                em.tt("dve", gt_[:], gmf, bfb, ALU.add, [Tgall, Tc], [Tgt])
                em.act(gsp[:], gt_[:], AF.Exp, [Tgt], [Tgsp], scale=-1.0)
                em.act(gsp[:], gsp[:], AF.Ln, [Tgsp], [Tgsp], bias=1.0)
                spf = gsp[:].rearrange("p j h -> p (j h)")
                NG = NCH * 4
                em.mm(pGate[:, 0:NG], tri[:, dr, :], spf, True, True, [Tc, Tgsp], [TpG])
                em.mm(pGate[:, 256:256 + NG], tri[:, 2, :], spf, True, True, [Tc, Tgsp], [TpG])
                pgc = pGate[:, 0:NG].rearrange("p (j h) -> p j h", h=4)
                pgt = pGate[:, 256:256 + NG].rearrange("p (j h) -> p j h", h=4)
                em.tt("dve", gt_[:], gmi, bib, ALU.add, [Tgall, Tc], [Tgt])
                em.tt("dve", gt_[:], gt_[:], pgc, ALU.add, [Tgt, TpG], [Tgt])
                em.act(gW[:], gt_[:], AF.Exp, [Tgt], [TgW])
                em.act(gTHR[:], pgc, AF.Exp, [TpG], [TgW])
                em.act(gDEC[:], pgt, AF.Exp, [TpG], [TgW], scale=-1.0)
                for h in range(8):
                    em.memset("pool", Cm[h][:], 0.0, [TCm[h]])
                    em.memset("pool", Cb[h][:], 0.0, [TCb[h]])
                order = list(range(NCH)) if dr == 0 else list(range(NCH - 1, -1, -1))

                def tr_begin():
                    return qkTrr.next()

                def tr_group(bufs, chunk, g8):
                    qTb, kTb, TqT, TkT = bufs
                    qb, kb, vb, Tq, Tk, Tv = chunk
                    if g8 < 4:
                        src, Tsrc, dstT, TdT, g4 = qb, Tq, qTb, TqT, g8
                    else:
                        src, Tsrc, dstT, TdT, g4 = kb, Tk, kTb, TkT, g8 - 4
                    pt, Tpt = ptq.next()
                    ptv = pt[:].bitcast(BF16)[:, 0:512].rearrange("p (a b) -> p a b", b=128)
                    for jj in range(4):
                        blk = g4 * 4 + jj
                        em.tr(ptv[:, jj, :], src[:, blk * 128:(blk + 1) * 128], identb[:], [Tsrc, Tid], [Tpt],
                              inc=(jj == 3))
                    em.copy("act", dstT[:, g4 * 4:(g4 + 1) * 4, :], ptv, [Tpt], [TdT])

                def make_epilogue(j, yb, Tyb):
                    def epi():
                        r0 = j * CH
                        yt_, Tyt = yTrr.next()
                        for g4 in range(4):
                            pt, Tpt = ptq.next()
                            ptv = pt[:].bitcast(BF16)[:, 0:512].rearrange("p (a b) -> p a b", b=128)
                            for jj in range(4):
                                blk = g4 * 4 + jj
                                em.tr(ptv[:, jj, :], yb[:, blk * 128:(blk + 1) * 128], identb[:], [Tyb, Tid], [Tpt],
                                      inc=(jj == 3))
                            em.copy("act", yt_[:, g4 * 4:(g4 + 1) * 4, :], ptv, [Tpt], [Tyt])
                        em.dma("act", yT.rearrange("(k p) t -> p k t", p=128)[:, :, r0:r0 + CH], yt_[:], Tyt, R=[Tyt])
                    return epi

                cur = load_chunk(order[0])
                curhb = load_hb(order[0]) if dr == 0 else None
                curT = tr_begin()
                for g8 in range(8):
                    tr_group(curT, cur, g8)
                pending_epi = None
                for oi, j in enumerate(order):
                    qb, kb, vb, Tq, Tk, Tv = cur
                    qTb, kTb, TqT, TkT = curT
                    nxt = nxthb = nxtT = None
                    if oi + 1 < NCH:
                        nxt = load_chunk(order[oi + 1])
                        nxtT = tr_begin()
                    if (dr == 0 and j == SEGCH) or (dr == 1 and j == SEGCH - 1):
                        for h in range(8):
                            em.ts("pool", Cm[h][:], Cm[h][:], carry[:, 0:1], None, ALU.mult, None,
                                  [TCm[h], Tcarry], [TCm[h]])
                            em.ts("pool", Cb[h][:], Cb[h][:], carry[:, 0:1], None, ALU.mult, None,
                                  [TCb[h], Tcarry], [TCb[h]])
                    ho, Tho = horr.next()
                    ctx = {}
                    if dr == 0:
                        hb_, og_, Thb_, Tog_ = curhb
                        yb, Tyb = ybrr.next()

                    def stA(head):
                        si = head % 2
                        for kc in range(2):
                            em.mm(pS[si], kTb[:, 2 * head + kc, :], qTb[:, 2 * head + kc, :], kc == 0, kc == 1,
                                  [TkT, TqT], [TpS[si]])

                    def stB(head):
                        is_m = head < 4
                        h = head % 4
                        ci = dr * 4 + h
                        si = head % 2
                        sc_, Tsc = scrr.next()
                        if is_m:
                            em.stt(sc_[:], pS[si], gW[:, j, h:h + 1], tri[:, dr, :], ALU.mult, ALU.mult,
                                   [TpS[si], TgW, Tc], [Tsc])
                            qsrc, Tqsrc = qTb[:, 2 * head:2 * head + 2, :], TqT
                        else:
                            em.tt("dve", sc_[:], pS[si], rmask[:, ci, :], ALU.mult, [TpS[si], Tc], [Tsc])
                            qx, Tqx = qxrr.next()
                            em.tt("pool", qx[:], qTb[:, 2 * head:2 * head + 2, :],
                                  rxi[:, ci:ci + 1, :].broadcast_to([128, 2, 128]), ALU.mult, [TqT, Tc], [Tqx])
                            qsrc, Tqsrc = qx[:], Tqx
                        kw, Tkw = kwrr.next()
                        wcol = gW[:, j, h:h + 1] if is_m else rzd[:, ci:ci + 1]
                        em.ts("pool", kw[:], kb[:, head * 256:(head + 1) * 256], wcol, None, ALU.mult, None,
                              [Tk, TgW, Tc], [Tkw])
                        if is_m:
                            em.ts("pool", Cm[head][:], Cm[head][:], gDEC[:, j, h:h + 1], None, ALU.mult, None,
                                  [TCm[head], TgW], [TCm[head]])
                        ctx[head] = (sc_, Tsc, qsrc, Tqsrc, kw, Tkw)

                    def stC(head):
                        is_m = head < 4
                        si = head % 2
                        NV = 257 if is_m else 256
                        sc_, Tsc, qsrc, Tqsrc, kw, Tkw = ctx[head]
                        pN, TpN = pNrr.next()
                        em.mm(pN[:, 0:NV], sc_[:], vb[:, head, 0:NV], True, False, [Tsc, Tv], [TpN], inc=False)
                        for kc in range(2):
                            em.mm(pN[:, 0:NV], qsrc[:, kc, :], Cb[head][:, kc, 0:NV], False, kc == 1,
                                  [Tqsrc, TCb[head]], [TpN])
                        pU, TpU = pUrr.next()
                        pUv = pU[:].rearrange("p (a b) -> p a b", b=256)
                        for kc in range(2):
                            em.mm(pUv[:, kc, :], kw[:, kc * 128:(kc + 1) * 128], vb[:, head, 0:256], True, True,
                                  [Tkw, Tv], [TpU], inc=(kc == 1))
                        pn2h = pN[:, 320:322]
                        if is_m:
                            for kc in range(2):
                                em.mm(pn2h[:, kc:kc + 1], kw[:, kc * 128:(kc + 1) * 128], vb[:, head, 256:257],
                                      False, True, [Tkw, Tv], [TpN], inc=(kc == 1))
                        ctx[head] = (pN, TpN, pUv, TpU, pn2h)

                    def stD(head):
                        is_m = head < 4
                        h = head % 4
                        ci = dr * 4 + h
                        si = head % 2
                        pN, TpN, pUv, TpU, pn2h = ctx[head]
                        if is_m:
                            ds_, Tds = dsmrr.next()
                            em.ts("dve", ds_[:, 0:1], pN[:, 256:257], -1.0, gTHR[:, j, h:h + 1], ALU.mult, ALU.max,
                                  [TpN, TgW], [Tds])
                            em.tt("dve", ds_[:, 0:1], ds_[:, 0:1], pN[:, 256:257], ALU.max, [TpN, Tds], [Tds])
                            em.op("dve", lambda e, o=ds_[:, 1:2], i_=ds_[:, 0:1]: e.reciprocal(out=o, in_=i_),
                                  [Tds], [Tds])
                            hsl = slice(head * 256, (head + 1) * 256)
                            if dr == 1:
                                em.ts("dve", ho[:, hsl], pN[:, 0:256], ds_[:, 1:2], None, ALU.mult, None,
                                      [TpN, Tds], [Tho])
                            else:
                                em.stt(ho[:, hsl], pN[:, 0:256], ds_[:, 1:2], hb_[:, hsl], ALU.mult, ALU.add,
                                       [TpN, Tds, Thb_], [Tho])
                                em.op("dve", lambda e, o=stats[:, h, :], i_=ho[:, hsl]: e.bn_stats(out=o, in_=i_),
                                      [Tho], [Tstats[h]])
                            dcol = gDEC[:, j, h:h + 1]
                            em.stt(Cm[head][:, :, 0:256], pUv, dcol, Cm[head][:, :, 0:256], ALU.mult, ALU.add,
                                   [TpU, TgW, TCm[head]], [TCm[head]])
                            em.stt(Cm[head][:, :, 256], pn2h, dcol, Cm[head][:, :, 256], ALU.mult, ALU.add,
                                   [TpN, TgW, TCm[head]], [TCm[head]])
                            em.copy("act", Cb[head][:], Cm[head][:], [TCm[head]], [TCb[head]])
                        else:
                            hsl = slice(head * 256, (head + 1) * 256)
                            if dr == 1:
                                em.copy("act", ho[:, hsl], pN[:, 0:256], [TpN], [Tho])
                            else:
                                em.tt("dve", ho[:, hsl], pN[:, 0:256], hb_[:, hsl], ALU.add, [TpN, Thb_], [Tho])
                                em.act(junk[:], ho[:, hsl], AF.Square, [Tho], [Tjunk, Tss[h]], accum=ss[:, h:h + 1])
                            dcol = rzd[:, 8 + ci:9 + ci]
                            em.stt(Cm[head][:, :, 0:256], Cm[head][:, :, 0:256], dcol, pUv, ALU.mult, ALU.add,
                                   [TpU, Tc, TCm[head]], [TCm[head]])
                            em.copy("act", Cb[head][:, :, 0:256], Cm[head][:, :, 0:256], [TCm[head]], [TCb[head]])

                    def stE1(head):
                        is_m = head < 4
                        h = head % 4
                        if is_m:
                            em.op("dve", lambda e, o=mv[:, h, :], i_=stats[:, h, :]: e.bn_aggr(out=o, in_=i_),
                                  [Tstats[h]], [Tmv[h]])
                            em.act(rs[:, h:h + 1], mv[:, h, 1:2], AF.Sqrt, [Tmv[h]], [Trs[h]], bias=NORM_EPS)
                        else:
                            em.act(rs[:, 4 + h:5 + h], ss[:, h:h + 1], AF.Sqrt, [Tss[h]], [Trs[4 + h]], bias=NORM_EPS,
                                   scale=1.0 / 256.0)

                    def stE2(head):
                        is_m = head < 4
                        h = head % 4
                        hsl = slice(head * 256, (head + 1) * 256)
                        if is_m:
                            em.op("dve", lambda e, o=rs[:, h:h + 1], i_=rs[:, h:h + 1]: e.reciprocal(out=o, in_=i_),
                                  [Trs[h]], [Trs[h]])
                            em.ts("dve", tmpn[:, hsl], ho[:, hsl], mv[:, h, 0:1], rs[:, h:h + 1], ALU.subtract, ALU.mult,
                                  [Tho, Tmv[h], Trs[h]], [Ttmpn[h]])
                        else:
                            em.op("dve", lambda e, o=rs[:, 4 + h:5 + h], i_=rs[:, 4 + h:5 + h]:
                                  e.reciprocal(out=o, in_=i_), [Trs[4 + h]], [Trs[4 + h]])
                            em.stt(yb[:, hsl], ho[:, hsl], rs[:, 4 + h:5 + h], og_[:, hsl], ALU.mult, ALU.mult,
                                   [Tho, Trs[4 + h], Tog_], [Tyb])

                    def stE3(head):
                        if head < 4:
                            hsl = slice(head * 256, (head + 1) * 256)
                            h = head
                            em.tt("pool", tmpn[:, hsl], tmpn[:, hsl], nw[:, hsl], ALU.mult, [Ttmpn[h], Tc], [Ttmpn[h]])
                            em.tt("pool", yb[:, hsl], tmpn[:, hsl], og_[:, hsl], ALU.mult, [Ttmpn[h], Tog_], [Tyb])

                    stA(0)
                    stB(0)
                    for head in range(8):
                        if head + 1 < 8:
                            stA(head + 1)
                            stB(head + 1)
                        stC(head)
                        stD(head)
                        if dr == 0:
                            stE1(head)
                            if head >= 1:
                                stE2(head - 1)
                            if head >= 2:
                                stE3(head - 2)
                        if head == 1:
                            if pending_epi is not None:
                                pending_epi()
                                pending_epi = None
                            if dr == 0 and oi + 1 < NCH:
                                nxthb = load_hb(order[oi + 1])
                        if nxt is not None:
                            tr_group(nxtT, nxt, head)
                    r0 = j * CH
                    if dr == 1:
                        em.dma("act", hb[r0:r0 + CH, :], ho[:], Tho, R=[Tho])
                    else:
                        stE2(7)
                        stE3(6)
                        stE3(7)
                        pending_epi = make_epilogue(j, yb, Tyb)
                    cur, curhb, curT = nxt, nxthb, nxtT
                if pending_epi is not None:
                    pending_epi()
                    pending_epi = None
                em.fence()
            em.flush()
            em.release_dsems()

        def layernorm_rows(r, Tr, out_ap, Tout, gam, bet, Tgb, stats6, mv2, rs1, Tst, eps):
            for q4 in range(4):
                em.op("dve", lambda e, o=stats6[:, q4, :], i_=r[:, q4 * 512:(q4 + 1) * 512]: e.bn_stats(out=o, in_=i_),
                      [Tr], [Tst])
            em.op("dve", lambda e, o=mv2[:], i_=stats6[:].rearrange("p a b -> p (a b)"): e.bn_aggr(out=o, in_=i_),
                  [Tst], [Tst])
            em.act(rs1[:], mv2[:, 1:2], AF.Sqrt, [Tst], [Tst], bias=eps)
            em.op("dve", lambda e, o=rs1[:], i_=rs1[:]: e.reciprocal(out=o, in_=i_), [Tst], [Tst])
            em.ts("dve", mv2[:, 1:2], mv2[:, 0:1], -1.0, rs1[:, 0:1], ALU.mult, ALU.mult, [Tst], [Tst])
            em.act(r, r, AF.Identity, [Tr, Tst], [Tr], bias=mv2[:, 1:2], scale=rs1[:, 0:1])
            em.tt("dve", r, r, gam, ALU.mult, [Tr, Tgb], [Tr])
            em.tt("pool", out_ap, r, bet, ALU.add, [Tr, Tgb], [Tout])

        with ExitStack() as st:
            wo = sb("wo", [128, KC, D], BF16, st)
            Two = em.tile("wo")
            yti = [sb("yti%d" % i, [128, KC, 512], BF16, st) for i in range(2)]
            ytrr = RR(list(zip(yti, em.tiles("yti", 2))))
            xin = [sb("xin%d" % i, [128, D], F32, st) for i in range(2)]
            xinrr = RR(list(zip(xin, em.tiles("xin", 2))))
            g1t = sb("g1t", [128, D], F32, st)
            Tg1 = em.tile("g1t")
            lng = sb("lng", [128, D], F32, st)
            lnb = sb("lnb", [128, D], F32, st)
            Tln = em.tile("ln")
            rr_ = [sb("r%d" % i, [128, D], F32, st) for i in range(2)]
            rrr = RR(list(zip(rr_, em.tiles("r", 2))))
            x1o = [sb("x1o%d" % i, [128, D], F32, st) for i in range(2)]
            x1rr = RR(list(zip(x1o, em.tiles("x1o", 2))))
            h2s = [sb("h2s%d" % i, [128, KC, 128], BF16, st) for i in range(2)]
            h2rr = RR(list(zip(h2s, em.tiles("h2s", 2))))
            stats6 = sb("stats6", [128, 4, 6], F32, st)
            mv2 = sb("mv2", [128, 2], F32, st)
            rs1 = sb("rs1", [128, 1], F32, st)
            Tst = em.tile("st")
            wov = w_out.rearrange("(k p) n -> p k n", p=128)
            for q4 in range(4):
                em.dma("pool", wo[:, :, q4 * 512:(q4 + 1) * 512], wov[:, :, q4 * 512:(q4 + 1) * 512], Two, W=[Two])
            em.dma("sp", lng[:], ln1g.partition_broadcast(128), Tln, W=[Tln])
            em.dma("sp", lnb[:], ln1b.partition_broadcast(128), Tln, W=[Tln])
            pmix = [(PB[i], PBT[i]) for i in range(4)]
            ptrr = RR([(PB[4], PBT[4]), (PB[5], PBT[5]), (PB[6], PBT[6]), (PB[7], PBT[7])])
            yTv = yT.rearrange("(k p) t -> p k t", p=128)
            h2Tv = h2T.rearrange("(k p) t -> p k t", p=128)
            pendX = None
            for i in range(NTILE):
                seg = (i * 512) // SEG
                if i == 0 or (i * 512) % SEG == 0:
                    em.dma("sp", g1t[:], gbc[0, seg], Tg1, W=[Tg1])
                yt_, Tyt = ytrr.next()
                em.dma("sp", yt_[:], yTv[:, :, i * 512:(i + 1) * 512], Tyt, W=[Tyt])
                for m in range(4):
                    r0 = i * 512 + m * 128
                    xi_, Txi = xinrr.next()
                    em.dma("sp", xi_[:], x[r0:r0 + 128, :], Txi, W=[Txi])
                    r, Tr = rrr.next()
                    for n4 in range(4):
                        pm, Tpm = pmix[n4]
                        for kc in range(KC):
                            em.mm(pm[:], yt_[:, kc, m * 128:(m + 1) * 128], wo[:, kc, n4 * 512:(n4 + 1) * 512],
                                  kc == 0, kc == KC - 1, [Tyt, Two], [Tpm])
                        sl = slice(n4 * 512, (n4 + 1) * 512)
                        em.tt("dve", r[:, sl], pm[:], g1t[:, sl], ALU.mult, [Tpm, Tg1], [Tr])
                    if pendX is not None:
                        pendX()
                        pendX = None
                    em.stt(r[:], xi_[:], ALPHA, r[:], ALU.mult, ALU.add, [Txi, Tr], [Tr])
                    xo, Txo = x1rr.next()
                    layernorm_rows(r[:], Tr, xo[:], Txo, lng[:], lnb[:], Tln, stats6, mv2, rs1, Tst, LN_EPS)

                    def postX(xo=xo, Txo=Txo, seg=seg, r0=r0):
                        em.dma("act", x1[r0:r0 + 128, :], xo[:], Txo, R=[Txo])
                        h2, Th2 = h2rr.next()
                        modulate_T(xo[:], Txo, h2, Th2, 0, 2, 3, seg, ptrr)
                        em.dma("act", h2Tv[:, :, r0:r0 + 128], h2[:], Th2, R=[Th2])
                    pendX = postX
            if pendX is not None:
                pendX()
            em.fence()
            em.flush()
            em.release_dsems()

        em.bg_tiles = []
        with ExitStack() as st:
            AW = 514
            h2i = sb("h2i", [128, KC, 512], BF16, st)
            Th2i = em.tile("h2i")
            act_ = sb("act_", [128, NFC, AW], BF16, st)
            Tact = em.tile("act")
            wua = [sb("wua%d" % i, [128, KC, 256], BF16, st) for i in range(2)]
            wug = [sb("wug%d" % i, [128, KC, 256], BF16, st) for i in range(2)]
            wurr = RR(list(zip(wua, wug, em.tiles("wua", 2), em.tiles("wug", 2))))
            wd = [sb("wd%d" % i, [128, 11, 512], BF16, st) for i in range(2)]
            wdrr = RR(list(zip(wd, em.tiles("wd", 2))))
            cf = sb("cf", [128, 2 * NFC, 4], F32, st)
            ncf = sb("ncf", [128, 2 * NFC, 2], F32, st)
            cm1 = sb("cm1", [128, 1], F32, st)
            Tcf = em.tile("cf")
            cwa = [sb("cwa%d" % i, [128, 516], F32, st) for i in range(3)]
            cwg = [sb("cwg%d" % i, [128, 516], F32, st) for i in range(3)]
            cwrr = RR(list(zip(cwa, cwg, em.tiles("cwa", 3), em.tiles("cwg", 3))))
            oa = [sb("oa%d" % i, [128, AW], F32, st) for i in range(3)]
            og2 = [sb("og2%d" % i, [128, AW], F32, st) for i in range(3)]
            oarr = RR(list(zip(oa, og2, em.tiles("oa", 3), em.tiles("og2", 3))))
            sav = sb("sav", [128, 2 * NFC, 2], F32, st)
            Tsav = em.tile("sav")
            g2t = sb("g2t", [128, D], F32, st)
            Tg2 = em.tile("g2t")
            g2r = sb("g2r", [128, 512], F32, st)
            Tg2r = em.tile("g2r")
            etmp = [sb("etmp%d" % i, [128, 512], F32, st) for i in range(2)]
            etrr = RR(list(zip(etmp, em.tiles("etmp", 2))))
            lng = sb("lng2", [128, D], F32, st)
            lnb = sb("lnb2", [128, D], F32, st)
            Tln = em.tile("ln2")
            racc = [sb("racc%d" % i, [128, D], F32, st) for i in range(4)]
            Tracc = em.tiles("racc", 4)
            stats6 = sb("stats6b", [128, 4, 6], F32, st)
            mv2 = sb("mv2b", [128, 2], F32, st)
            rs1 = sb("rs1b", [128, 1], F32, st)
            Tst = em.tile("stb")
            em.dma("sp", cf[:], conv_fm, Tcf, W=[Tcf])
            em.dma("sp", lng[:], ln2g.partition_broadcast(128), Tln, W=[Tln])
            em.dma("sp", lnb[:], ln2b.partition_broadcast(128), Tln, W=[Tln])
            em.ts("dve", cm1[:], carry[:], -1.0, None, ALU.add, None, [Tcarry], [Tcf])
            em.ts("dve", ncf[:, :, 0], cf[:, :, 0], cm1[:, 0:1], None, ALU.mult, None, [Tcf], [Tcf])
            em.ts("dve", ncf[:, :, 1], cf[:, :, 2], cm1[:, 0:1], None, ALU.mult, None, [Tcf], [Tcf])
            em.memset("pool", sav[:], 0.0, [Tsav])
            for wi in range(4):
                em.memset("pool", racc[wi][:], 0.0, [Tracc[wi]])
            for b in range(3):
                em.memset("pool", cwa[b][:], 0.0, [cwrr.items[b][2]])
                em.memset("pool", cwg[b][:], 0.0, [cwrr.items[b][3]])
            purr = RR([((PB[0], PBT[0]), (PB[1], PBT[1])), ((PB[2], PBT[2]), (PB[3], PBT[3]))])
            pdn = [(PB[4 + i], PBT[4 + i]) for i in range(4)]
            h2Tv = h2T.rearrange("(k p) t -> p k t", p=128)
            wuv2 = w_up.rearrange("(k p) n -> p k n", p=128)
            wdv2 = w_down.rearrange("(c p) n -> p c n", p=128)
            em.dma("sp", h2i[:], h2Tv[:, :, 0:512], Th2i, W=[Th2i])
            pending = []
            for i in range(NTILE):
                t0 = i * 512
                seg = t0 // SEG
                last = (i == NTILE - 1)
                boundary = (i > 0 and t0 % SEG == 0)
                Wd = 513 if last else 512
                if i == 0 or t0 % SEG == 0:
                    em.dma("sp", g2t[:], gbc[1, seg], Tg2, W=[Tg2])
                tail = None
                for cp in range(NFC // 2):
                    if pending and cp >= 2 and cp % 2 == 0:
                        pending.pop(0)()
                    wa, wg, Twa, Twg = wurr.next()
                    if NOCONV_UP:
                        em.dma("pool", wa[:], wuv2[:, :, cp * 256:(cp + 1) * 256], Twa, W=[Twa])
                        em.dma("pool", wg[:], wuv2[:, :, DFF + cp * 256:DFF + (cp + 1) * 256], Twg, W=[Twg])
                    else:
                        em.dma("sp", wa[:], wbup[cp, 0], Twa, R=[Twup], W=[Twa])
                        em.dma("sp", wg[:], wbup[cp, 1], Twg, R=[Twup], W=[Twg])
                    for c2 in range(2):
                        c = cp * 2 + c2
                        (pa, Tpa), (pg, Tpg) = purr.next()
                        for kc in range(KC):
                            em.mm(pa[:], wa[:, kc, c2 * 128:(c2 + 1) * 128], h2i[:, kc, :], kc == 0, kc == KC - 1,
                                  [Twa, Th2i], [Tpa])
                        for kc in range(KC):
                            em.mm(pg[:], wg[:, kc, c2 * 128:(c2 + 1) * 128], h2i[:, kc, :], kc == 0, kc == KC - 1,
                                  [Twg, Th2i], [Tpg])
                        ca, cg, Tca, Tcg = cwrr.next()
                        o_a, o_g, Toa, Tog = oarr.next()
                        for half, (ps_, Tps, cw, Tcw, oo, Too) in enumerate(
                                ((pa, Tpa, ca, Tca, o_a, Toa), (pg, Tpg, cg, Tcg, o_g, Tog))):
                            ch = c + half * NFC
                            em.copy("act", cw[:, 2:514], ps_[:], [Tps], [Tcw])
                            em.copy("act", cw[:, 0:2], sav[:, ch, :], [Tsav], [Tcw])
                            em.copy("act", sav[:, ch, :], cw[:, 512:514], [Tcw], [Tsav])
                            em.ts("pool", oo[:, 0:Wd], cw[:, 1:1 + Wd], cf[:, ch, 1:2], cf[:, ch, 3:4], ALU.mult, ALU.add,
                                  [Tcw, Tcf], [Too])
                            em.stt(oo[:, 0:Wd], cw[:, 0:Wd], cf[:, ch, 0:1], oo[:, 0:Wd], ALU.mult, ALU.add,
                                   [Tcw, Tcf, Too], [Too])
                            em.stt(oo[:, 0:Wd], cw[:, 2:2 + Wd], cf[:, ch, 2:3], oo[:, 0:Wd], ALU.mult, ALU.add,
                                   [Tcw, Tcf, Too], [Too])
                            if boundary:
                                em.stt(oo[:, 0:1], cw[:, 2:3], ncf[:, ch, 1:2], oo[:, 0:1], ALU.mult, ALU.add,
                                       [Tcw, Tcf, Too], [Too])
                                em.stt(oo[:, 1:2], cw[:, 1:2], ncf[:, ch, 0:1], oo[:, 1:2], ALU.mult, ALU.add,
                                       [Tcw, Tcf, Too], [Too])
                        if tail is not None:
                            tail()

                        def tail(o_a=o_a, o_g=o_g, Toa=Toa, Tog=Tog, c=c, Wd=Wd):
                            em.act(o_a[:, 0:Wd], o_a[:, 0:Wd], AF.Silu, [Toa], [Toa])
                            em.tt("pool", act_[:, c, 0:Wd], o_a[:, 0:Wd], o_g[:, 0:Wd], ALU.mult, [Toa, Tog], [Tact])
                if tail is not None:
                    tail()
                    tail = None
                if not last:
                    em.dma("act", h2i[:], h2Tv[:, :, t0 + 512:t0 + 1024], Th2i, W=[Th2i])
                wins = []
                for m in range(4):
                    wins.append((m * 128, t0 - 1 + m * 128, 1 if (i == 0 and m == 0) else 0))
                groups = [wins]
                if last:
                    groups.append([(385, t0 + 384, 0)])
                for gi, grp in enumerate(groups):
                    while pending:
                        pending.pop(0)()
                    for wi, (col0, tok0, p0) in enumerate(grp):
                        em.dma("sp", racc[wi][p0:128, :], x1[tok0 + p0:tok0 + 128, :], Tracc[wi], W=[Tracc[wi]])
                    for n4 in range(4):
                        sl = slice(n4 * 512, (n4 + 1) * 512)
                        if boundary:
                            em.dma("sp", g2r[0:1, :], gbc[1, seg - 1, 0:1, sl], Tg2r, W=[Tg2r])
                        for ks in range(4):
                            wdt, Twd = wdrr.next()
                            if NOCONV_UP:
                                em.dma("pool", wdt[:], wdv2[:, ks * 11:(ks + 1) * 11, n4 * 512:(n4 + 1) * 512], Twd, W=[Twd])
                            else:
                                em.dma("sp", wdt[:], wbdn[n4, ks], Twd, R=[Twdn], W=[Twd])
                            for wi, (col0, tok0, p0) in enumerate(grp):
                                pd, Tpd = pdn[wi]
                                for cc in range(11):
                                    em.mm(pd[:], act_[:, ks * 11 + cc, col0:col0 + 128], wdt[:, cc, :],
                                          ks == 0 and cc == 0, ks == 3 and cc == 10, [Tact, Twd], [Tpd],
                                          inc=(cc == 10))
                        for wi in range(len(grp)):
                            pd, Tpd = pdn[wi]
                            tm, Ttm = etrr.next()
                            em.tt("dve", tm[:], pd[:], g2t[:, sl], ALU.mult, [Tpd, Tg2], [Ttm])
                            if boundary and grp[wi][0] == 0:
                                em.tt("dve", tm[0:1, :], pd[0:1, :], g2r[0:1, :], ALU.mult, [Tpd, Tg2r], [Ttm])
                            em.stt(racc[wi][:, sl], racc[wi][:, sl], ALPHA, tm[:], ALU.mult, ALU.add,
                                   [Tracc[wi], Ttm], [Tracc[wi]])

                    def make_ep(wi, tok0, p0):
                        def ep():
                            r, Tr = racc[wi], Tracc[wi]
                            layernorm_rows(r[:], Tr, r[:], Tr, lng[:], lnb[:], Tln, stats6, mv2, rs1, Tst, LN_EPS)
                        return ep

                    def make_st(wi, tok0, p0):
                        def stf():
                            em.dma("act", y[tok0 + p0:tok0 + 128, :], racc[wi][p0:128, :], Tracc[wi], R=[Tracc[wi]])
                        return stf
                    pending = []
                    prev_st = None
                    for wi, (col0, tok0, p0) in enumerate(grp):
                        ep_, st_ = make_ep(wi, tok0, p0), make_st(wi, tok0, p0)
                        if prev_st is None:
                            pending.append(ep_)
                        else:
                            pending.append(lambda a=ep_, b=prev_st: (a(), b()))
                        prev_st = st_
                    pending.append(prev_st)
            while pending:
                pending.pop(0)()
            em.fence()
            em.flush()
    nc._n_ops = (em.n_ops, em.n_wait, dict(em.cnt))
    return nc


def _consts():
    pos = np.arange(128, dtype=np.float32)
    s = pos[:, None]
    l = pos[None, :]
    tri = np.stack([(s <= l), (s >= l), np.ones((128, 128), bool)]).astype(np.float32)
    hd = np.arange(4, dtype=np.float32)
    lg_f = np.log1p(-np.exp2(-5.0 - hd)).astype(np.float32)
    lg_b = np.log1p(-np.exp2(-5.5 - hd)).astype(np.float32)
    rmask = np.zeros((8, 128, 128), np.float32)
    rxi = np.zeros((8, 128), np.float32)
    rzd = np.zeros((128, 16), np.float32)
    for h in range(4):
        rmask[h] = np.where(s <= l, np.exp(lg_f[h] * np.maximum(l - s, 0.0)), 0.0)
        rxi[h] = np.exp(lg_f[h] * (pos + 1.0))
        rzd[:, h] = np.exp(lg_f[h] * (127.0 - pos))
        rzd[:, 8 + h] = np.exp(lg_f[h] * 128.0)
        rmask[4 + h] = np.where(s >= l, np.exp(lg_b[h] * np.maximum(s - l, 0.0)), 0.0)
        rxi[4 + h] = np.exp(lg_b[h] * (128.0 - pos))
        rzd[:, 4 + h] = np.exp(lg_b[h] * pos)
        rzd[:, 12 + h] = np.exp(lg_b[h] * 128.0)
    return tri, rmask.astype(np.float32), rxi.reshape(-1).astype(np.float32), rzd.astype(np.float32)


def _rot_table(npos):
    inv = (1.0 / (np.float32(10000.0) ** np.linspace(0.0, 1.0, 128, dtype=np.float32))).astype(np.float32)
    ang = (np.arange(npos, dtype=np.float32)[:, None] * inv[None, :]).astype(np.float32)
    return np.stack([np.cos(ang), np.sin(ang)], axis=1).astype(np.float32)


def prepare_shared(w_ada, b_ada, w_in, b_igate, b_fgate, mlstm_norm_w, w_out, ln1_g, ln1_b, w_up, conv_w, conv_b,
                   w_down, ln2_g, ln2_b):
    f = lambda a: np.ascontiguousarray(np.asarray(a, dtype=np.float32))
    w_in0 = f(w_in)[0]
    perm = np.arange(N_IN)
    for base in (4112, 5136):
        for h in range(4):
            b0 = base + h * 256
            perm[b0:b0 + 128] = b0 + 2 * np.arange(128)
            perm[b0 + 128:b0 + 256] = b0 + 2 * np.arange(128) + 1
    w_in_p = np.ascontiguousarray(w_in0[:, perm])
    cw = f(conv_w)[0]
    cb = f(conv_b)[0]
    conv4 = np.concatenate([cw, cb[None, :]], axis=0)
    conv_fm = np.ascontiguousarray(conv4.T.reshape(2 * NFC, 128, 4).transpose(1, 0, 2))
    tri, rmask, rxi, rzd = _consts()
    b_ada0 = f(b_ada)[0]
    return {
        "w_ada": f(w_ada)[0], "b_ada": b_ada0,
        "b_ada_fm": np.ascontiguousarray(b_ada0.reshape(96, 128).T),
        "w_in": w_in_p, "b_gate": np.concatenate([f(b_igate)[0], f(b_fgate)[0]]),
        "norm_w": f(mlstm_norm_w)[0], "w_out": f(w_out)[0], "ln1_g": f(ln1_g)[0], "ln1_b": f(ln1_b)[0],
        "w_up": f(w_up)[0], "conv_fm": conv_fm, "w_down": f(w_down)[0], "ln2_g": f(ln2_g)[0], "ln2_b": f(ln2_b)[0],
        "tri": tri, "rmask": rmask, "rxi": rxi, "rzd": rzd,
    }


def core_inputs(xc, c2, carry, rot):
    return {
        "x": np.ascontiguousarray(xc, dtype=np.float32),
        "c_lay": np.ascontiguousarray(np.asarray(c2, np.float32).reshape(2, KC, 128).transpose(2, 1, 0)),
        "carry": np.full((128, 1), carry, np.float32),
        "rot": np.ascontiguousarray(rot),
    }


_NC_CACHE = {}


def kernel(x_prompt, x_sample, c_prompt, c_sample, w_ada, b_ada, w_in, b_igate, b_fgate, mlstm_norm_w, w_out,
           ln1_g, ln1_b, w_up, conv_w, conv_b, w_down, ln2_g, ln2_b):
    SEG = 2048
    x_prompt = np.asarray(x_prompt, np.float32)
    x_sample = np.asarray(x_sample, np.float32)
    c_prompt = np.asarray(c_prompt, np.float32)
    c_sample = np.asarray(c_sample, np.float32)
    shared = prepare_shared(w_ada, b_ada, w_in, b_igate, b_fgate, mlstm_norm_w, w_out, ln1_g, ln1_b, w_up, conv_w,
                            conv_b, w_down, ln2_g, ln2_b)
    rot_full = _rot_table(2 * SEG)
    rot_p = rot_full
    rot_s = np.concatenate([rot_full[:SEG], rot_full[:SEG]], axis=0)
    in_maps = []
    for i in range(4):
        m = core_inputs(x_prompt[i], np.stack([c_prompt[i], c_prompt[i]]), 1.0, rot_p)
        m.update(shared)
        in_maps.append(m)
    for i in range(4):
        m = core_inputs(x_sample[2 * i:2 * i + 2].reshape(2 * SEG, D), c_sample[2 * i:2 * i + 2], 0.0, rot_s)
        m.update(shared)
        in_maps.append(m)
    if SEG not in _NC_CACHE:
        _NC_CACHE[SEG] = build_program(SEG)
    nc = _NC_CACHE[SEG]
    res = run_bass_kernel_spmd(nc, in_maps, core_ids=list(range(8)))
    outs = [np.asarray(r["y"], np.float32) for r in res.results]
    y_prompt = np.stack(outs[:4], axis=0)
    y_sample = np.concatenate([o.reshape(2, SEG, D) for o in outs[4:]], axis=0)
    return (y_prompt, y_sample)
```

```python
import numpy as np
from contextlib import ExitStack
import concourse.bass as bass
import concourse.mybir as mybir
from concourse.bass_utils import run_bass_kernel_spmd

F32 = mybir.dt.float32
BF16 = mybir.dt.bfloat16
ALU = mybir.AluOpType
AF = mybir.ActivationFunctionType

D = 2048
KC = 16
DFF = 5632
NFC = 44
N_IN = 8208
CH = 128
ALPHA = 2.0 ** 0.25
LN_EPS = 1e-5
NORM_EPS = 1e-6

ENGS = ("pe", "act", "dve", "pool", "sp")
SAME_ENGINE_SYNC = True
import os
NOCONV = os.environ.get("MK_NOCONV", "0") == "1"
NOCONV_IN = NOCONV or os.environ.get("MK_NOCONV_IN", "0") == "1"
NOCONV_UP = NOCONV or os.environ.get("MK_NOCONV_UP", "0") == "1"
PROD_ACT = os.environ.get("MK_PROD_ACT", "1") == "1"
PROD_POOL = os.environ.get("MK_PROD_POOL", "1") == "1"


class T:
    __slots__ = ("name", "lastw", "reads", "dsem")

    def __init__(self, name):
        self.name = name
        self.lastw = None
        self.reads = {}
        self.dsem = None


class Emitter:
    def __init__(self, nc, es):
        self.nc = nc
        self.es = es
        self.sems = {}
        for e in ("pe", "act", "dve", "pool"):
            self.sems[e] = es.enter_context(nc.semaphore("sem_" + e))
        self.cnt = {e: 0 for e in ENGS}
        self.seen = {e: {} for e in ENGS}
        self.q = {e: [] for e in ENGS}
        self.dma_sems = {}
        self.free_dsems = []
        self.bound = []
        self.bg_tiles = []
        self.ntile = 0
        self.n_wait = 0
        self.n_ops = 0

    def tile(self, name=None):
        self.ntile += 1
        return T((name or "t") + "_%d" % self.ntile)

    def tiles(self, name, n):
        return [self.tile(name) for _ in range(n)]

    def dsem_for(self, t):
        if t.dsem is None:
            if self.free_dsems:
                key = self.free_dsems.pop()
            else:
                key = "d%d" % len(self.dma_sems)
                self.sems[key] = self.es.enter_context(self.nc.semaphore(key))
                self.dma_sems[key] = 0
            t.dsem = key
            self.bound.append(t)
        return t.dsem

    def release_dsems(self, keep=()):
        nb = []
        for t in self.bound:
            if t in keep or t in self.bg_tiles:
                nb.append(t)
            else:
                self.free_dsems.append(t.dsem)
                t.dsem = None
        self.bound = nb

    def _need(self, eng, ev, waits):
        if ev is None:
            return
        k, v = ev
        if k == eng and (eng == "pe" or not SAME_ENGINE_SYNC):
            return
        if self.seen[eng].get(k, 0) >= v:
            return
        self.seen[eng][k] = v
        waits.append((k, v))

    def _deps(self, eng, reads, writes):
        waits = []
        for t in reads:
            self._need(eng, t.lastw, waits)
        for t in writes:
            self._need(eng, t.lastw, waits)
            for k, v in t.reads.items():
                if k == eng:
                    continue
                self._need(eng, (k, v), waits)
        return waits

    def _mark(self, ev, reads, writes):
        k, v = ev
        for t in reads:
            if t.reads.get(k, 0) < v:
                t.reads[k] = v
        for t in writes:
            t.lastw = ev
            t.reads = {}

    def op(self, eng, fn, R=(), W=(), inc=True):
        waits = self._deps(eng, R, W)
        if inc:
            self.cnt[eng] += 1
            ev = (eng, self.cnt[eng])
        else:
            ev = (eng, self.cnt[eng] + 1)
        self._mark(ev, R, W)
        self.q[eng].append((waits, fn, (eng, 1) if inc else None))
        self.n_wait += len(waits)
        self.n_ops += 1

    def dma(self, queue, out, in_, sem_tile, R=(), W=(), cast=False):
        if queue == "act" and not PROD_ACT:
            queue = "sp"
        if queue == "pool" and not PROD_POOL and out.dtype == in_.dtype:
            queue = "sp"
        waits = self._deps(queue, R, W)
        key = self.dsem_for(sem_tile)
        self.dma_sems[key] += 16
        ev = (key, self.dma_sems[key])
        self._mark(ev, R, W)
        self.q[queue].append((waits, lambda e: e.dma_start(out=out, in_=in_), (key, 16)))
        self.n_wait += len(waits)
        self.n_ops += 1

    def fence(self):
        bgk = set(t.dsem for t in self.bg_tiles if t.dsem is not None)
        evs = [(e, self.cnt[e]) for e in ("pe", "act", "dve", "pool") if self.cnt[e] > 0]
        evs += [(k, v) for k, v in self.dma_sems.items() if v > 0 and k not in bgk]
        for eng in ENGS:
            waits = []
            for k, v in evs:
                if k == eng or self.seen[eng].get(k, 0) >= v:
                    continue
                self.seen[eng][k] = v
                waits.append((k, v))
            if waits:
                self.q[eng].append((waits, None, None))
                self.n_wait += len(waits)

    def flush(self):
        nc, sems, q = self.nc, self.sems, self.q
        if os.environ.get("MK_VERBOSE"):
            print("flush: sbuf_remaining", nc.sbuf_bytes_remaining, "ops", self.n_ops, "sems", len(self.sems), flush=True)

        def run(name):
            def body(eng):
                for waits, fn, inc in q[name]:
                    for k, v in waits:
                        eng.wait_ge(sems[k], v)
                    if fn is not None:
                        ins = fn(eng)
                        if inc is not None:
                            ins.then_inc(sems[inc[0]], inc[1])
            return body

        with nc.Block(no_gpsimd_drain=True) as block:
            block.tensor(run("pe"))
            block.scalar(run("act"))
            block.vector(run("dve"))
            block.gpsimd(run("pool"))
            block.sync(run("sp"))
        self.q = {e: [] for e in ENGS}

    def mm(self, out, lhsT, rhs, start, stop, R, W, inc=None):
        self.op("pe", lambda e: e.matmul(out, lhsT=lhsT, rhs=rhs, start=start, stop=stop), R, W,
                inc=stop if inc is None else inc)

    def tr(self, out, in_, ident, R, W, inc=True):
        self.op("pe", lambda e: e.transpose(out, in_, ident), R, W, inc=inc)

    def act(self, out, in_, func, R, W, bias=None, scale=None, accum=None):
        kw = {}
        if bias is not None:
            kw["bias"] = bias
        if scale is not None:
            kw["scale"] = scale
        if accum is not None:
            kw["accum_out"] = accum
        self.op("act", lambda e: e.activation(out=out, in_=in_, func=func, **kw), R, W)

    def tt(self, eng, out, in0, in1, op, R, W):
        self.op(eng, lambda e: e.tensor_tensor(out=out, in0=in0, in1=in1, op=op), R, W)

    def ts(self, eng, out, in0, s1, s2, op0, op1, R, W):
        if s2 is None and eng == "pool":
            s2, op1 = 1.0, ALU.mult
        if s2 is None:
            self.op(eng, lambda e: e.tensor_scalar(out=out, in0=in0, scalar1=s1, scalar2=None, op0=op0), R, W)
        else:
            self.op(eng, lambda e: e.tensor_scalar(out=out, in0=in0, scalar1=s1, scalar2=s2, op0=op0, op1=op1), R, W)

    def stt(self, out, in0, scalar, in1, op0, op1, R, W):
        self.op("dve", lambda e: e.scalar_tensor_tensor(out=out, in0=in0, scalar=scalar, in1=in1, op0=op0, op1=op1),
                R, W)

    def copy(self, eng, out, in_, R, W):
        if eng == "act":
            self.op("act", lambda e: e.copy(out=out, in_=in_), R, W)
        else:
            self.op(eng, lambda e: e.tensor_copy(out=out, in_=in_), R, W)

    def memset(self, eng, ap, val, W):
        self.op(eng, lambda e: e.memset(ap, val), (), W)


class RR:
    def __init__(self, items):
        self.items = items
        self.i = 0

    def next(self):
        it = self.items[self.i % len(self.items)]
        self.i += 1
        return it


def build_program(SEG, debug=False):
    NT = 2 * SEG
    NTILE = NT // 512
    NCH = NT // CH
    SEGCH = SEG // CH
    nc = bass.Bass("TRN2", target_bir_lowering=False)

    def din(name, shape, dt=F32):
        return nc.dram_tensor(name, list(shape), dt, kind="ExternalInput").ap()

    def dscr(name, shape, dt):
        return nc.dram_tensor(name, list(shape), dt, kind="ExternalOutput" if debug else "Internal").ap()

    x = din("x", [NT, D])
    c_lay = din("c_lay", [128, KC, 2])
    carry_in = din("carry", [128, 1])
    rot = din("rot", [NT, 2, 128])
    w_ada = din("w_ada", [D, 6 * D])
    b_ada_fm = din("b_ada_fm", [128, 96])
    b_ada = din("b_ada", [6 * D])
    w_in = din("w_in", [D, N_IN])
    b_gate = din("b_gate", [16])
    norm_w = din("norm_w", [1024])
    w_out = din("w_out", [D, D])
    ln1g = din("ln1_g", [D])
    ln1b = din("ln1_b", [D])
    w_up = din("w_up", [D, 2 * DFF])
    conv_fm = din("conv_fm", [128, 2 * NFC, 4])
    w_down = din("w_down", [DFF, D])
    ln2g = din("ln2_g", [D])
    ln2b = din("ln2_b", [D])
    tri_in = din("tri", [3, 128, 128])
    rmask_in = din("rmask", [8, 128, 128])
    rxi_in = din("rxi", [8 * 128])
    rzd_in = din("rzd", [128, 16])
    y = nc.dram_tensor("y", [NT, D], F32, kind="ExternalOutput").ap()

    zq = dscr("zq", [NT, D], BF16)
    zk = dscr("zk", [NT, D], BF16)
    zv = dscr("zv", [NT, D], BF16)
    zg = dscr("zg", [NT, D], F32)
    zgate = dscr("zgate", [NT, 16], F32)
    gbc = dscr("gbc", [2, 2, 128, D], F32)
    hb = dscr("hb", [NT, D], F32)
    yT = dscr("yT", [D, NT], BF16)
    x1 = dscr("x1", [NT, D], F32)
    h2T = dscr("h2T", [D, NT], BF16)
    wbin = nc.dram_tensor("wbin", [16, 128, KC, 512], BF16, kind="Internal").ap()
    wbup = nc.dram_tensor("wbup", [NFC // 2, 2, 128, KC, 256], BF16, kind="Internal").ap()
    wbdn = nc.dram_tensor("wbdn", [4, 4, 128, 11, 512], BF16, kind="Internal").ap()

    with ExitStack() as es:
        em = Emitter(nc, es)

        sbn = [0]

        def sb(name, shape, dt, st=None):
            sbn[0] += 1
            return (st or es).enter_context(nc.sbuf_tensor("s%d_%s" % (sbn[0], name), list(shape), dt))

        PB = [es.enter_context(nc.psum_tensor("pb%d" % i, [128, 512], F32)) for i in range(8)]
        PBT = [em.tile("pb%d" % i) for i in range(8)]

        cv = sb("cv", [128, 4, KC, 2], F32)
        Tcv = em.tile("cv")
        identf = sb("identf", [128, 128], F32)
        identb = sb("identb", [128, 128], BF16)
        Tid = em.tile("ident")
        carry = sb("carry_sb", [128, 1], F32)
        Tcarry = em.tile("carry")
        em.memset("pool", identf[:], 0.0, [Tid])
        em.op("pool", lambda e: e.affine_select(out=identf[:], in_=identf[:], compare_op=ALU.not_equal, fill=1.0,
                                                base=0, pattern=[[-1, 128]], channel_multiplier=1), [Tid], [Tid])
        em.copy("pool", identb[:], identf[:], [Tid], [Tid])
        em.dma("sp", carry[:], carry_in, Tcarry, W=[Tcarry])

        Twin = em.tiles("wbin", 16)
        Twup = em.tile("wbup")
        Twdn = em.tile("wbdn")
        wiv = w_in.rearrange("(k p) n -> p k n", p=128)
        wuv = w_up.rearrange("(k p) n -> p k n", p=128)
        wdv = w_down.rearrange("(c p) n -> p c n", p=128)
        blocks = []
        for cb in range(16):
            c0 = cb * 512 if cb < 8 else 4112 + (cb - 8) * 512
            kind = ["qs", "cast", "cast", "sig", "rotq", "rotk", "cast", "silu"][cb // 2]
            dst = [zq, zk, zv, zg, zq, zk, zv, zg][cb // 2]
            d0 = (cb % 2) * 512 + (1024 if cb >= 8 else 0)
            blocks.append((c0, kind, dst, d0))

        def convert_in():
            for cb in range(16):
                em.dma("pool", wbin[cb], wiv[:, :, blocks[cb][0]:blocks[cb][0] + 512], Twin[cb], W=[Twin[cb]])

        def convert_up():
            for cp in range(NFC // 2):
                em.dma("pool", wbup[cp, 0], wuv[:, :, cp * 256:(cp + 1) * 256], Twup, W=[Twup])
                em.dma("pool", wbup[cp, 1], wuv[:, :, DFF + cp * 256:DFF + (cp + 1) * 256], Twup, W=[Twup])

        def convert_down():
            for n4 in range(4):
                for ks in range(4):
                    em.dma("pool", wbdn[n4, ks], wdv[:, ks * 11:(ks + 1) * 11, n4 * 512:(n4 + 1) * 512], Twdn,
                           W=[Twdn])

        wG = sb("wG", [128, KC, 16], BF16)
        TwG = em.tile("wG")
        wGf = sb("wGf", [128, KC, 16], F32)
        TwGf = em.tile("wGf")
        em.dma("sp", wGf[:], wiv[:, :, 4096:4112], TwGf, W=[TwGf])
        em.copy("dve", wG[:], wGf[:], [TwGf], [TwG])
        CONV_AT = os.environ.get("MK_CONV_AT", "a_end")
        if not NOCONV_IN and CONV_AT == "a_start":
            convert_in()
        if not NOCONV_UP and CONV_AT == "a_start":
            convert_up()
            convert_down()
        CONV_UP_LATE = (not NOCONV_UP) and CONV_AT == "a_end"
        if not NOCONV_IN and CONV_AT == "a_end":
            convert_in()

        with ExitStack() as st:
            cl = sb("cl", [128, KC, 2], F32, st)
            sT = sb("sT", [128, KC, 2], BF16, st)
            sB = [sb("sB%d" % s, [128, KC, 128], BF16, st) for s in range(2)]
            bfm = sb("bfm", [128, 96], F32, st)
            brow = sb("brow", [128, 2, D], F32, st)
            wA = [sb("wA%d" % i, [128, KC, 512], BF16, st) for i in range(3)]
            gst = [sb("gst%d" % i, [128, 512], F32, st) for i in range(2)]
            Tcl, TsT, TsB, Tbfm, Tbrow = em.tile("cl"), em.tile("sT"), em.tile("sB"), em.tile("bfm"), em.tile("brow")
            wArr = RR(list(zip(wA, em.tiles("wA", 3))))
            gstrr = RR(list(zip(gst, em.tiles("gst", 2))))
            em.dma("sp", cl[:], c_lay, Tcl, W=[Tcl])
            em.dma("sp", bfm[:], b_ada_fm, Tbfm, W=[Tbfm])
            em.dma("sp", brow[:, 0, :], b_ada[2 * D:3 * D].partition_broadcast(128), Tbrow, W=[Tbrow])
            em.dma("sp", brow[:, 1, :], b_ada[5 * D:6 * D].partition_broadcast(128), Tbrow, W=[Tbrow])
            em.act(sT[:], cl[:], AF.Silu, [Tcl], [TsT])
            for s in range(2):
                em.copy("dve", sB[s][:], sT[:, :, s:s + 1].broadcast_to([128, KC, 128]), [TsT], [TsB])
            pcv = PB[0][:, 0:128].rearrange("p (a b) -> p a b", b=2)
            Tpcv = PBT[0]
            pgrr = RR([(PB[1], PBT[1]), (PB[2], PBT[2])])
            wav = w_ada.rearrange("(k p) n -> p k n", p=128)
            order = [0, 1, 3, 4, 2, 5]
            fm_index = {0: 0, 1: 1, 3: 2, 4: 3}
            for v6 in order:
                for sub in range(4):
                    jb = v6 * 4 + sub
                    wt, Tw = wArr.next()
                    em.dma("pool", wt[:], wav[:, :, jb * 512:(jb + 1) * 512], Tw, W=[Tw])
                    if v6 in fm_index:
                        vi = fm_index[v6]
                        for qq in range(4):
                            fc = sub * 4 + qq
                            for kc in range(KC):
                                em.mm(pcv[:, vi * 16 + fc, :], wt[:, kc, qq * 128:(qq + 1) * 128], sT[:, kc, :],
                                      kc == 0, kc == KC - 1, [Tw, TsT], [Tpcv])
                    else:
                        vi2 = 0 if v6 == 2 else 1
                        for s in range(2):
                            pg, Tpg = pgrr.next()
                            for kc in range(KC):
                                em.mm(pg[:], sB[s][:, kc, :], wt[:, kc, :], kc == 0, kc == KC - 1, [Tw, TsB], [Tpg])
                            g, Tg = gstrr.next()
                            em.tt("dve", g[:], pg[:], brow[:, vi2, sub * 512:(sub + 1) * 512], ALU.add,
                                  [Tpg, Tbrow], [Tg])
                            em.dma("sp", gbc[vi2, s, :, sub * 512:(sub + 1) * 512], g[:], Tg, R=[Tg])
                if v6 == 1 or v6 == 4:
                    for v6b in ((0, 1) if v6 == 1 else (3, 4)):
                        vi = fm_index[v6b]
                        em.tt("dve", cv[:, vi, :, :], pcv[:, vi * 16:(vi + 1) * 16, :],
                              bfm[:, v6b * 16:(v6b + 1) * 16].unsqueeze(2).broadcast_to([128, KC, 2]), ALU.add,
                              [Tpcv, Tbfm], [Tcv])
                        if v6b in (1, 4):
                            em.ts("dve", cv[:, vi, :, :], cv[:, vi, :, :], 1.0, None, ALU.add, None, [Tcv], [Tcv])
            em.fence()
            em.flush()
            em.release_dsems()

        def modulate_T(src, Tsrc, dst, Tdst, col0, vi_sh, vi_sc, seg, pbanks):
            for g4 in range(4):
                pt, Tpt = pbanks.next()
                ptv = pt[:].rearrange("p (a b) -> p a b", b=128)
                for j in range(4):
                    kc = g4 * 4 + j
                    em.tr(ptv[:, j, :], src[:, kc * 128:(kc + 1) * 128], identf[:], [Tsrc, Tid], [Tpt], inc=(j == 3))
                for j in range(4):
                    kc = g4 * 4 + j
                    em.act(dst[:, kc, col0:col0 + 128], ptv[:, j, :], AF.Identity, [Tpt, Tcv], [Tdst],
                           bias=cv[:, vi_sh, kc, seg:seg + 1], scale=cv[:, vi_sc, kc, seg:seg + 1])

        with ExitStack() as st:
            xt = [sb("xt%d" % i, [128, 4, D], F32, st) for i in range(2)]
            rt = [sb("rt%d" % i, [128, 4, 2, 128], F32, st) for i in range(2)]
            hT = sb("hT", [128, KC, 512], BF16, st)
            wB = [sb("wB%d" % i, [128, KC, 512], BF16, st) for i in range(3)]
            szb = [sb("szb%d" % i, [128, 512], BF16, st) for i in range(4)]
            szf = [sb("szf%d" % i, [128, 512], F32, st) for i in range(3)]
            rtmp = [sb("rtmp%d" % i, [128, 256], F32, st) for i in range(4)]
            xrr = RR(list(zip(xt, em.tiles("xt", 2), rt, em.tiles("rt", 2))))
            ThT = em.tile("hT")
            wrr = RR(list(zip(wB, em.tiles("wB", 3))))
            szbrr = RR(list(zip(szb, em.tiles("szb", 4))))
            szfrr = RR(list(zip(szf, em.tiles("szf", 3))))
            Trtmp = em.tiles("rtmp", 4)
            ptrr = RR([(PB[0], PBT[0]), (PB[1], PBT[1])])
            pzrr = RR([(PB[i], PBT[i]) for i in range(2, 8)])

            def load_x(i):
                xb, Tx, rb, Tr = xrr.next()
                em.dma("sp", xb[:], x[i * 512:(i + 1) * 512, :].rearrange("(m p) d -> p m d", p=128), Tx, W=[Tx])
                em.dma("sp", rb[:], rot[i * 512:(i + 1) * 512].rearrange("(m p) a b -> p m a b", p=128), Tr, W=[Tr])
                return xb, Tx, rb, Tr

            if CONV_UP_LATE:
                em.bg_tiles = [Twup, Twdn]
                convert_up()
                convert_down()
            nxt = load_x(0)
            for i in range(NTILE):
                xb, Tx, rb, Tr = nxt
                seg = (i * 512) // SEG
                if i + 1 < NTILE:
                    nxt = load_x(i + 1)
                for m in range(4):
                    modulate_T(xb[:, m, :], Tx, hT, ThT, m * 128, 0, 1, seg, ptrr)
                for m in range(4):
                    pz, Tpz = pzrr.next()
                    for kc in range(KC):
                        em.mm(pz[:, 0:16], hT[:, kc, m * 128:(m + 1) * 128], wG[:, kc, :], kc == 0, kc == KC - 1,
                              [ThT, TwG], [Tpz])
                    sf, Tsf = szfrr.next()
                    em.copy("act", sf[:, 0:16], pz[:, 0:16], [Tpz], [Tsf])
                    r0 = i * 512 + m * 128
                    em.dma("act", zgate[r0:r0 + 128, :], sf[:, 0:16], Tsf, R=[Tsf])
                for cb in range(16):
                    c0, kind, dst, d0 = blocks[cb]
                    wt, Tw = wrr.next()
                    if NOCONV_IN:
                        em.dma("pool", wt[:], wiv[:, :, c0:c0 + 512], Tw, W=[Tw])
                    else:
                        em.dma("sp", wt[:], wbin[cb], Tw, R=[Twin[cb]], W=[Tw])
                    for m in range(4):
                        pz, Tpz = pzrr.next()
                        for kc in range(KC):
                            em.mm(pz[:], hT[:, kc, m * 128:(m + 1) * 128], wt[:, kc, :], kc == 0, kc == KC - 1,
                                  [ThT, Tw], [Tpz])
                        r0 = i * 512 + m * 128
                        sq = "act"
                        if kind in ("sig", "silu"):
                            so, Tso = szfrr.next()
                            em.act(so[:], pz[:], AF.Sigmoid if kind == "sig" else AF.Silu, [Tpz], [Tso])
                        elif kind == "qs":
                            so, Tso = szbrr.next()
                            em.act(so[:], pz[:], AF.Copy, [Tpz], [Tso], scale=1.0 / 16.0)
                        elif kind == "cast":
                            so, Tso = szbrr.next()
                            em.copy("act", so[:], pz[:], [Tpz], [Tso])
                        else:
                            so, Tso = szbrr.next()
                            sc = 1.0 if kind == "rotq" else 1.0 / 16.0
                            pv = pz[:].rearrange("p (h two i) -> p h two i", two=2, i=128)
                            sov = so[:].rearrange("p (h two i) -> p h two i", two=2, i=128)
                            a_, b_ = pv[:, :, 0, :], pv[:, :, 1, :]
                            cosb = rb[:, m, 0:1, :].broadcast_to([128, 2, 128])
                            sinb = rb[:, m, 1:2, :].broadcast_to([128, 2, 128])
                            t = [rtmp[k][:].rearrange("p (h i) -> p h i", i=128) for k in range(4)]
                            em.stt(t[0], a_, sc, cosb, ALU.mult, ALU.mult, [Tpz, Tr], [Trtmp[0]])
                            em.stt(t[1], b_, sc, sinb, ALU.mult, ALU.mult, [Tpz, Tr], [Trtmp[1]])
                            em.stt(t[2], a_, sc, sinb, ALU.mult, ALU.mult, [Tpz, Tr], [Trtmp[2]])
                            em.stt(t[3], b_, sc, cosb, ALU.mult, ALU.mult, [Tpz, Tr], [Trtmp[3]])
                            em.tt("dve", sov[:, :, 0, :], t[0], t[1], ALU.subtract, [Trtmp[0], Trtmp[1]], [Tso])
                            em.tt("dve", sov[:, :, 1, :], t[2], t[3], ALU.add, [Trtmp[2], Trtmp[3]], [Tso])
                        em.dma(sq, dst[r0:r0 + 128, d0:d0 + 512], so[:], Tso, R=[Tso])
            em.fence()
            em.flush()
            em.release_dsems()

        with ExitStack() as st:
            tri = sb("tri", [128, 3, 128], F32, st)
            rmask = sb("rmask", [128, 8, 128], F32, st)
            rxi = sb("rxi", [128, 8, 128], F32, st)
            rzd = sb("rzd", [128, 16], F32, st)
            bg = sb("bg", [128, 16], F32, st)
            nw = sb("nw", [128, 1024], F32, st)
            gall = sb("gall", [128, NCH, 16], F32, st)
            gt_ = sb("gt_", [128, NCH, 4], F32, st)
            gsp = sb("gsp", [128, NCH, 4], F32, st)
            gW = sb("gW", [128, NCH, 4], F32, st)
            gTHR = sb("gTHR", [128, NCH, 4], F32, st)
            gDEC = sb("gDEC", [128, NCH, 4], F32, st)
            Cm = [sb("Cm%d" % h, [128, 2, 257], F32, st) for h in range(8)]
            Cb = [sb("Cb%d" % h, [128, 2, 257], BF16, st) for h in range(8)]
            TCm = em.tiles("Cm", 8)
            TCb = em.tiles("Cb", 8)
            NB = 2
            qin = [sb("qin%d" % i, [128, D], BF16, st) for i in range(NB)]
            kin = [sb("kin%d" % i, [128, D], BF16, st) for i in range(NB)]
            vex = [sb("vex%d" % i, [128, 8, 257], BF16, st) for i in range(NB)]
            inrr = RR(list(zip(qin, kin, vex, em.tiles("qin", NB), em.tiles("kin", NB), em.tiles("vex", NB))))
            qT_ = [sb("qT%d" % i, [128, 16, 128], BF16, st) for i in range(2)]
            kT_ = [sb("kT%d" % i, [128, 16, 128], BF16, st) for i in range(2)]
            qkTrr = RR(list(zip(qT_, kT_, em.tiles("qT", 2), em.tiles("kT", 2))))
            scT = [sb("scT%d" % i, [128, 128], BF16, st) for i in range(2)]
            scrr = RR(list(zip(scT, em.tiles("scT", 2))))
            kw_ = [sb("kw%d" % i, [128, 256], BF16, st) for i in range(2)]
            kwrr = RR(list(zip(kw_, em.tiles("kw", 2))))
            qx_ = [sb("qx%d" % i, [128, 2, 128], BF16, st) for i in range(2)]
            qxrr = RR(list(zip(qx_, em.tiles("qx", 2))))
            hout = [sb("hout%d" % i, [128, D], F32, st) for i in range(2)]
            horr = RR(list(zip(hout, em.tiles("hout", 2))))
            hbl = [sb("hbl%d" % i, [128, D], F32, st) for i in range(2)]
            ogl = [sb("ogl%d" % i, [128, D], F32, st) for i in range(2)]
            hbrr = RR(list(zip(hbl, ogl, em.tiles("hbl", 2), em.tiles("ogl", 2))))
            dsm = [sb("dsm%d" % i, [128, 4], F32, st) for i in range(2)]
            dsmrr = RR(list(zip(dsm, em.tiles("dsm", 2))))
            stats = sb("stats", [128, 4, 6], F32, st)
            mv = sb("mv", [128, 4, 2], F32, st)
            rs = sb("rs", [128, 8], F32, st)
            ss = sb("ss", [128, 4], F32, st)
            junk = sb("junk", [128, 256], F32, st)
            tmpn = sb("tmpn", [128, 1024], F32, st)
            ybf = [sb("ybf%d" % i, [128, D], BF16, st) for i in range(2)]
            ybrr = RR(list(zip(ybf, em.tiles("ybf", 2))))
            yTs = [sb("yTs%d" % i, [128, 16, 128], BF16, st) for i in range(2)]
            yTrr = RR(list(zip(yTs, em.tiles("yTs", 2))))
            Tc = em.tile("consts")
            Tgall, Tgt, Tgsp, TgW = em.tile("gall"), em.tile("gt"), em.tile("gsp"), em.tile("gW")
            Tstats, Tmv, Trs, Tss, Ttmpn = (em.tiles("stats", 4), em.tiles("mv", 4), em.tiles("rs", 8),
                                            em.tiles("ss", 4), em.tiles("tmpn", 4))
            Tjunk = em.tile("junk")

            em.dma("sp", tri[:], tri_in.rearrange("a s l -> s a l"), Tc, W=[Tc])
            em.dma("sp", rmask[:], rmask_in.rearrange("a s l -> s a l"), Tc, W=[Tc])
            em.dma("sp", rxi[:].rearrange("p a l -> p (a l)"), rxi_in.partition_broadcast(128), Tc, W=[Tc])
            em.dma("sp", rzd[:], rzd_in, Tc, W=[Tc])
            em.dma("sp", bg[:], b_gate.partition_broadcast(128), Tc, W=[Tc])
            em.dma("sp", nw[:], norm_w.partition_broadcast(128), Tc, W=[Tc])
            em.dma("sp", gall[:], zgate.rearrange("(j p) g -> p j g", p=128), Tgall, W=[Tgall])
            for b in range(NB):
                em.memset("pool", vex[b][:, :, 256:257], 1.0, [inrr.items[b][5]])

            ptq = RR([(PB[0], PBT[0]), (PB[1], PBT[1])])
            TpS = [PBT[2], PBT[7]]
            pS = [PB[2][:, 0:128], PB[7][:, 0:128]]
            pn2 = [PB[2][:, 128:130], PB[7][:, 128:130]]
            pNrr = RR([(PB[3], PBT[3]), (PB[4], PBT[4])])
            pUrr = RR([(PB[5], PBT[5]), (PB[6], PBT[6])])
            pGate = PB[7]
            TpG = PBT[7]

            def load_chunk(j):
                qb, kb, vb, Tq, Tk, Tv = inrr.next()
                r0 = j * CH
                em.dma("sp", qb[:], zq[r0:r0 + CH, :], Tq, W=[Tq])
                em.dma("sp", kb[:], zk[r0:r0 + CH, :], Tk, W=[Tk])
                em.dma("sp", vb[:, :, 0:256], zv[r0:r0 + CH, :].rearrange("p (h e) -> p h e", e=256), Tv, W=[Tv])
                return qb, kb, vb, Tq, Tk, Tv

            def load_hb(j):
                hb_, og_, Thb_, Tog_ = hbrr.next()
                r0 = j * CH
                em.dma("sp", hb_[:], hb[r0:r0 + CH, :], Thb_, W=[Thb_])
                em.dma("sp", og_[:], zg[r0:r0 + CH, :], Tog_, W=[Tog_])
                return hb_, og_, Thb_, Tog_

            for sweep in range(2):
                dr = 1 - sweep
                gmf = gall[:, :, 8 + 4 * dr:12 + 4 * dr]
                gmi = gall[:, :, 4 * dr:4 * dr + 4]
                bfb = bg[:, 8 + 4 * dr:12 + 4 * dr].unsqueeze(1).broadcast_to([128, NCH, 4])
                bib = bg[:, 4 * dr:4 * dr + 4].unsqueeze(1).broadcast_to([128, NCH, 4])
                em.tt("dve", gt_[:], gmf, bfb, ALU.add, [Tgall, Tc], [Tgt])
                em.act(gsp[:], gt_[:], AF.Exp, [Tgt], [Tgsp], scale=-1.0)
                em.act(gsp[:], gsp[:], AF.Ln, [Tgsp], [Tgsp], bias=1.0)
                spf = gsp[:].rearrange("p j h -> p (j h)")
                NG = NCH * 4
                em.mm(pGate[:, 0:NG], tri[:, dr, :], spf, True, True, [Tc, Tgsp], [TpG])
                em.mm(pGate[:, 256:256 + NG], tri[:, 2, :], spf, True, True, [Tc, Tgsp], [TpG])
                pgc = pGate[:, 0:NG].rearrange("p (j h) -> p j h", h=4)
                pgt = pGate[:, 256:256 + NG].rearrange("p (j h) -> p j h", h=4)
                em.tt("dve", gt_[:], gmi, bib, ALU.add, [Tgall, Tc], [Tgt])
                em.tt("dve", gt_[:], gt_[:], pgc, ALU.add, [Tgt, TpG], [Tgt])
                em.act(gW[:], gt_[:], AF.Exp, [Tgt], [TgW])
                em.act(gTHR[:], pgc, AF.Exp, [TpG], [TgW])
                em.act(gDEC[:], pgt, AF.Exp, [TpG], [TgW], scale=-1.0)
                for h in range(8):
                    em.memset("pool", Cm[h][:], 0.0, [TCm[h]])
                    em.memset("pool", Cb[h][:], 0.0, [TCb[h]])
                order = list(range(NCH)) if dr == 0 else list(range(NCH - 1, -1, -1))

                def tr_begin():
                    return qkTrr.next()

                def tr_group(bufs, chunk, g8):
                    qTb, kTb, TqT, TkT = bufs
                    qb, kb, vb, Tq, Tk, Tv = chunk
                    if g8 < 4:
                        src, Tsrc, dstT, TdT, g4 = qb, Tq, qTb, TqT, g8
                    else:
                        src, Tsrc, dstT, TdT, g4 = kb, Tk, kTb, TkT, g8 - 4
                    pt, Tpt = ptq.next()
                    ptv = pt[:].bitcast(BF16)[:, 0:512].rearrange("p (a b) -> p a b", b=128)
                    for jj in range(4):
                        blk = g4 * 4 + jj
                        em.tr(ptv[:, jj, :], src[:, blk * 128:(blk + 1) * 128], identb[:], [Tsrc, Tid], [Tpt],
                              inc=(jj == 3))
                    em.copy("act", dstT[:, g4 * 4:(g4 + 1) * 4, :], ptv, [Tpt], [TdT])

                def make_epilogue(j, yb, Tyb):
                    def epi():
                        r0 = j * CH
                        yt_, Tyt = yTrr.next()
                        for g4 in range(4):
                            pt, Tpt = ptq.next()
                            ptv = pt[:].bitcast(BF16)[:, 0:512].rearrange("p (a b) -> p a b", b=128)
                            for jj in range(4):
                                blk = g4 * 4 + jj
                                em.tr(ptv[:, jj, :], yb[:, blk * 128:(blk + 1) * 128], identb[:], [Tyb, Tid], [Tpt],
                                      inc=(jj == 3))
                            em.copy("act", yt_[:, g4 * 4:(g4 + 1) * 4, :], ptv, [Tpt], [Tyt])
                        em.dma("act", yT.rearrange("(k p) t -> p k t", p=128)[:, :, r0:r0 + CH], yt_[:], Tyt, R=[Tyt])
                    return epi

                cur = load_chunk(order[0])
                curhb = load_hb(order[0]) if dr == 0 else None
                curT = tr_begin()
                for g8 in range(8):
                    tr_group(curT, cur, g8)
                pending_epi = None
                for oi, j in enumerate(order):
                    qb, kb, vb, Tq, Tk, Tv = cur
                    qTb, kTb, TqT, TkT = curT
                    nxt = nxthb = nxtT = None
                    if oi + 1 < NCH:
                        nxt = load_chunk(order[oi + 1])
                        nxtT = tr_begin()
                    if (dr == 0 and j == SEGCH) or (dr == 1 and j == SEGCH - 1):
                        for h in range(8):
                            em.ts("pool", Cm[h][:], Cm[h][:], carry[:, 0:1], None, ALU.mult, None,
                                  [TCm[h], Tcarry], [TCm[h]])
                            em.ts("pool", Cb[h][:], Cb[h][:], carry[:, 0:1], None, ALU.mult, None,
                                  [TCb[h], Tcarry], [TCb[h]])
                    ho, Tho = horr.next()
                    ctx = {}
                    if dr == 0:
                        hb_, og_, Thb_, Tog_ = curhb
                        yb, Tyb = ybrr.next()

                    def stA(head):
                        si = head % 2
                        for kc in range(2):
                            em.mm(pS[si], kTb[:, 2 * head + kc, :], qTb[:, 2 * head + kc, :], kc == 0, kc == 1,
                                  [TkT, TqT], [TpS[si]])

                    def stB(head):
                        is_m = head < 4
                        h = head % 4
                        ci = dr * 4 + h
                        si = head % 2
                        sc_, Tsc = scrr.next()
                        if is_m:
                            em.stt(sc_[:], pS[si], gW[:, j, h:h + 1], tri[:, dr, :], ALU.mult, ALU.mult,
                                   [TpS[si], TgW, Tc], [Tsc])
                            qsrc, Tqsrc = qTb[:, 2 * head:2 * head + 2, :], TqT
                        else:
                            em.tt("dve", sc_[:], pS[si], rmask[:, ci, :], ALU.mult, [TpS[si], Tc], [Tsc])
                            qx, Tqx = qxrr.next()
                            em.tt("pool", qx[:], qTb[:, 2 * head:2 * head + 2, :],
                                  rxi[:, ci:ci + 1, :].broadcast_to([128, 2, 128]), ALU.mult, [TqT, Tc], [Tqx])
                            qsrc, Tqsrc = qx[:], Tqx
                        kw, Tkw = kwrr.next()
                        wcol = gW[:, j, h:h + 1] if is_m else rzd[:, ci:ci + 1]
                        em.ts("pool", kw[:], kb[:, head * 256:(head + 1) * 256], wcol, None, ALU.mult, None,
                              [Tk, TgW, Tc], [Tkw])
                        if is_m:
                            em.ts("pool", Cm[head][:], Cm[head][:], gDEC[:, j, h:h + 1], None, ALU.mult, None,
                                  [TCm[head], TgW], [TCm[head]])
                        ctx[head] = (sc_, Tsc, qsrc, Tqsrc, kw, Tkw)

                    def stC(head):
                        is_m = head < 4
                        si = head % 2
                        NV = 257 if is_m else 256
                        sc_, Tsc, qsrc, Tqsrc, kw, Tkw = ctx[head]
                        pN, TpN = pNrr.next()
                        em.mm(pN[:, 0:NV], sc_[:], vb[:, head, 0:NV], True, False, [Tsc, Tv], [TpN], inc=False)
                        for kc in range(2):
                            em.mm(pN[:, 0:NV], qsrc[:, kc, :], Cb[head][:, kc, 0:NV], False, kc == 1,
                                  [Tqsrc, TCb[head]], [TpN])
                        pU, TpU = pUrr.next()
                        pUv = pU[:].rearrange("p (a b) -> p a b", b=256)
                        for kc in range(2):
                            em.mm(pUv[:, kc, :], kw[:, kc * 128:(kc + 1) * 128], vb[:, head, 0:256], True, True,
                                  [Tkw, Tv], [TpU], inc=(kc == 1))
                        pn2h = pN[:, 320:322]
                        if is_m:
                            for kc in range(2):
                                em.mm(pn2h[:, kc:kc + 1], kw[:, kc * 128:(kc + 1) * 128], vb[:, head, 256:257],
                                      False, True, [Tkw, Tv], [TpN], inc=(kc == 1))
                        ctx[head] = (pN, TpN, pUv, TpU, pn2h)

                    def stD(head):
                        is_m = head < 4
                        h = head % 4
                        ci = dr * 4 + h
                        si = head % 2
                        pN, TpN, pUv, TpU, pn2h = ctx[head]
                        if is_m:
                            ds_, Tds = dsmrr.next()
                            em.ts("dve", ds_[:, 0:1], pN[:, 256:257], -1.0, gTHR[:, j, h:h + 1], ALU.mult, ALU.max,
                                  [TpN, TgW], [Tds])
                            em.tt("dve", ds_[:, 0:1], ds_[:, 0:1], pN[:, 256:257], ALU.max, [TpN, Tds], [Tds])
                            em.op("dve", lambda e, o=ds_[:, 1:2], i_=ds_[:, 0:1]: e.reciprocal(out=o, in_=i_),
                                  [Tds], [Tds])
                            hsl = slice(head * 256, (head + 1) * 256)
                            if dr == 1:
                                em.ts("dve", ho[:, hsl], pN[:, 0:256], ds_[:, 1:2], None, ALU.mult, None,
                                      [TpN, Tds], [Tho])
                            else:
                                em.stt(ho[:, hsl], pN[:, 0:256], ds_[:, 1:2], hb_[:, hsl], ALU.mult, ALU.add,
                                       [TpN, Tds, Thb_], [Tho])
                                em.op("dve", lambda e, o=stats[:, h, :], i_=ho[:, hsl]: e.bn_stats(out=o, in_=i_),
                                      [Tho], [Tstats[h]])
                            dcol = gDEC[:, j, h:h + 1]
                            em.stt(Cm[head][:, :, 0:256], pUv, dcol, Cm[head][:, :, 0:256], ALU.mult, ALU.add,
                                   [TpU, TgW, TCm[head]], [TCm[head]])
                            em.stt(Cm[head][:, :, 256], pn2h, dcol, Cm[head][:, :, 256], ALU.mult, ALU.add,
                                   [TpN, TgW, TCm[head]], [TCm[head]])
                            em.copy("act", Cb[head][:], Cm[head][:], [TCm[head]], [TCb[head]])
                        else:
                            hsl = slice(head * 256, (head + 1) * 256)
                            if dr == 1:
                                em.copy("act", ho[:, hsl], pN[:, 0:256], [TpN], [Tho])
                            else:
                                em.tt("dve", ho[:, hsl], pN[:, 0:256], hb_[:, hsl], ALU.add, [TpN, Thb_], [Tho])
                                em.act(junk[:], ho[:, hsl], AF.Square, [Tho], [Tjunk, Tss[h]], accum=ss[:, h:h + 1])
                            dcol = rzd[:, 8 + ci:9 + ci]
                            em.stt(Cm[head][:, :, 0:256], Cm[head][:, :, 0:256], dcol, pUv, ALU.mult, ALU.add,
                                   [TpU, Tc, TCm[head]], [TCm[head]])
                            em.copy("act", Cb[head][:, :, 0:256], Cm[head][:, :, 0:256], [TCm[head]], [TCb[head]])

                    def stE1(head):
                        is_m = head < 4
                        h = head % 4
                        if is_m:
                            em.op("dve", lambda e, o=mv[:, h, :], i_=stats[:, h, :]: e.bn_aggr(out=o, in_=i_),
                                  [Tstats[h]], [Tmv[h]])
                            em.act(rs[:, h:h + 1], mv[:, h, 1:2], AF.Sqrt, [Tmv[h]], [Trs[h]], bias=NORM_EPS)
                        else:
                            em.act(rs[:, 4 + h:5 + h], ss[:, h:h + 1], AF.Sqrt, [Tss[h]], [Trs[4 + h]], bias=NORM_EPS,
                                   scale=1.0 / 256.0)

                    def stE2(head):
                        is_m = head < 4
                        h = head % 4
                        hsl = slice(head * 256, (head + 1) * 256)
                        if is_m:
                            em.op("dve", lambda e, o=rs[:, h:h + 1], i_=rs[:, h:h + 1]: e.reciprocal(out=o, in_=i_),
                                  [Trs[h]], [Trs[h]])
                            em.ts("dve", tmpn[:, hsl], ho[:, hsl], mv[:, h, 0:1], rs[:, h:h + 1], ALU.subtract, ALU.mult,
                                  [Tho, Tmv[h], Trs[h]], [Ttmpn[h]])
                        else:
                            em.op("dve", lambda e, o=rs[:, 4 + h:5 + h], i_=rs[:, 4 + h:5 + h]:
                                  e.reciprocal(out=o, in_=i_), [Trs[4 + h]], [Trs[4 + h]])
                            em.stt(yb[:, hsl], ho[:, hsl], rs[:, 4 + h:5 + h], og_[:, hsl], ALU.mult, ALU.mult,
                                   [Tho, Trs[4 + h], Tog_], [Tyb])

                    def stE3(head):
                        if head < 4:
                            hsl = slice(head * 256, (head + 1) * 256)
                            h = head
                            em.tt("pool", tmpn[:, hsl], tmpn[:, hsl], nw[:, hsl], ALU.mult, [Ttmpn[h], Tc], [Ttmpn[h]])
                            em.tt("pool", yb[:, hsl], tmpn[:, hsl], og_[:, hsl], ALU.mult, [Ttmpn[h], Tog_], [Tyb])

                    stA(0)
                    stB(0)
                    for head in range(8):
                        if head + 1 < 8:
                            stA(head + 1)
                            stB(head + 1)
                        stC(head)
                        stD(head)
                        if dr == 0:
                            stE1(head)
                            if head >= 1:
                                stE2(head - 1)
                            if head >= 2:
                                stE3(head - 2)
                        if head == 1:
                            if pending_epi is not None:
                                pending_epi()
                                pending_epi = None
                            if dr == 0 and oi + 1 < NCH:
                                nxthb = load_hb(order[oi + 1])
                        if nxt is not None:
                            tr_group(nxtT, nxt, head)
                    r0 = j * CH
                    if dr == 1:
                        em.dma("act", hb[r0:r0 + CH, :], ho[:], Tho, R=[Tho])
                    else:
                        stE2(7)
                        stE3(6)
                        stE3(7)
                        pending_epi = make_epilogue(j, yb, Tyb)
                    cur, curhb, curT = nxt, nxthb, nxtT
                if pending_epi is not None:
                    pending_epi()
                    pending_epi = None
                em.fence()
            em.flush()
            em.release_dsems()

        def layernorm_rows(r, Tr, out_ap, Tout, gam, bet, Tgb, stats6, mv2, rs1, Tst, eps, part=None):
            if part == 2:
                em.act(r, r, AF.Identity, [Tr, Tst], [Tr], bias=mv2[:, 1:2], scale=rs1[:, 0:1])
                em.tt("dve", r, r, gam, ALU.mult, [Tr, Tgb], [Tr])
                em.tt("pool", out_ap, r, bet, ALU.add, [Tr, Tgb], [Tout])
                return
            for q4 in range(4):
                em.op("dve", lambda e, o=stats6[:, q4, :], i_=r[:, q4 * 512:(q4 + 1) * 512]: e.bn_stats(out=o, in_=i_),
                      [Tr], [Tst])
            em.op("dve", lambda e, o=mv2[:], i_=stats6[:].rearrange("p a b -> p (a b)"): e.bn_aggr(out=o, in_=i_),
                  [Tst], [Tst])
            em.act(rs1[:], mv2[:, 1:2], AF.Sqrt, [Tst], [Tst], bias=eps)
            em.op("dve", lambda e, o=rs1[:], i_=rs1[:]: e.reciprocal(out=o, in_=i_), [Tst], [Tst])
            em.ts("dve", mv2[:, 1:2], mv2[:, 0:1], -1.0, rs1[:, 0:1], ALU.mult, ALU.mult, [Tst], [Tst])
            if part == 1:
                return
            em.act(r, r, AF.Identity, [Tr, Tst], [Tr], bias=mv2[:, 1:2], scale=rs1[:, 0:1])
            em.tt("dve", r, r, gam, ALU.mult, [Tr, Tgb], [Tr])
            em.tt("pool", out_ap, r, bet, ALU.add, [Tr, Tgb], [Tout])

        with ExitStack() as st:
            wo = sb("wo", [128, KC, D], BF16, st)
            Two = em.tile("wo")
            yti = [sb("yti%d" % i, [128, KC, 512], BF16, st) for i in range(2)]
            ytrr = RR(list(zip(yti, em.tiles("yti", 2))))
            xin = [sb("xin%d" % i, [128, D], F32, st) for i in range(2)]
            xinrr = RR(list(zip(xin, em.tiles("xin", 2))))
            g1t = sb("g1t", [128, D], F32, st)
            Tg1 = em.tile("g1t")
            lng = sb("lng", [128, D], F32, st)
            lnb = sb("lnb", [128, D], F32, st)
            Tln = em.tile("ln")
            rr_ = [sb("r%d" % i, [128, D], F32, st) for i in range(2)]
            rrr = RR(list(zip(rr_, em.tiles("r", 2))))
            x1o = [sb("x1o%d" % i, [128, D], F32, st) for i in range(2)]
            x1rr = RR(list(zip(x1o, em.tiles("x1o", 2))))
            h2s = [sb("h2s%d" % i, [128, KC, 128], BF16, st) for i in range(2)]
            h2rr = RR(list(zip(h2s, em.tiles("h2s", 2))))
            stats6 = sb("stats6", [128, 4, 6], F32, st)
            mv2 = sb("mv2", [128, 2], F32, st)
            rs1 = sb("rs1", [128, 1], F32, st)
            Tst = em.tile("st")
            wov = w_out.rearrange("(k p) n -> p k n", p=128)
            for q4 in range(4):
                em.dma("pool", wo[:, :, q4 * 512:(q4 + 1) * 512], wov[:, :, q4 * 512:(q4 + 1) * 512], Two, W=[Two])
            em.dma("sp", lng[:], ln1g.partition_broadcast(128), Tln, W=[Tln])
            em.dma("sp", lnb[:], ln1b.partition_broadcast(128), Tln, W=[Tln])
            pmix = [(PB[i], PBT[i]) for i in range(4)]
            ptrr = RR([(PB[4], PBT[4]), (PB[5], PBT[5]), (PB[6], PBT[6]), (PB[7], PBT[7])])
            yTv = yT.rearrange("(k p) t -> p k t", p=128)
            h2Tv = h2T.rearrange("(k p) t -> p k t", p=128)
            pendX = None
            for i in range(NTILE):
                seg = (i * 512) // SEG
                if i == 0 or (i * 512) % SEG == 0:
                    em.dma("sp", g1t[:], gbc[0, seg], Tg1, W=[Tg1])
                yt_, Tyt = ytrr.next()
                em.dma("sp", yt_[:], yTv[:, :, i * 512:(i + 1) * 512], Tyt, W=[Tyt])
                for m in range(4):
                    r0 = i * 512 + m * 128
                    xi_, Txi = xinrr.next()
                    em.dma("sp", xi_[:], x[r0:r0 + 128, :], Txi, W=[Txi])
                    r, Tr = rrr.next()
                    for n4 in range(4):
                        pm, Tpm = pmix[n4]
                        for kc in range(KC):
                            em.mm(pm[:], yt_[:, kc, m * 128:(m + 1) * 128], wo[:, kc, n4 * 512:(n4 + 1) * 512],
                                  kc == 0, kc == KC - 1, [Tyt, Two], [Tpm])
                        sl = slice(n4 * 512, (n4 + 1) * 512)
                        em.tt("dve", r[:, sl], pm[:], g1t[:, sl], ALU.mult, [Tpm, Tg1], [Tr])
                    if pendX is not None:
                        pendX()
                        pendX = None
                    em.stt(r[:], xi_[:], ALPHA, r[:], ALU.mult, ALU.add, [Txi, Tr], [Tr])
                    xo, Txo = x1rr.next()
                    layernorm_rows(r[:], Tr, xo[:], Txo, lng[:], lnb[:], Tln, stats6, mv2, rs1, Tst, LN_EPS)

                    def postX(xo=xo, Txo=Txo, seg=seg, r0=r0):
                        em.dma("act", x1[r0:r0 + 128, :], xo[:], Txo, R=[Txo])
                        h2, Th2 = h2rr.next()
                        modulate_T(xo[:], Txo, h2, Th2, 0, 2, 3, seg, ptrr)
                        em.dma("act", h2Tv[:, :, r0:r0 + 128], h2[:], Th2, R=[Th2])
                    pendX = postX
            if pendX is not None:
                pendX()
            em.fence()
            em.flush()
            em.release_dsems()

        em.bg_tiles = []
        with ExitStack() as st:
            AW = 514
            h2i = sb("h2i", [128, KC, 512], BF16, st)
            Th2i = em.tile("h2i")
            act_ = sb("act_", [128, NFC, AW], BF16, st)
            Tact = em.tile("act")
            wua = [sb("wua%d" % i, [128, KC, 256], BF16, st) for i in range(2)]
            wug = [sb("wug%d" % i, [128, KC, 256], BF16, st) for i in range(2)]
            wurr = RR(list(zip(wua, wug, em.tiles("wua", 2), em.tiles("wug", 2))))
            wd = [sb("wd%d" % i, [128, 11, 512], BF16, st) for i in range(2)]
            wdrr = RR(list(zip(wd, em.tiles("wd", 2))))
            cf = sb("cf", [128, 2 * NFC, 4], F32, st)
            ncf = sb("ncf", [128, 2 * NFC, 2], F32, st)
            cm1 = sb("cm1", [128, 1], F32, st)
            Tcf = em.tile("cf")
            cwa = [sb("cwa%d" % i, [128, 516], F32, st) for i in range(3)]
            cwg = [sb("cwg%d" % i, [128, 516], F32, st) for i in range(3)]
            cwrr = RR(list(zip(cwa, cwg, em.tiles("cwa", 3), em.tiles("cwg", 3))))
            oa = [sb("oa%d" % i, [128, AW], F32, st) for i in range(3)]
            og2 = [sb("og2%d" % i, [128, AW], F32, st) for i in range(3)]
            oarr = RR(list(zip(oa, og2, em.tiles("oa", 3), em.tiles("og2", 3))))
            sav = sb("sav", [128, 2 * NFC, 2], F32, st)
            Tsav = em.tile("sav")
            g2t = sb("g2t", [128, D], F32, st)
            Tg2 = em.tile("g2t")
            g2r = sb("g2r", [128, 512], F32, st)
            Tg2r = em.tile("g2r")
            etmp = [sb("etmp%d" % i, [128, 512], F32, st) for i in range(2)]
            etrr = RR(list(zip(etmp, em.tiles("etmp", 2))))
            lng = sb("lng2", [128, D], F32, st)
            lnb = sb("lnb2", [128, D], F32, st)
            Tln = em.tile("ln2")
            racc = [sb("racc%d" % i, [128, D], F32, st) for i in range(4)]
            Tracc = em.tiles("racc", 4)
            stats6 = sb("stats6b", [128, 4, 6], F32, st)
            mv2 = sb("mv2b", [128, 2], F32, st)
            rs1 = sb("rs1b", [128, 1], F32, st)
            Tst = em.tile("stb")
            em.dma("sp", cf[:], conv_fm, Tcf, W=[Tcf])
            em.dma("sp", lng[:], ln2g.partition_broadcast(128), Tln, W=[Tln])
            em.dma("sp", lnb[:], ln2b.partition_broadcast(128), Tln, W=[Tln])
            em.ts("dve", cm1[:], carry[:], -1.0, None, ALU.add, None, [Tcarry], [Tcf])
            em.ts("dve", ncf[:, :, 0], cf[:, :, 0], cm1[:, 0:1], None, ALU.mult, None, [Tcf], [Tcf])
            em.ts("dve", ncf[:, :, 1], cf[:, :, 2], cm1[:, 0:1], None, ALU.mult, None, [Tcf], [Tcf])
            em.memset("pool", sav[:], 0.0, [Tsav])
            for wi in range(4):
                em.memset("pool", racc[wi][:], 0.0, [Tracc[wi]])
            for b in range(3):
                em.memset("pool", cwa[b][:], 0.0, [cwrr.items[b][2]])
                em.memset("pool", cwg[b][:], 0.0, [cwrr.items[b][3]])
            purr = RR([((PB[0], PBT[0]), (PB[1], PBT[1])), ((PB[2], PBT[2]), (PB[3], PBT[3]))])
            pdn = [(PB[4 + i], PBT[4 + i]) for i in range(4)]
            h2Tv = h2T.rearrange("(k p) t -> p k t", p=128)
            wuv2 = w_up.rearrange("(k p) n -> p k n", p=128)
            wdv2 = w_down.rearrange("(c p) n -> p c n", p=128)
            em.dma("sp", h2i[:], h2Tv[:, :, 0:512], Th2i, W=[Th2i])
            pending = []
            for i in range(NTILE):
                t0 = i * 512
                seg = t0 // SEG
                last = (i == NTILE - 1)
                boundary = (i > 0 and t0 % SEG == 0)
                Wd = 513 if last else 512
                if i == 0 or t0 % SEG == 0:
                    em.dma("sp", g2t[:], gbc[1, seg], Tg2, W=[Tg2])
                tail = None
                for cp in range(NFC // 2):
                    if pending and cp >= 2:
                        pending.pop(0)()
                    wa, wg, Twa, Twg = wurr.next()
                    if NOCONV_UP:
                        em.dma("pool", wa[:], wuv2[:, :, cp * 256:(cp + 1) * 256], Twa, W=[Twa])
                        em.dma("pool", wg[:], wuv2[:, :, DFF + cp * 256:DFF + (cp + 1) * 256], Twg, W=[Twg])
                    else:
                        em.dma("sp", wa[:], wbup[cp, 0], Twa, R=[Twup], W=[Twa])
                        em.dma("sp", wg[:], wbup[cp, 1], Twg, R=[Twup], W=[Twg])
                    for c2 in range(2):
                        c = cp * 2 + c2
                        (pa, Tpa), (pg, Tpg) = purr.next()
                        for kc in range(KC):
                            em.mm(pa[:], wa[:, kc, c2 * 128:(c2 + 1) * 128], h2i[:, kc, :], kc == 0, kc == KC - 1,
                                  [Twa, Th2i], [Tpa])
                        for kc in range(KC):
                            em.mm(pg[:], wg[:, kc, c2 * 128:(c2 + 1) * 128], h2i[:, kc, :], kc == 0, kc == KC - 1,
                                  [Twg, Th2i], [Tpg])
                        ca, cg, Tca, Tcg = cwrr.next()
                        o_a, o_g, Toa, Tog = oarr.next()
                        for half, (ps_, Tps, cw, Tcw, oo, Too) in enumerate(
                                ((pa, Tpa, ca, Tca, o_a, Toa), (pg, Tpg, cg, Tcg, o_g, Tog))):
                            ch = c + half * NFC
                            em.copy("act", cw[:, 2:514], ps_[:], [Tps], [Tcw])
                            em.copy("act", cw[:, 0:2], sav[:, ch, :], [Tsav], [Tcw])
                            em.copy("act", sav[:, ch, :], cw[:, 512:514], [Tcw], [Tsav])
                            em.ts("pool", oo[:, 0:Wd], cw[:, 1:1 + Wd], cf[:, ch, 1:2], cf[:, ch, 3:4], ALU.mult, ALU.add,
                                  [Tcw, Tcf], [Too])
                            em.stt(oo[:, 0:Wd], cw[:, 0:Wd], cf[:, ch, 0:1], oo[:, 0:Wd], ALU.mult, ALU.add,
                                   [Tcw, Tcf, Too], [Too])
                            em.stt(oo[:, 0:Wd], cw[:, 2:2 + Wd], cf[:, ch, 2:3], oo[:, 0:Wd], ALU.mult, ALU.add,
                                   [Tcw, Tcf, Too], [Too])
                            if boundary:
                                em.stt(oo[:, 0:1], cw[:, 2:3], ncf[:, ch, 1:2], oo[:, 0:1], ALU.mult, ALU.add,
                                       [Tcw, Tcf, Too], [Too])
                                em.stt(oo[:, 1:2], cw[:, 1:2], ncf[:, ch, 0:1], oo[:, 1:2], ALU.mult, ALU.add,
                                       [Tcw, Tcf, Too], [Too])
                        if tail is not None:
                            tail()

                        def tail(o_a=o_a, o_g=o_g, Toa=Toa, Tog=Tog, c=c, Wd=Wd):
                            em.act(o_a[:, 0:Wd], o_a[:, 0:Wd], AF.Silu, [Toa], [Toa])
                            em.tt("pool", act_[:, c, 0:Wd], o_a[:, 0:Wd], o_g[:, 0:Wd], ALU.mult, [Toa, Tog], [Tact])
                if tail is not None:
                    tail()
                    tail = None
                if not last:
                    em.dma("act", h2i[:], h2Tv[:, :, t0 + 512:t0 + 1024], Th2i, W=[Th2i])
                wins = []
                for m in range(4):
                    wins.append((m * 128, t0 - 1 + m * 128, 1 if (i == 0 and m == 0) else 0))
                groups = [wins]
                if last:
                    groups.append([(385, t0 + 384, 0)])
                for gi, grp in enumerate(groups):
                    while pending:
                        pending.pop(0)()
                    for wi, (col0, tok0, p0) in enumerate(grp):
                        em.dma("sp", racc[wi][p0:128, :], x1[tok0 + p0:tok0 + 128, :], Tracc[wi], W=[Tracc[wi]])
                    for n4 in range(4):
                        sl = slice(n4 * 512, (n4 + 1) * 512)
                        if boundary:
                            em.dma("sp", g2r[0:1, :], gbc[1, seg - 1, 0:1, sl], Tg2r, W=[Tg2r])
                        for ks in range(4):
                            wdt, Twd = wdrr.next()
                            if NOCONV_UP:
                                em.dma("pool", wdt[:], wdv2[:, ks * 11:(ks + 1) * 11, n4 * 512:(n4 + 1) * 512], Twd, W=[Twd])
                            else:
                                em.dma("sp", wdt[:], wbdn[n4, ks], Twd, R=[Twdn], W=[Twd])
                            for wi, (col0, tok0, p0) in enumerate(grp):
                                pd, Tpd = pdn[wi]
                                for cc in range(11):
                                    em.mm(pd[:], act_[:, ks * 11 + cc, col0:col0 + 128], wdt[:, cc, :],
                                          ks == 0 and cc == 0, ks == 3 and cc == 10, [Tact, Twd], [Tpd],
                                          inc=(cc == 10))
                        for wi in range(len(grp)):
                            pd, Tpd = pdn[wi]
                            tm, Ttm = etrr.next()
                            em.tt("dve", tm[:], pd[:], g2t[:, sl], ALU.mult, [Tpd, Tg2], [Ttm])
                            if boundary and grp[wi][0] == 0:
                                em.tt("dve", tm[0:1, :], pd[0:1, :], g2r[0:1, :], ALU.mult, [Tpd, Tg2r], [Ttm])
                            em.stt(racc[wi][:, sl], racc[wi][:, sl], ALPHA, tm[:], ALU.mult, ALU.add,
                                   [Tracc[wi], Ttm], [Tracc[wi]])

                    def make_ep(wi, tok0, p0, part):
                        def ep():
                            r, Tr = racc[wi], Tracc[wi]
                            layernorm_rows(r[:], Tr, r[:], Tr, lng[:], lnb[:], Tln, stats6, mv2, rs1, Tst, LN_EPS,
                                           part=part)
                        return ep

                    def make_st(wi, tok0, p0):
                        def stf():
                            em.dma("act", y[tok0 + p0:tok0 + 128, :], racc[wi][p0:128, :], Tracc[wi], R=[Tracc[wi]])
                        return stf
                    pending = []
                    prev_st = None
                    for wi, (col0, tok0, p0) in enumerate(grp):
                        e1, e2, st_ = make_ep(wi, tok0, p0, 1), make_ep(wi, tok0, p0, 2), make_st(wi, tok0, p0)
                        if prev_st is None:
                            pending.append(e1)
                        else:
                            pending.append(lambda a=e1, b=prev_st: (a(), b()))
                        pending.append(e2)
                        prev_st = st_
                    pending.append(prev_st)
            while pending:
                pending.pop(0)()
            em.fence()
            em.flush()
    nc._n_ops = (em.n_ops, em.n_wait, dict(em.cnt))
    return nc


def _consts():
    pos = np.arange(128, dtype=np.float32)
    s = pos[:, None]
    l = pos[None, :]
    tri = np.stack([(s <= l), (s >= l), np.ones((128, 128), bool)]).astype(np.float32)
    hd = np.arange(4, dtype=np.float32)
    lg_f = np.log1p(-np.exp2(-5.0 - hd)).astype(np.float32)
    lg_b = np.log1p(-np.exp2(-5.5 - hd)).astype(np.float32)
    rmask = np.zeros((8, 128, 128), np.float32)
    rxi = np.zeros((8, 128), np.float32)
    rzd = np.zeros((128, 16), np.float32)
    for h in range(4):
        rmask[h] = np.where(s <= l, np.exp(lg_f[h] * np.maximum(l - s, 0.0)), 0.0)
        rxi[h] = np.exp(lg_f[h] * (pos + 1.0))
        rzd[:, h] = np.exp(lg_f[h] * (127.0 - pos))
        rzd[:, 8 + h] = np.exp(lg_f[h] * 128.0)
        rmask[4 + h] = np.where(s >= l, np.exp(lg_b[h] * np.maximum(s - l, 0.0)), 0.0)
        rxi[4 + h] = np.exp(lg_b[h] * (128.0 - pos))
        rzd[:, 4 + h] = np.exp(lg_b[h] * pos)
        rzd[:, 12 + h] = np.exp(lg_b[h] * 128.0)
    return tri, rmask.astype(np.float32), rxi.reshape(-1).astype(np.float32), rzd.astype(np.float32)


def _rot_table(npos):
    inv = (1.0 / (np.float32(10000.0) ** np.linspace(0.0, 1.0, 128, dtype=np.float32))).astype(np.float32)
    ang = (np.arange(npos, dtype=np.float32)[:, None] * inv[None, :]).astype(np.float32)
    return np.stack([np.cos(ang), np.sin(ang)], axis=1).astype(np.float32)


def prepare_shared(w_ada, b_ada, w_in, b_igate, b_fgate, mlstm_norm_w, w_out, ln1_g, ln1_b, w_up, conv_w, conv_b,
                   w_down, ln2_g, ln2_b):
    f = lambda a: np.ascontiguousarray(np.asarray(a, dtype=np.float32))
    w_in0 = f(w_in)[0]
    perm = np.arange(N_IN)
    for base in (4112, 5136):
        for h in range(4):
            b0 = base + h * 256
            perm[b0:b0 + 128] = b0 + 2 * np.arange(128)
            perm[b0 + 128:b0 + 256] = b0 + 2 * np.arange(128) + 1
    w_in_p = np.ascontiguousarray(w_in0[:, perm])
    cw = f(conv_w)[0]
    cb = f(conv_b)[0]
    conv4 = np.concatenate([cw, cb[None, :]], axis=0)
    conv_fm = np.ascontiguousarray(conv4.T.reshape(2 * NFC, 128, 4).transpose(1, 0, 2))
    tri, rmask, rxi, rzd = _consts()
    b_ada0 = f(b_ada)[0]
    return {
        "w_ada": f(w_ada)[0], "b_ada": b_ada0,
        "b_ada_fm": np.ascontiguousarray(b_ada0.reshape(96, 128).T),
        "w_in": w_in_p, "b_gate": np.concatenate([f(b_igate)[0], f(b_fgate)[0]]),
        "norm_w": f(mlstm_norm_w)[0], "w_out": f(w_out)[0], "ln1_g": f(ln1_g)[0], "ln1_b": f(ln1_b)[0],
        "w_up": f(w_up)[0], "conv_fm": conv_fm, "w_down": f(w_down)[0], "ln2_g": f(ln2_g)[0], "ln2_b": f(ln2_b)[0],
        "tri": tri, "rmask": rmask, "rxi": rxi, "rzd": rzd,
    }


def core_inputs(xc, c2, carry, rot):
    return {
        "x": np.ascontiguousarray(xc, dtype=np.float32),
        "c_lay": np.ascontiguousarray(np.asarray(c2, np.float32).reshape(2, KC, 128).transpose(2, 1, 0)),
        "carry": np.full((128, 1), carry, np.float32),
        "rot": np.ascontiguousarray(rot),
    }


_NC_CACHE = {}


def kernel(x_prompt, x_sample, c_prompt, c_sample, w_ada, b_ada, w_in, b_igate, b_fgate, mlstm_norm_w, w_out,
           ln1_g, ln1_b, w_up, conv_w, conv_b, w_down, ln2_g, ln2_b):
    SEG = 2048
    x_prompt = np.asarray(x_prompt, np.float32)
    x_sample = np.asarray(x_sample, np.float32)
    c_prompt = np.asarray(c_prompt, np.float32)
    c_sample = np.asarray(c_sample, np.float32)
    shared = prepare_shared(w_ada, b_ada, w_in, b_igate, b_fgate, mlstm_norm_w, w_out, ln1_g, ln1_b, w_up, conv_w,
                            conv_b, w_down, ln2_g, ln2_b)
    rot_full = _rot_table(2 * SEG)
    rot_p = rot_full
    rot_s = np.concatenate([rot_full[:SEG], rot_full[:SEG]], axis=0)
    in_maps = []
    for i in range(4):
        m = core_inputs(x_prompt[i], np.stack([c_prompt[i], c_prompt[i]]), 1.0, rot_p)
        m.update(shared)
        in_maps.append(m)
    for i in range(4):
        m = core_inputs(x_sample[2 * i:2 * i + 2].reshape(2 * SEG, D), c_sample[2 * i:2 * i + 2], 0.0, rot_s)
        m.update(shared)
        in_maps.append(m)
    if SEG not in _NC_CACHE:
        _NC_CACHE[SEG] = build_program(SEG)
    nc = _NC_CACHE[SEG]
    res = run_bass_kernel_spmd(nc, in_maps, core_ids=list(range(8)))
    outs = [np.asarray(r["y"], np.float32) for r in res.results]
    y_prompt = np.stack(outs[:4], axis=0)
    y_sample = np.concatenate([o.reshape(2, SEG, D) for o in outs[4:]], axis=0)
    return (y_prompt, y_sample)
```
